# Optimizing a Trainium2 kernel written in Bass

```python
import jax, jax.numpy as jnp
from jax import lax
import numpy as np

D_MODEL = 1024
BATCH = 32
SEQ = 256
DEPTH = 1
DEC_BATCH = 8
DEC_SEQ = 4096
PAST_LEN = 512

GRID_W = 64
D_RNN = 1024
RNN_HEADS = 16
RNN_HEAD_DIM = D_RNN // RNN_HEADS
CONV_W = 4
CONV_LEFT = 2
LRU_C = 8.0
D_GMLP = 1024
GMLP_GROUPS = 8
GMLP_GROUP_DIM = D_GMLP // GMLP_GROUPS
CHUNK = 128
D_FF = 2816
N_MOD = 9
EPS = 1e-6
IN_COLS = 2 * D_RNN + 2 * D_GMLP + 2 * D_MODEL
IN_SPLITS = (D_RNN, 2 * D_RNN, 2 * D_RNN + D_GMLP, 2 * D_RNN + 2 * D_GMLP,
             2 * D_RNN + 2 * D_GMLP + D_MODEL)

kernel_name = "hybrid_rglru_gmlp_diffusion_step"


def rms_norm(x, g):
    xf = x.astype(jnp.float32)
    y = xf * lax.rsqrt(jnp.mean(xf * xf, axis=-1, keepdims=True) + EPS)
    return (y * g.astype(jnp.float32)).astype(x.dtype)


def modulate(x, g, shift, scale):
    return rms_norm(x, g) * (1 + scale) + shift


def grid_pos_embed(n_tokens, dtype):
    rows = n_tokens // GRID_W
    t = jnp.arange(rows * GRID_W)
    r = (t // GRID_W).astype(jnp.float32)
    col = (t % GRID_W).astype(jnp.float32)
    q = D_MODEL // 4
    freqs = 1.0 / (10000.0 ** (jnp.arange(q, dtype=jnp.float32) / q))
    ang_r = r[:, None] * freqs
    ang_c = col[:, None] * freqs
    pe = jnp.concatenate([jnp.sin(ang_r), jnp.cos(ang_r), jnp.sin(ang_c), jnp.cos(ang_c)], axis=-1)
    return pe.astype(dtype)


def swiglu(x, w_gate, w_up, w_down):
    return (jax.nn.silu(x @ w_gate) * (x @ w_up)) @ w_down


def centred_dwconv(x, w, b):
    T = x.shape[1]
    xp = jnp.pad(x, ((0, 0), (CONV_LEFT, CONV_W - 1 - CONV_LEFT), (0, 0)))
    out = xp[:, 0:T] * w[0]
    for k in range(1, CONV_W):
        out = out + xp[:, k:k + T] * w[k]
    return out + b


def _lru_combine(lhs, rhs):
    a1, b1 = lhs
    a2, b2 = rhs
    return a1 * a2, a2 * b1 + b2


def rg_lru(x, h0, w_r, b_r, w_i, b_i, lam, reverse):
    B, T, _ = x.shape
    xh = x.reshape(B, T, RNN_HEADS, RNN_HEAD_DIM)
    r = jax.nn.sigmoid(jnp.einsum('bthi,hij->bthj', xh, w_r).reshape(B, T, D_RNN) + b_r)
    i = jax.nn.sigmoid(jnp.einsum('bthi,hij->bthj', xh, w_i).reshape(B, T, D_RNN) + b_i)
    log_a = (LRU_C * r.astype(jnp.float32)) * jax.nn.log_sigmoid(lam.astype(jnp.float32))
    a = jnp.exp(log_a)
    mult = jnp.sqrt(jnp.maximum(-jnp.expm1(2.0 * log_a), 0.0))
    bx = mult * (i * x).astype(jnp.float32)
    a_cum, b_cum = lax.associative_scan(_lru_combine, (a, bx), reverse=reverse, axis=1)
    h = a_cum * h0[:, None].astype(jnp.float32) + b_cum
    return h.astype(x.dtype)


def chunk_gmlp(u, v, g_v, w_s, b_s):
    B, T, _ = v.shape
    n = T // CHUNK
    vn = rms_norm(v, g_v).reshape(B, n, CHUNK, GMLP_GROUPS, GMLP_GROUP_DIM)
    mixed = jnp.einsum('gpq,bnqgc->bnpgc', w_s, vn) + b_s.T[None, None, :, :, None]
    return u * mixed.reshape(B, T, D_GMLP)


def mixer(h, h0_f, h0_b, p):
    z = h @ p['w_in']
    xr, gr, u, v, ga, gb = jnp.split(z, IN_SPLITS, axis=-1)
    xr = centred_dwconv(xr, p['conv_w'], p['conv_b'])
    hf = rg_lru(xr, h0_f, p['w_r'][0], p['b_r'][0], p['w_i'][0], p['b_i'][0], p['lam'][0], False)
    hb = rg_lru(xr, h0_b, p['w_r'][1], p['b_r'][1], p['w_i'][1], p['b_i'][1], p['lam'][1], True)
    y_rnn = jax.nn.gelu(gr) * (hf + hb)
    y_g = chunk_gmlp(jax.nn.gelu(u), jax.nn.gelu(v), p['gmlp_norm'], p['w_s'], p['b_s'])
    merged = jax.nn.sigmoid(ga) * (y_rnn @ p['w_br']) + jax.nn.sigmoid(gb) * (y_g @ p['w_bg'])
    return merged @ p['w_out'], hf[:, -1], hb[:, 0]


def trunk_layer(x, cond, h0_f, h0_b, p):
    m = (jax.nn.silu(cond) @ p['w_mod'] + p['b_mod']).reshape(cond.shape[0], N_MOD, 1, D_MODEL)
    sh1, sc1, g1 = m[:, 0], m[:, 1], m[:, 2]
    sh2, sc2, g2 = m[:, 3], m[:, 4], m[:, 5]
    sh3, sc3, g3 = m[:, 6], m[:, 7], m[:, 8]
    x = x + 0.5 * g1 * swiglu(modulate(x, p['norm1'], sh1, sc1), p['ff1_gate'], p['ff1_up'], p['ff1_down'])
    y, hf, hb = mixer(modulate(x, p['norm2'], sh2, sc2), h0_f, h0_b, p)
    x = x + g2 * y
    x = x + 0.5 * g3 * swiglu(modulate(x, p['norm3'], sh3, sc3), p['ff2_gate'], p['ff2_up'], p['ff2_down'])
    return x, hf, hb


def setup_inputs(seed: int = 0) -> dict:
    key = jax.random.key(seed)
    ks = jax.random.split(key, 40)
    nrm = lambda k, shape, s: jax.random.normal(k, shape, jnp.float32) * s
    gain = lambda k, shape: 1.0 + 0.02 * jax.random.normal(k, shape, jnp.float32)
    L = DEPTH
    a0 = jax.random.uniform(ks[18], (L, 2, D_RNN), jnp.float32, 0.9, 0.999)
    return {
        "x_prompt": nrm(ks[0], (BATCH, SEQ, D_MODEL), 1.0),
        "x_sample": nrm(ks[1], (DEC_BATCH, DEC_SEQ, D_MODEL), 1.0),
        "state_rnn_fwd": nrm(ks[2], (DEC_BATCH, DEPTH, D_RNN), 1.0),
        "state_rnn_bwd": nrm(ks[3], (DEC_BATCH, DEPTH, D_RNN), 1.0),
        "c": nrm(ks[4], (DEC_BATCH, D_MODEL), 1.0),
        "c_ctx": nrm(ks[5], (D_MODEL,), 1.0),
        "w_mod": nrm(ks[6], (L, D_MODEL, N_MOD * D_MODEL), D_MODEL ** -0.5),
        "b_mod": nrm(ks[7], (L, N_MOD * D_MODEL), 0.02),
        "norm1": gain(ks[8], (L, D_MODEL)),
        "norm2": gain(ks[9], (L, D_MODEL)),
        "norm3": gain(ks[10], (L, D_MODEL)),
        "ff1_gate": nrm(ks[11], (L, D_MODEL, D_FF), D_MODEL ** -0.5),
        "ff1_up": nrm(ks[12], (L, D_MODEL, D_FF), D_MODEL ** -0.5),
        "ff1_down": nrm(ks[13], (L, D_FF, D_MODEL), D_FF ** -0.5),
        "w_in": nrm(ks[14], (L, D_MODEL, IN_COLS), D_MODEL ** -0.5),
        "conv_w": nrm(ks[15], (L, CONV_W, D_RNN), CONV_W ** -0.5),
        "conv_b": nrm(ks[16], (L, D_RNN), 0.02),
        "w_r": nrm(ks[17], (L, 2, RNN_HEADS, RNN_HEAD_DIM, RNN_HEAD_DIM), RNN_HEAD_DIM ** -0.5),
        "b_r": nrm(ks[19], (L, 2, D_RNN), 0.02),
        "w_i": nrm(ks[20], (L, 2, RNN_HEADS, RNN_HEAD_DIM, RNN_HEAD_DIM), RNN_HEAD_DIM ** -0.5),
        "b_i": nrm(ks[21], (L, 2, D_RNN), 0.02),
        "lam": jnp.log(a0 / (1.0 - a0)),
        "gmlp_norm": gain(ks[22], (L, D_GMLP)),
        "w_s": nrm(ks[23], (L, GMLP_GROUPS, CHUNK, CHUNK), CHUNK ** -0.5),
        "b_s": gain(ks[24], (L, GMLP_GROUPS, CHUNK)),
        "w_br": nrm(ks[25], (L, D_RNN, D_MODEL), D_RNN ** -0.5),
        "w_bg": nrm(ks[26], (L, D_GMLP, D_MODEL), D_GMLP ** -0.5),
        "w_out": nrm(ks[27], (L, D_MODEL, D_MODEL), D_MODEL ** -0.5),
        "ff2_gate": nrm(ks[28], (L, D_MODEL, D_FF), D_MODEL ** -0.5),
        "ff2_up": nrm(ks[29], (L, D_MODEL, D_FF), D_MODEL ** -0.5),
        "ff2_down": nrm(ks[30], (L, D_FF, D_MODEL), D_FF ** -0.5),
        "norm_f": gain(ks[31], (D_MODEL,)),
    }


def reference(x_prompt, x_sample, state_rnn_fwd, state_rnn_bwd, c, c_ctx,
              w_mod, b_mod, norm1, norm2, norm3, ff1_gate, ff1_up, ff1_down,
              w_in, conv_w, conv_b, w_r, b_r, w_i, b_i, lam, gmlp_norm, w_s, b_s,
              w_br, w_bg, w_out, ff2_gate, ff2_up, ff2_down, norm_f):
    def layer_params(l):
        return dict(w_mod=w_mod[l], b_mod=b_mod[l], norm1=norm1[l], norm2=norm2[l], norm3=norm3[l],
                    ff1_gate=ff1_gate[l], ff1_up=ff1_up[l], ff1_down=ff1_down[l],
                    w_in=w_in[l], conv_w=conv_w[l], conv_b=conv_b[l],
                    w_r=w_r[l], b_r=b_r[l], w_i=w_i[l], b_i=b_i[l], lam=lam[l],
                    gmlp_norm=gmlp_norm[l], w_s=w_s[l], b_s=b_s[l],
                    w_br=w_br[l], w_bg=w_bg[l], w_out=w_out[l],
                    ff2_gate=ff2_gate[l], ff2_up=ff2_up[l], ff2_down=ff2_down[l])

    xc = x_prompt
    zeros = jnp.zeros((x_prompt.shape[0], D_RNN), x_prompt.dtype)
    st_f, st_b = [], []
    for l in range(DEPTH):
        xc, hf, hb = trunk_layer(xc, c_ctx[None], zeros, zeros, layer_params(l))
        st_f.append(hf)
        st_b.append(hb)
    y_prompt = rms_norm(xc, norm_f)
    new_state_rnn_fwd = jnp.stack(st_f, axis=1)
    new_state_rnn_bwd = jnp.stack(st_b, axis=1)

    xs = x_sample + grid_pos_embed(x_sample.shape[1], x_sample.dtype)[None]
    for l in range(DEPTH):
        xs, _, _ = trunk_layer(xs, c, state_rnn_fwd[:, l], state_rnn_bwd[:, l], layer_params(l))
    y_sample = rms_norm(xs, norm_f)

    return (y_prompt, y_sample, new_state_rnn_fwd, new_state_rnn_bwd)
```

```python
import contextlib
import math
import numpy as np
import concourse.bass as bass
import concourse.mybir as mybir
from concourse.bass_utils import run_bass_kernel_spmd

F32 = mybir.dt.float32
BF16 = mybir.dt.bfloat16
I32 = mybir.dt.int32
AF = mybir.ActivationFunctionType
ALU = mybir.AluOpType

D = 1024
DFF = 2816
NJ = DFF // 128
NTOK = 5120
S1_ORDER = 1
PRECAST = True
HZ_ALL = True
EPS = 1e-6

R_C0, R_C1, R_BMOD, R_N1, R_N2, R_N3, R_NF = 0, 8, 16, 88, 96, 104, 112
R_CW, R_CB, R_BR, R_BI, R_LAM, R_SF, R_SB = 128, 160, 168, 184, 200, 216, 224


class Prog:
    ENG = ("pe", "act", "dve", "pool", "sp")

    def __init__(self):
        self.ops = []
        self.last_w = {}
        self.readers = {}
        self.fence = None
        self.last_eng = {}
        self.pend_dma = []
        self.hz = HZ_ALL

    @contextlib.contextmanager
    def hazard(self):
        old = self.hz
        self.hz = True
        try:
            yield
        finally:
            self.hz = old

    def add(self, eng, emit, reads=(), writes=(), dma=False):
        i = len(self.ops)
        deps = set()
        lw = self.last_w
        rd = self.readers
        for k in reads:
            j = lw.get(k)
            if j is not None:
                deps.add(j)
        for k in writes:
            j = lw.get(k)
            if j is not None:
                deps.add(j)
            r = rd.get(k)
            if r:
                deps.update(r[0].values())
                deps.update(r[1])
        deps.discard(i)
        if self.fence is not None:
            deps.add(self.fence)
        if dma:
            self.pend_dma.append(i)
        else:
            self.last_eng[eng] = i
        for k in reads:
            r = rd.get(k)
            if r is None:
                r = rd[k] = ({}, [])
            if dma:
                r[1].append(i)
            else:
                r[0][eng] = i
        for k in writes:
            lw[k] = i
            rd[k] = ({}, [])
        self.ops.append(dict(eng=eng, emit=emit, dma=dma, deps=deps, hz=self.hz))

    def barrier(self, keys=None):
        i = len(self.ops)
        deps = set(self.last_eng.values()) | set(self.pend_dma)
        if self.fence is not None:
            deps.add(self.fence)
        self.ops.append(dict(eng="sp", emit=lambda e: e.nop(), dma=False, deps=deps, hz=False))
        self.fence = i
        self.last_eng = {"sp": i}
        self.pend_dma = []
        self.last_w = {}
        self.readers = {}

    def build(self, nc, stack, dma_ring):
        ops = self.ops
        n_dma = {e: 0 for e in self.ENG}
        dma_ops = {e: [] for e in self.ENG}
        for i, op in enumerate(ops):
            if op["dma"]:
                e = op["eng"]
                n = n_dma[e]
                K = dma_ring[e]
                op["dma_n"] = n
                if n >= K:
                    op["deps"].add(dma_ops[e][n - K])
                dma_ops[e].append(i)
                n_dma[e] += 1
        need = [False] * len(ops)
        for i, op in enumerate(ops):
            nd = set()
            for d in op["deps"]:
                p = ops[d]
                if (not p["dma"]) and (not op["dma"]) and p["eng"] == op["eng"] and not (op["hz"] and op["eng"] != "pe"):
                    continue
                nd.add(d)
                need[d] = True
            op["deps"] = nd
        esem = {e: stack.enter_context(nc.semaphore("s_" + e)) for e in self.ENG}
        dsem = {e: [stack.enter_context(nc.semaphore("d_%s%d" % (e, j))) for j in range(dma_ring[e])]
                for e in self.ENG if n_dma[e] > 0}
        cnt = {e: 0 for e in self.ENG}
        for i, op in enumerate(ops):
            if op["dma"]:
                e = op["eng"]
                n = op["dma_n"]
                K = dma_ring[e]
                op["sig"] = (dsem[e][n % K], 16 * (n // K + 1), 16)
            elif need[i]:
                e = op["eng"]
                cnt[e] += 1
                op["sig"] = (esem[e], cnt[e], 1)
            else:
                op["sig"] = None
        block = stack.enter_context(nc.Block())

        def run(engname, eng):
            waited = {}
            for i, op in enumerate(ops):
                if op["eng"] != engname:
                    continue
                for d in sorted(op["deps"]):
                    sem, val, _ = ops[d]["sig"]
                    key = id(sem)
                    if waited.get(key, 0) >= val:
                        continue
                    waited[key] = val
                    eng.wait_ge(sem, val)
                ins = op["emit"](eng)
                if op["sig"] is not None:
                    sem, val, inc = op["sig"]
                    ins.then_inc(sem, inc)
            if engname in dsem:
                n = n_dma[engname]
                K = dma_ring[engname]
                for j in range(min(n, K)):
                    last_n = ((n - 1 - j) // K) * K + j
                    val = 16 * (last_n // K + 1)
                    if waited.get(id(dsem[engname][j]), 0) < val:
                        eng.wait_ge(dsem[engname][j], val)

        @block.tensor
        def _(e):
            run("pe", e)

        @block.scalar
        def _(e):
            run("act", e)

        @block.vector
        def _(e):
            run("dve", e)

        @block.gpsimd
        def _(e):
            run("pool", e)

        @block.sync
        def _(e):
            run("sp", e)


def build_nc(debug=False, stop_after=None):
    nc = bass.Bass("TRN2", target_bir_lowering=False)
    P = Prog()

    def din(name, shape):
        return nc.dram_tensor(name, list(shape), F32, kind="ExternalInput").ap()

    def dout(name, shape, dt=F32):
        return nc.dram_tensor(name, list(shape), dt, kind="ExternalOutput").ap()

    xs = din("xs", [4096, D])
    xp = din("xp", [1024, D])
    vecs = din("vecs", [256, 128])
    ident_d = din("ident", [128, 128])
    gvb_d = din("gvb", [128, D])
    bs_d = din("bs", [1, D])
    w_mod = din("w_mod", [D, 9 * D])
    ffw = {1: (din("ff1_gate", [D, DFF]), din("ff1_up", [D, DFF]), din("ff1_down", [DFF, D])),
           2: (din("ff2_gate", [D, DFF]), din("ff2_up", [D, DFF]), din("ff2_down", [DFF, D]))}
    w_in = din("w_in", [D, 6 * D])
    w_r = din("w_r", [2, 16, 64, 64])
    w_i = din("w_i", [2, 16, 64, 64])
    w_s = din("w_s", [8, 128, 128])
    w_br = din("w_br", [D, D])
    w_bg = din("w_bg", [D, D])
    w_out = din("w_out", [D, D])
    ys = dout("ys", [4096, D])
    yp = dout("yp", [1024, D])
    stf_d = dout("stf", [4, D])
    stb_d = dout("stb", [4, D])
    skind = "ExternalOutput" if debug else "Internal"
    XA = nc.dram_tensor("XA", [8, 128, NTOK], F32, kind=skind).ap()
    HM = nc.dram_tensor("HM", [8, 128, NTOK], BF16, kind=skind).ap()
    YR = nc.dram_tensor("YR", [8, 128, NTOK], BF16, kind=skind).ap()
    if debug:
        DBG = nc.dram_tensor("DBG", [128, 2048], F32, kind="ExternalOutput").ap()
    GUS = {w: nc.dram_tensor("GUS%d" % w, [NJ // 2, 128, 2 * 8 * 256], BF16).ap() for w in (1, 2)}
    WDS = {w: nc.dram_tensor("WDS%d" % w, [8, 128, NJ * 128], BF16).ap() for w in (1, 2)}
    WXS = nc.dram_tensor("WXS", [8, 128, 2 * 8 * 128], BF16).ap()
    WVS = nc.dram_tensor("WVS", [128, 8 * D], BF16).ap()
    WUS = nc.dram_tensor("WUS", [8, 128, 8 * 128], BF16).ap()
    WM4S = nc.dram_tensor("WM4S", [8, 128, 4 * 8 * 128], BF16).ap()
    WOS = nc.dram_tensor("WOS", [8, 128, 8 * 128], BF16).ap()
    WSTS = nc.dram_tensor("WSTS", [128, 8 * 128], BF16).ap()
    BSHS = nc.dram_tensor("BSHS", [2, 8 * 128], BF16).ap()

    def xrows(t0, n):
        return xs[t0:t0 + n, :] if t0 < 4096 else xp[t0 - 4096:t0 - 4096 + n, :]

    def yrows(t0, n):
        return ys[t0:t0 + n, :] if t0 < 4096 else yp[t0 - 4096:t0 - 4096 + n, :]

    def dma(eng, out, in_, r, w):
        P.add(eng, lambda e: e.dma_start(out=out, in_=in_), r, w, dma=True)

    def wload(first, parts, flat, scr, keys, skey):
        if first:
            for (dst, src, k) in parts:
                dma("pool", dst, src, (), [k])
            dma("pool", scr, flat, keys, [skey])
        else:
            dma("sp", flat, scr, [skey], keys)

    def precast_job(stg, nelem, parts, scr, skey):
        for (dst, src) in parts:
            dma("pool", dst, src, (), [("STG", id(stg))])
        src_v = stg[:, 0:nelem]
        if len(scr.shape) == 3:
            src_v = src_v.rearrange("p (a b) -> p a b", a=scr.shape[1])
        dma("pool", scr, src_v, [("STG", id(stg))], [skey])

    def blk(src):
        return src.rearrange("(kc p) n -> p kc n", p=128)

    def mixer_precast_jobs(STG):
        jobs = []
        cnt = [0]

        def nxt():
            st = STG[cnt[0] % len(STG)]
            cnt[0] += 1
            return st
        for c in range(8):
            def j(c=c):
                st = nxt()
                v = st[:, 0:2048].rearrange("p (a k n) -> p a k n", a=2, k=8)
                precast_job(st, 2048, [(v[:, 0], blk(w_in[:, c * 128:(c + 1) * 128])),
                                       (v[:, 1], blk(w_in[:, D + c * 128:D + (c + 1) * 128]))], WXS[c], ("WXS", c))
            jobs.append(j)
        for h in range(2):
            def j(h=h):
                st = nxt()
                v = st[:, :].rearrange("p (k n) -> p k n", k=8)
                precast_job(st, 4096, [(v, blk(w_in[:, 3 * D + h * 512:3 * D + (h + 1) * 512]))],
                            WVS.rearrange("p (k n) -> p k n", k=8)[:, :, h * 512:(h + 1) * 512], "WVS")
            jobs.append(j)
        for g0 in range(0, 8, 4):
            def j(g0=g0):
                st = nxt()
                v = st[:, :].rearrange("p (g k n) -> p g k n", g=4, k=8)
                precast_job(st, 4096, [(v[:, i], blk(w_in[:, 2 * D + (g0 + i) * 128:2 * D + (g0 + i + 1) * 128])) for i in range(4)],
                            WUS[g0:g0 + 4].rearrange("g p e -> p g e"), ("WUS", g0))
            jobs.append(j)
        for f in range(8):
            def j(f=f):
                st = nxt()
                v = st[:, :].rearrange("p (a k n) -> p a k n", a=4, k=8)
                srcs = (w_in[:, 4 * D + f * 128:4 * D + (f + 1) * 128], w_in[:, 5 * D + f * 128:5 * D + (f + 1) * 128],
                        w_br[:, f * 128:(f + 1) * 128], w_bg[:, f * 128:(f + 1) * 128])
                precast_job(st, 4096, [(v[:, i], blk(srcs[i])) for i in range(4)], WM4S[f], ("WM4S", f))
            jobs.append(j)
        for m0 in range(0, 8, 4):
            def j(m0=m0):
                st = nxt()
                v = st[:, :].rearrange("p (g k n) -> p g k n", g=4, k=8)
                precast_job(st, 4096, [(v[:, i], blk(w_out[:, (m0 + i) * 128:(m0 + i + 1) * 128])) for i in range(4)],
                            WOS[m0:m0 + 4].rearrange("g p e -> p g e"), ("WOS", m0))
            jobs.append(j)
        return jobs

    def ff_precast_jobs(which, STG):
        wg, wu, wd = ffw[which]
        jobs = []
        cnt = [0]

        def nxt():
            st = STG[cnt[0] % len(STG)]
            cnt[0] += 1
            return st
        for jp in range(NJ // 2):
            def j(jp=jp):
                st = nxt()
                v = st[:, :].rearrange("p (a k n) -> p a k n", a=2, k=8)
                precast_job(st, 4096, [(v[:, 0], blk(wg[:, jp * 256:(jp + 1) * 256])), (v[:, 1], blk(wu[:, jp * 256:(jp + 1) * 256]))],
                            GUS[which][jp], ("GUS", which, jp))
            jobs.append(j)
        for m in range(8):
            def j(m=m):
                st = nxt()
                v = st[:, 0:NJ * 128].rearrange("p (k n) -> p k n", k=NJ)
                precast_job(st, NJ * 128, [(v, blk(wd[:, m * 128:(m + 1) * 128]))], WDS[which][m], ("WDS", which, m))
            jobs.append(j)
        return jobs

    def mm(ps_ap, pairs, r, w, first_start=True):
        def emit(e):
            n = len(pairs)
            ins = None
            for i, (l, rh) in enumerate(pairs):
                ins = e.matmul(ps_ap, lhsT=l, rhs=rh, start=(first_start and i == 0), stop=(i == n - 1))
            return ins
        P.add("pe", emit, r, w)

    def act(out, in_, func, r, w, bias=None, scale=None, accum=None):
        def emit(e):
            kw = {}
            if bias is not None:
                kw["bias"] = bias
            if scale is not None:
                kw["scale"] = scale
            if accum is not None:
                kw["accum_out"] = accum
            return e.activation(out=out, in_=in_, func=func, **kw)
        P.add("act", emit, r, w)

    def tt(eng, out, in0, in1, op, r, w):
        P.add(eng, lambda e: e.tensor_tensor(out=out, in0=in0, in1=in1, op=op), r, w)

    def ts(eng, out, in0, s1, s2, op0, op1, r, w):
        if op1 is None:
            P.add(eng, lambda e: e.tensor_scalar(out=out, in0=in0, scalar1=s1, scalar2=None, op0=op0), r, w)
        else:
            P.add(eng, lambda e: e.tensor_scalar(out=out, in0=in0, scalar1=s1, scalar2=s2, op0=op0, op1=op1), r, w)

    def stt(out, in0, scalar, in1, op0, op1, r, w):
        P.add("dve", lambda e: e.scalar_tensor_tensor(out=out, in0=in0, scalar=scalar, in1=in1, op0=op0, op1=op1), r, w)

    def cp(eng, out, in_, r, w):
        P.add(eng, lambda e: e.tensor_copy(out=out, in_=in_), r, w)

    def memset(eng, ap, val, w):
        P.add(eng, lambda e: e.memset(ap, val), (), w)

    with contextlib.ExitStack() as top:
        uid = [0]

        def sb(name, shape, dt, st=None):
            uid[0] += 1
            return (st or top).enter_context(nc.sbuf_tensor("%s_%d" % (name, uid[0]), list(shape), dt))

        PS = top.enter_context(nc.psum_tensor("PS", [128, 8, 512], F32))

        def bank(b):
            return PS[:, b, :]

        def kb(b):
            return ("ps", b)

        IDN = sb("IDN", [128, 128], F32)
        VT = sb("VT", [128, 256], F32)
        MOD = sb("MOD", [128, 2, 72], F32)
        NS = sb("NS", [128, 2, 3, 8], F32)
        GH = sb("GH", [128, 2, 3, 8], F32)
        LL = sb("LL", [128, 16], F32)
        L4 = sb("L4", [128, 16], F32)
        L8 = sb("L8", [128, 16], F32)
        HBR = sb("HBR", [128, 16], F32)
        HBI = sb("HBI", [128, 16], F32)
        TAB = sb("TAB", [128, 4, 64], F32)
        ONES = sb("ONES", [128, 128], BF16)
        NEGH = sb("NEGH", [128, 1], F32)
        BD = sb("BD", [128, 2, 2, 8, 128], BF16)
        STF = sb("STF", [128, 32], F32)
        STB = sb("STB", [128, 32], F32)
        SSV = sb("SSV", [128, 4], F32)

        def sh_ap(cond, k, m):
            return MOD[:, cond, (3 * k) * 8 + m:(3 * k) * 8 + m + 1]

        def ns_ap(cond, k, m):
            return NS[:, cond, k, m:m + 1]

        def gh_ap(cond, k, m):
            return GH[:, cond, k, m:m + 1]

        with contextlib.ExitStack() as ph, P.hazard():
            V0 = sb("V0", [128, 128], F32, ph)
            V1 = sb("V1", [128, 128], F32, ph)
            SB2 = sb("SB2", [128, 8, 2], BF16, ph)
            WM = [sb("WM%d" % i, [128, 8, D], BF16, ph) for i in range(2)]
            WSL = sb("WSL", [128, 8, 128], F32, ph)
            BS0 = sb("BS0", [1, D], F32, ph)
            BSHI = sb("BSHI", [1, D], BF16, ph)
            WST = sb("WSTset", [128, 8, 128], BF16, ph)
            BSLO = sb("BSLO", [1, D], BF16, ph)
            SIG = sb("SIG", [128, 16], F32, ph)
            QI = sb("QI", [128, 2], I32, ph)
            QF = sb("QF", [128, 2], F32, ph)
            FR = sb("FR", [128, 2], F32, ph)
            RI = sb("RI", [128, 64], I32, ph)
            RV = sb("RV", [128, 64], F32, ph)
            ANG = sb("ANG", [128, 4, 64], F32, ph)
            TQ = sb("TQ", [128, 4, 64], F32, ph)
            KI = sb("KI", [128, 4, 64], I32, ph)
            KF = sb("KF", [128, 4, 64], F32, ph)
            CM = sb("CM", [128, 4, 64], F32, ph)

            dma("sp", IDN[:], ident_d[:, :], (), ["IDN"])
            dma("sp", V0[:], vecs[0:128, :], (), ["V0"])
            dma("sp", V1[:], vecs[128:256, :], (), ["V1"])
            dma("sp", BS0[:], bs_d[:, :], (), ["BS0"])
            dma("sp", WSL[:], w_s.rearrange("g p q -> p g q"), (), ["WSL"])
            memset("pool", ONES[:], 1.0, ["ONES"])
            memset("pool", NEGH[:], -0.5, ["NEGH"])
            memset("pool", BD[:], 0.0, ["BD"])
            memset("dve", STF[:], 0.0, ["STF"])
            memset("dve", STB[:], 0.0, ["STB"])
            P.add("pe", lambda e: e.transpose(out=PS[:, 7, 0:128], in_=V0[:], identity=IDN[:]), ["V0", "IDN"], [kb(7)])
            P.add("pe", lambda e: e.transpose(out=PS[:, 7, 128:256], in_=V1[:], identity=IDN[:]), ["V1", "IDN"], [kb(7)])
            cp("dve", VT[:], PS[:, 7, 0:256], [kb(7)], ["VT"])
            for cond in range(2):
                act(SB2[:, :, cond], VT[:, 8 * cond:8 * cond + 8], AF.Silu, ["VT"], ["SB2"])
            act(SIG[:], VT[:, R_LAM:R_LAM + 16], AF.Sigmoid, ["VT"], ["SIG"])
            act(LL[:], SIG[:], AF.Ln, ["SIG"], ["LL"])
            ts("dve", L4[:], LL[:], 4.0, None, ALU.mult, None, ["LL"], ["L4"])
            ts("dve", L8[:], LL[:], 8.0, None, ALU.mult, None, ["LL"], ["L8"])
            ts("dve", HBR[:], VT[:, R_BR:R_BR + 16], 0.5, None, ALU.mult, None, ["VT"], ["HBR"])
            ts("dve", HBI[:], VT[:, R_BI:R_BI + 16], 0.5, None, ALU.mult, None, ["VT"], ["HBI"])
            for g in range(8):
                P.add("pe", lambda e, g=g: e.transpose(out=PS[:, 5, (g % 4) * 128:(g % 4) * 128 + 128], in_=WSL[:, g, :],
                                                       identity=IDN[:]), ["WSL", "IDN"], [kb(5)])
                cp("dve", WST[:, g, :], PS[:, 5, (g % 4) * 128:(g % 4) * 128 + 128], [kb(5)], ["WST"])
            cp("dve", BSHI[:], BS0[:], ["BS0"], ["BSHI"])
            tt("dve", BSLO[:], BS0[:], BSHI[:], ALU.subtract, ["BS0", "BSHI"], ["BSLO"])
            dma("sp", BSHS[0:1, :], BSHI[0:1, :], ["BSHI"], ["BSHS"])
            dma("sp", BSHS[1:2, :], BSLO[0:1, :], ["BSLO"], ["BSHS"])
            dma("sp", WSTS[:, :], WST[:].rearrange("p g q -> p (g q)"), ["WST"], ["WSTS"])
            P.add("pool", lambda e: e.iota(QI[:], [[128, 2]], base=0, channel_multiplier=1), (), ["QI"])
            P.add("pool", lambda e: e.iota(RI[:], [[1, 64]], base=0, channel_multiplier=0), (), ["RI"])
            cp("dve", QF[:], QI[:], ["QI"], ["QF"])
            cp("dve", RV[:], RI[:], ["RI"], ["RV"])
            act(FR[:], QF[:], AF.Exp, ["QF"], ["FR"], scale=-math.log(10000.0) / 256.0)
            for q2 in range(2):
                ts("dve", ANG[:, q2, :], RV[:], FR[:, q2:q2 + 1], None, ALU.mult, None, ["RV", "FR"], ["ANG"])
            ts("dve", ANG[:, 2:4, :], ANG[:, 0:2, :], math.pi / 2, None, ALU.add, None, ["ANG"], ["ANG"])
            ts("dve", TQ[:], ANG[:], 1.0 / (2 * math.pi), 0.5, ALU.mult, ALU.add, ["ANG"], ["TQ"])
            cp("dve", KI[:], TQ[:], ["TQ"], ["KI"])
            cp("dve", KF[:], KI[:], ["KI"], ["KF"])
            tt("dve", CM[:], KF[:], TQ[:], ALU.is_gt, ["KF", "TQ"], ["CM"])
            tt("dve", KF[:], KF[:], CM[:], ALU.subtract, ["KF", "CM"], ["KF"])
            stt(ANG[:], KF[:], -2 * math.pi, ANG[:], ALU.mult, ALU.add, ["KF", "ANG"], ["ANG"])
            ts("dve", ANG[:], ANG[:], 3.14159, -3.14159, ALU.min, ALU.max, ["ANG"], ["ANG"])
            act(TAB[:], ANG[:], AF.Sin, ["ANG"], ["TAB"])
            WF = [sb("WF", [128, 8, D], F32, ph) for _ in range(2)]
            WMo = [sb("WMo", [128, 8, D], BF16, ph) for _ in range(2)]
            hwn = 0
            for i in (0, 1, 2, 3, 5, 6, 7, 4, 8):
                src_i = w_mod[:, i * D:(i + 1) * D].rearrange("(kc p) n -> p kc n", p=128)
                if i == 4:
                    slot = 0
                    wt = WM[slot]
                    wkey = ("WM", slot)
                    dma("pool", wt[:], src_i, (), [wkey])
                else:
                    slot = hwn % 2
                    hwn += 1
                    wt = WMo[slot]
                    wkey = ("WMo", slot)
                    dma("sp", WF[slot][:], src_i, (), [("WF", slot)])
                    cp("dve", wt[:, 0:4, :], WF[slot][:, 0:4, :], [("WF", slot)], [wkey])
                    act(wt[:, 4:8, :], WF[slot][:, 4:8, :], AF.Copy, [("WF", slot)], [wkey])

                def emit_mod(e, i=i, wt=wt):
                    ins = None
                    for m in range(8):
                        for kc in range(8):
                            ins = e.matmul(PS[:, 6, (i * 8 + m) * 2:(i * 8 + m) * 2 + 2],
                                           lhsT=wt[:, kc, m * 128:(m + 1) * 128], rhs=SB2[:, kc, :],
                                           start=(kc == 0), stop=(kc == 7))
                    return ins
                P.add("pe", emit_mod, [wkey, "SB2"], [kb(6)])
            for d in range(2):
                for kind, wsrc in enumerate((w_r, w_i)):
                    v = wsrc[d].rearrange("(c two) i j -> two i c j", two=2)
                    for h in range(2):
                        dma("pool", BD[64 * h:64 * h + 64, d, kind, :, 64 * h:64 * h + 64], v[h], ["BD"], ["BD"])
            psmod = PS[:, 6, 0:144].rearrange("p (r c) -> p r c", c=2)
            for cond in range(2):
                tt("dve", MOD[:, cond, :], psmod[:, :, cond], VT[:, R_BMOD:R_BMOD + 72], ALU.add, [kb(6), "VT"], ["MOD"])
            for cond in range(2):
                for k in range(3):
                    stt(NS[:, cond, k, :], MOD[:, cond, (3 * k + 1) * 8:(3 * k + 1) * 8 + 8], 1.0,
                        VT[:, R_N1 + 8 * k:R_N1 + 8 * k + 8], ALU.add, ALU.mult, ["MOD", "VT"], ["NS"])
                    ts("dve", GH[:, cond, k, :], MOD[:, cond, (3 * k + 2) * 8:(3 * k + 2) * 8 + 8], 0.5, None,
                       ALU.mult, None, ["MOD"], ["GH"])
            if debug:
                dma("sp", DBG[:, 0:256], VT[:], ["VT"], ["DBG"])
                dma("sp", DBG[:, 256:400], MOD[:].rearrange("p a b -> p (a b)"), ["MOD"], ["DBG"])
                dma("sp", DBG[:, 400:656], TAB[:].rearrange("p a b -> p (a b)"), ["TAB"], ["DBG"])
                dma("sp", DBG[:, 656:672], LL[:], ["LL"], ["DBG"])
            P.barrier(["IDN", "VT", "MOD", "NS", "GH", "LL", "L4", "L8", "HBR", "HBI", "TAB", "ONES", "NEGH", "GVB",
                       "BSH", "WST", "BD", "STF", "STB"] + [kb(b) for b in range(8)])

        consts = ["IDN", "VT", "MOD", "NS", "GH", "L4", "L8", "HBR", "HBI", "TAB", "ONES", "NEGH", "GVB", "BSH", "WST", "BD"]

        def norm_stats(xg, B):
            SQ, LNT, RS = B["SQ"], B["MS"], B["RS"]
            for tti in range(2):
                cols = slice(tti * 512, (tti + 1) * 512)
                xk = [("xg", m, tti) for m in range(8)]
                act(SQ[:], xg[:, :, cols], AF.Square, xk, ["SQ"])
                mm(bank(6 + tti), [(ONES[:], SQ[:, m, :]) for m in range(8)], ["SQ", "ONES"], [kb(6 + tti)])
            act(LNT[:], PS[:, 6:8, :], AF.Ln, [kb(6), kb(7), "EPSC"], ["MS"], bias=EPSC[:, 0:1], scale=1.0 / D)
            act(RS[:], LNT[:], AF.Exp, ["MS"], ["RS"], scale=-0.5)

        def norm_apply(xg, tti, cond, k, B, xmod_out=None, yf_out=None):
            cols = slice(tti * 512, (tti + 1) * 512)
            RS, TMP = B["RS"], B["TMP"]
            for m in range(8):
                if xmod_out is not None:
                    sl = m % 2
                    stt(TMP[sl][:], xg[:, m, cols], ns_ap(cond, k, m), RS[:, tti, :], ALU.mult, ALU.mult,
                        [("xg", m, tti), "RS", "NS"], [("TMP", sl)])
                    act(xmod_out[:, m, cols], TMP[sl][:], AF.Identity, [("TMP", sl), "MOD"], [("xmod", m, tti)],
                        bias=sh_ap(cond, k, m))
                else:
                    stt(yf_out[:, m, :], xg[:, m, cols], VT[:, R_NF + m:R_NF + m + 1], RS[:, tti, :], ALU.mult, ALU.mult,
                        [("xg", m, tti), "RS", "VT"], [("YF", m)])

        def ffn_group(which, cond, k_gate, xg, B, first=True):
            wg, wu, wd = ffw[which]
            xmod, H, GU, WD, SG = B["xmod"], B["H"], B["GU"], B["WD"], B["SG"]
            cnt = 0
            for jp in range(NJ // 2):
                slot = jp % 3
                wload(first, [(GU[slot][:, 0, :, :], wg[:, jp * 256:(jp + 1) * 256].rearrange("(kc p) n -> p kc n", p=128), ("GU", slot, 0)),
                              (GU[slot][:, 1, :, :], wu[:, jp * 256:(jp + 1) * 256].rearrange("(kc p) n -> p kc n", p=128), ("GU", slot, 1))],
                      GU[slot][:].rearrange("p a k n -> p (a k n)"), GUS[which][jp], [("GU", slot, 0), ("GU", slot, 1)],
                      ("GUS", which, jp))
                for jj in range(2):
                    j = 2 * jp + jj
                    for tti in range(2):
                        cols = slice(tti * 512, (tti + 1) * 512)
                        bg = cnt % 2
                        bu = 2 + cnt % 2
                        cnt += 1
                        xk = [("xmod", m, tti) for m in range(8)]
                        mm(bank(bg), [(GU[slot][:, 0, kc, jj * 128:(jj + 1) * 128], xmod[:, kc, cols]) for kc in range(8)],
                           [("GU", slot, 0)] + xk, [kb(bg)])
                        mm(bank(bu), [(GU[slot][:, 1, kc, jj * 128:(jj + 1) * 128], xmod[:, kc, cols]) for kc in range(8)],
                           [("GU", slot, 1)] + xk, [kb(bu)])
                        sl = cnt % 2
                        act(SG[sl][:], bank(bg), AF.Silu, [kb(bg)], [("SG", sl)])
                        tt("dve", H[:, j, cols], SG[sl][:], bank(bu), ALU.mult, [("SG", sl), kb(bu)], [("H", j, tti)])
            cnt = 0
            for m in range(8):
                slot = m % 2
                wload(first, [(WD[slot][:], wd[:, m * 128:(m + 1) * 128].rearrange("(kc p) n -> p kc n", p=128), ("WD", slot))],
                      WD[slot][:].rearrange("p k n -> p (k n)"), WDS[which][m], [("WD", slot)], ("WDS", which, m))
                for tti in range(2):
                    cols = slice(tti * 512, (tti + 1) * 512)
                    b = 4 + cnt % 2
                    cnt += 1
                    mm(bank(b), [(WD[slot][:, j, :], H[:, j, cols]) for j in range(NJ)],
                       [("WD", slot)] + [("H", j, tti) for j in range(NJ)], [kb(b)])
                    stt(xg[:, m, cols], bank(b), gh_ap(cond, k_gate, m), xg[:, m, cols], ALU.mult, ALU.add,
                        [kb(b), "GH", ("xg", m, tti)], [("xg", m, tti)])

        def ff_buffers(ph, first):
            B = {}
            B["xmod"] = sb("xmod", [128, 8, 1024], BF16, ph)
            B["H"] = sb("H", [128, NJ, 1024], BF16, ph)
            B["GU"] = [sb("GU%d" % i, [128, 2, 8, 256], BF16, ph) for i in range(3)]
            B["WD"] = [sb("WD%d" % i, [128, NJ, 128], BF16, ph) for i in range(2)]
            B["SG"] = [sb("SG%d" % i, [128, 512], F32, ph) for i in range(2)]
            B["SQ"] = sb("SQ", [128, 8, 512], BF16, ph)
            B["MS"] = sb("MS", [128, 2, 512], F32, ph)
            B["RS"] = sb("RS", [128, 2, 512], F32, ph)
            B["TMP"] = [sb("TMP%d" % i, [128, 512], F32, ph) for i in range(2)]
            if first:
                B["XT"] = [sb("XT%d" % i, [128, D], F32, ph) for i in range(4)]
            else:
                B["YF"] = sb("YF", [128, 8, 512], F32, ph)
                B["YT"] = [sb("YT%d" % i, [128, D], F32, ph) for i in range(2)]
            return B

        def ff_keys():
            ks = [("xmod", m, t) for m in range(8) for t in range(2)] + [("H", j, t) for j in range(NJ) for t in range(2)]
            ks += [("GU", s, i) for s in range(3) for i in range(2)] + [("WD", s) for s in range(2)]
            ks += [("SG", 0), ("SG", 1), "SQ", "MS", "RS", ("TMP", 0), ("TMP", 1)]
            ks += [("XT", i) for i in range(4)] + [("YF", m) for m in range(8)] + [("YT", 0), ("YT", 1)]
            ks += [("xg", m, t) for m in range(8) for t in range(2)]
            return ks

        allps = [kb(b) for b in range(8)]

        def ff1_group(g, xg, B):
            cond = 0 if g < 4 else 1
            XT = B["XT"]
            for tti in range(2):
                T = 2 * g + tti
                t0 = T * 512
                cols = slice(tti * 512, (tti + 1) * 512)
                for s in range(4):
                    dma("sp", XT[s][:], xrows(t0 + s * 128, 128), (), [("XT", s)])
                for m in range(8):
                    b = 4 + m % 4

                    def emit_tr(e, m=m, b=b):
                        ins = None
                        for s in range(4):
                            ins = e.transpose(out=PS[:, b, s * 128:(s + 1) * 128], in_=XT[s][:, m * 128:(m + 1) * 128],
                                              identity=IDN[:])
                        return ins
                    P.add("pe", emit_tr, [("XT", s) for s in range(4)] + ["IDN"], [kb(b)])
                    if cond == 0:
                        pv = bank(b).rearrange("p (a b) -> p a b", b=64)
                        ov = xg[:, m, cols].rearrange("p (a b) -> p a b", b=64)
                        if m < 4:
                            tv = TAB[:, m, 8 * T:8 * T + 8].unsqueeze(2).broadcast_to([128, 8, 64])
                        else:
                            tv = TAB[:, m - 4, :].unsqueeze(1).broadcast_to([128, 8, 64])
                        tt("dve", ov, pv, tv, ALU.add, [kb(b), "TAB"], [("xg", m, tti)])
                    else:
                        cp("dve", xg[:, m, cols], bank(b), [kb(b)], [("xg", m, tti)])
            norm_stats(xg, B)
            for tti in range(2):
                norm_apply(xg, tti, cond, 0, B, xmod_out=B["xmod"])
            ffn_group(1, cond, 0, xg, B, first=(g == 0))
            norm_stats(xg, B)
            for tti in range(2):
                T = 2 * g + tti
                t0 = T * 512
                cols = slice(tti * 512, (tti + 1) * 512)
                norm_apply(xg, tti, cond, 1, B, xmod_out=B["xmod"])
                dma("sp", HM[:, :, t0:t0 + 512].rearrange("m p t -> p m t"), B["xmod"][:, :, cols],
                    [("xmod", m, tti) for m in range(8)], [("HM", T)])
                dma("sp", XA[:, :, t0:t0 + 512].rearrange("m p t -> p m t"), xg[:, :, cols],
                    [("xg", m, tti) for m in range(8)], [("XA", T)])

        def s1_group(sg):
            if sg == 0:
                T0, nt, nseq, L, cond = 0, 8, 1, 4096, 0
            else:
                T0, nt, nseq, L, cond = 8, 2, 4, 256, 1
            Ts = nt * 512
            spt = 512 // L if L < 512 else 1
            with contextlib.ExitStack() as ph:
                HMr = [sb("HMr", [128, 8, 512], BF16, ph) for _ in range(3)]
                XR = sb("XR", [128, nseq * (L + 3)], F32, ph)
                XC = sb("XC", [128, nseq, L], F32, ph)
                XCB = sb("XCB", [128, Ts], BF16, ph)
                AB = [[sb("ABI", [128, nseq, L], F32, ph) for _ in range(3)] for _ in range(2)]
                GGf = [sb("GGf", [128, Ts], BF16, ph) for _ in range(2)]
                YRt = [sb("YRt", [128, 512], BF16, ph) for _ in range(4)]
                WX = [sb("WX", [128, 2, 8, 128], BF16, ph) for _ in range(2)]
                TH = [sb("TH", [128, 512], F32, ph) for _ in range(2)]
                XR3 = XR[:, :].rearrange("p (s l) -> p s l", l=L + 3)
                XCf = XC[:].rearrange("p s l -> p (s l)")
                flat = lambda t3: t3[:].rearrange("p s l -> p (s l)")
                tk = lambda name: [(name, t) for t in range(nt)]
                memset("dve", XR3[:, :, 0:2], 0.0, ["XRhalo"])
                memset("dve", XR3[:, :, L + 2:L + 3], 0.0, ["XRhalo"])
                cnt = [0, 0, 0]

                def load_wx(c_):
                    sl_ = c_ % 2
                    wload(sg == 0 and not PRECAST, [(WX[sl_][:, 0, :, :], w_in[:, c_ * 128:(c_ + 1) * 128].rearrange("(kc p) n -> p kc n", p=128), ("WX", sl_, 0)),
                                    (WX[sl_][:, 1, :, :], w_in[:, D + c_ * 128:D + (c_ + 1) * 128].rearrange("(kc p) n -> p kc n", p=128), ("WX", sl_, 1))],
                          WX[sl_][:].rearrange("p a k n -> p (a k n)"), WXS[c_], [("WX", sl_, 0), ("WX", sl_, 1)], ("WXS", c_))

                def prep_tile(c, t):
                    slot = c % 2
                    GGc = GGf[c % 2]
                    T = T0 + t
                    cols = slice(t * 512, (t + 1) * 512)
                    hs = cnt[1] % 3
                    cnt[1] += 1
                    dma("sp", HMr[hs][:], HM[:, :, T * 512:(T + 1) * 512].rearrange("m p t -> p m t"), [("HM", T)], [("HMr", hs)])
                    bx = cnt[0] % 2
                    bgr = 6 + cnt[0] % 2
                    cnt[0] += 1
                    mm(bank(bx), [(WX[slot][:, 0, kc, :], HMr[hs][:, kc, :]) for kc in range(8)],
                       [("WX", slot, 0), ("HMr", hs)], [kb(bx)])
                    mm(bank(bgr), [(WX[slot][:, 1, kc, :], HMr[hs][:, kc, :]) for kc in range(8)],
                       [("WX", slot, 1), ("HMr", hs)], [kb(bgr)])
                    if L >= 512:
                        ov = XR3[:, 0, 2 + t * 512:2 + (t + 1) * 512]
                        iv = bank(bx)
                    else:
                        ov = XR3[:, t * spt:(t + 1) * spt, 2:2 + L]
                        iv = bank(bx).rearrange("p (s l) -> p s l", l=L)
                    act(ov, iv, AF.Copy, [kb(bx)], [("XR", t)])
                    act(GGc[:, cols], bank(bgr), AF.Copy, [kb(bgr)], [("GGf", c % 2, t)])

                def stage_gelu(c):
                    GGc = GGf[c % 2]
                    gk = [("GGf", c % 2, t) for t in range(nt)]
                    act(GGc[:], GGc[:], AF.Gelu_apprx_tanh, gk, gk)

                def stage_V(c):
                    ts("dve", XC[:], XR3[:, :, 0:L], VT[:, R_CW + c:R_CW + c + 1], VT[:, R_CB + c:R_CB + c + 1],
                       ALU.mult, ALU.add, tk("XR") + ["XRhalo", "VT"], tk("XC"))
                    for k in range(1, 4):
                        stt(XC[:], XR3[:, :, k:k + L], VT[:, R_CW + 8 * k + c:R_CW + 8 * k + c + 1], XC[:],
                            ALU.mult, ALU.add, tk("XR") + tk("XC") + ["VT", "XRhalo"], tk("XC"))
                    act(XCB[:], XCf, AF.Copy, tk("XC"), tk("XCB"))

                def stage_G_act(c, d, prep_c=None):
                    dc = d * 8 + c
                    A_, B_, I_ = AB[d]
                    Af, Bf, If = flat(A_), flat(B_), flat(I_)
                    for t in range(nt):
                        cols = slice(t * 512, (t + 1) * 512)
                        br = 2 + cnt[2] % 2
                        bi = 4 + cnt[2] % 2
                        sl = cnt[2] % 2
                        cnt[2] += 1
                        mm(bank(br), [(BD[:, d, 0, c, :], XCB[:, cols])], ["BD", ("XCB", t)], [kb(br)])
                        mm(bank(bi), [(BD[:, d, 1, c, :], XCB[:, cols])], ["BD", ("XCB", t)], [kb(bi)])
                        act(TH[sl][:], bank(br), AF.Tanh, [kb(br), "HBR"], [("TH", sl)], bias=HBR[:, dc:dc + 1], scale=0.5)
                        act(Af[:, cols], TH[sl][:], AF.Exp, [("TH", sl), "L4"], [("A", d, t)],
                            bias=L4[:, dc:dc + 1], scale=L4[:, dc:dc + 1])
                        tt("dve", Bf[:, cols], Af[:, cols], Af[:, cols], ALU.mult, [("A", d, t)], [("B", d, t)])
                        act(If[:, cols], bank(bi), AF.Tanh, [kb(bi), "HBI"], [("I", d, t)], bias=HBI[:, dc:dc + 1], scale=0.5)
                        if prep_c is not None:
                            prep_tile(prep_c, t)
                    if prep_c is not None and prep_c + 1 < 8:
                        load_wx(prep_c + 1)

                def stage_sqrt(c, d):
                    Bf = flat(AB[d][1])
                    ka = lambda n: [(n, d, t) for t in range(nt)]
                    act(Bf, Bf, AF.Sqrt, ka("B") + ["QUART"], ka("B"), bias=QUART[:, 0:1], scale=-0.25)

                def stage_G_dve1(c, d):
                    A_, B_, I_ = AB[d]
                    If = flat(I_)
                    ka = lambda n: [(n, d, t) for t in range(nt)]
                    stt(If, If, 1.0, XCf, ALU.add, ALU.mult, ka("I") + tk("XC"), ka("I"))

                def stage_G_dve2(c, d):
                    A_, B_, I_ = AB[d]
                    Af, Bf, If = flat(A_), flat(B_), flat(I_)
                    ka = lambda n: [(n, d, t) for t in range(nt)]
                    tt("dve", Bf, Bf, If, ALU.mult, ka("B") + ka("I"), ka("B"))
                    with P.hazard():
                        for s_ in range(nseq):
                            if sg == 0:
                                r0 = R_SF if d == 0 else R_SB
                                init = VT[:, r0 + c:r0 + c + 1]
                            else:
                                init = 0.0
                            if d == 0:
                                P.add("dve", lambda e, s_=s_, init=init, A_=A_, B_=B_: e.tensor_tensor_scan(
                                    out=B_[:, s_, :], data0=A_[:, s_, :], data1=B_[:, s_, :], initial=init,
                                    op0=ALU.mult, op1=ALU.add), ka("A") + ka("B") + ["VT"], ka("B"))
                            else:
                                P.add("dve", lambda e, s_=s_, init=init, A_=A_, B_=B_: e.tensor_tensor_scan(
                                    out=B_[:, s_, ::-1], data0=A_[:, s_, ::-1], data1=B_[:, s_, ::-1], initial=init,
                                    op0=ALU.mult, op1=ALU.add), ka("A") + ka("B") + ["VT"], ka("B"))
                        if sg == 1:
                            if d == 0:
                                cp("dve", STF[:, :].rearrange("p (s c) -> p s c", c=8)[:, :, c], B_[:, :, L - 1], ka("B"), ["STF"])
                            else:
                                cp("dve", STB[:, :].rearrange("p (s c) -> p s c", c=8)[:, :, c], B_[:, :, 0], ka("B"), ["STB"])

                def stage_ADD(c):
                    B0f, B1f = flat(AB[0][1]), flat(AB[1][1])
                    with P.hazard():
                        tt("dve", B0f, B0f, B1f, ALU.add, [("B", 0, t) for t in range(nt)] + [("B", 1, t) for t in range(nt)],
                           [("B", 0, t) for t in range(nt)])

                def stage_Y(c):
                    B0f = flat(AB[0][1])
                    GGc = GGf[c % 2]
                    for t in range(nt):
                        cols = slice(t * 512, (t + 1) * 512)
                        ys_ = t % 4
                        tt("dve", YRt[ys_][:], GGc[:, cols], B0f[:, cols], ALU.mult, [("GGf", c % 2, t), ("B", 0, t)], [("YRt", ys_)])
                        t0 = (T0 + t) * 512
                        dma("pool", YR[c, :, t0:t0 + 512], YRt[ys_][:], [("YRt", ys_)], [("YR", sg, c, t)])

                pre2 = []
                if False and sg == 1 and PRECAST:
                    STG2 = [sb("STG2", [128, 4096], BF16, ph) for _ in range(3)]
                    pre2 = ff_precast_jobs(2, STG2)
                load_wx(0)
                for t in range(nt):
                    prep_tile(0, t)
                load_wx(1)
                stage_V(0)
                for c in range(8):
                    for _ in range(3):
                        if pre2:
                            pre2.pop(0)()
                    nxt = c + 1 if c + 1 < 8 else None
                    if sg == 0:
                        stage_G_act(c, 0)
                        stage_sqrt(c, 0)
                        stage_G_dve1(c, 0)
                        stage_G_dve2(c, 0)
                        stage_G_act(c, 1, nxt)
                        stage_sqrt(c, 1)
                    else:
                        stage_G_act(c, 0)
                        stage_G_act(c, 1, nxt)
                        stage_sqrt(c, 0)
                        stage_sqrt(c, 1)
                        stage_G_dve1(c, 0)
                        stage_G_dve2(c, 0)
                    stage_gelu(c)
                    stage_G_dve1(c, 1)
                    if nxt is not None:
                        stage_V(nxt)
                    stage_G_dve2(c, 1)
                    stage_ADD(c)
                    stage_Y(c)
                if sg == 1:
                    STT_ = sb("STT_", [32, 128], F32, ph)
                    for nm, src, dst in (("f", STF, stf_d), ("b", STB, stb_d)):
                        P.add("pe", lambda e, src=src: e.transpose(out=PS[0:32, 0, 0:128], in_=src[:, :], identity=IDN[:]),
                              ["STF", "STB", "IDN"], [kb(0)])
                        cp("dve", STT_[:], PS[0:32, 0, 0:128], [kb(0)], ["STT_"])
                        dma("pool", dst.rearrange("s (c p) -> (s c) p", p=128), STT_[:], ["STT_"], [("st", nm)])
                P.barrier()

        def s2_keys():
            ks = [("HMg", t) for t in range(2)] + [("YRg", t) for t in range(2)] + ["WV", ("WU", 0), ("WU", 1)]
            ks += [("GV", 0), ("GV", 1), "JUNK", "SSV", "RSV"] + [("VN", n) for n in range(4)]
            ks += [("GUt", 0), ("GUt", 1)] + [("YG", g, t) for g in range(8) for t in range(2)]
            ks += [("MG", f, t) for f in range(8) for t in range(2)] + [("WM4", s, i) for s in range(2) for i in range(4)]
            ks += [("TA", 0), ("TA", 1), ("TB", 0), ("TB", 1), ("M1", 0), ("M1", 1), ("M2", 0), ("M2", 1), ("WO", 0), ("WO", 1)]
            return ks

        S2B = {}

        def s2_group(g, xg, ph_outer):
            cond = 0 if g < 4 else 1
            firstg = (len(S2B) == 0)
            castg = firstg and not PRECAST

            def sbm(name, shape, dt):
                if name not in S2B:
                    S2B[name] = sb(name, shape, dt, ph_outer)
                return S2B[name]
            if True:
                ph = None
                HMg = sbm("HMg", [128, 8, 1024], BF16)
                YRg = sbm("YRg", [128, 8, 1024], BF16)
                WV = sbm("WV", [128, 8, D], BF16)
                GVB = sbm("GVB", [128, D], F32)
                BSH = sbm("BSH", [2, 8, 128], BF16)
                WST = sbm("WST", [128, 8, 128], BF16)
                if firstg:
                    dma("sp", GVB[:], gvb_d[:, :], (), ["GVB"])
                    dma("sp", BSH[:].rearrange("o g p -> o (g p)"), BSHS[:, :], (), ["BSH"])
                    dma("sp", WST[:].rearrange("p g q -> p (g q)"), WSTS[:, :], (), ["WST"])
                WU = [sbm("WU%d" % i, [128, 8, 128], BF16) for i in range(3)]
                GV = [sbm("GV%d" % i, [128, D], F32) for i in range(2)]
                RSV = sbm("RSV", [128, 4], F32)
                VN = sbm("VN", [128, 4, D], BF16)
                GUt = [sbm("GUt%d" % i, [128, 512], F32) for i in range(2)]
                YG = sbm("YG", [128, 8, 1024], BF16)
                MG = sbm("MG", [128, 8, 1024], BF16)
                WM4 = [sbm("WM4%d" % i, [128, 4, 8, 128], BF16) for i in range(3)]
                TA = [sbm("TA%d" % i, [128, 512], F32) for i in range(2)]
                TB = [sbm("TB%d" % i, [128, 512], F32) for i in range(2)]
                M1 = [sbm("M10", [128, 512], F32)] * 2
                M2 = [sbm("M20", [128, 512], F32)] * 2
                WO = WU
                if firstg:
                    wload(castg, [(WV[:], w_in[:, 3 * D:4 * D].rearrange("(kc p) n -> p kc n", p=128), "WV")],
                          WV[:].rearrange("p k n -> p (k n)"), WVS, ["WV"], "WVS")
                def load_hm_yr(g_):
                    for tti_ in range(2):
                        T_ = 2 * g_ + tti_
                        cols_ = slice(tti_ * 512, (tti_ + 1) * 512)
                        dma("sp", HMg[:, :, cols_], HM[:, :, T_ * 512:(T_ + 1) * 512].rearrange("m p t -> p m t"), [("HM", T_)], [("HMg", tti_)])
                    for tti_ in range(2):
                        T_ = 2 * g_ + tti_
                        cols_ = slice(tti_ * 512, (tti_ + 1) * 512)
                        dma("sp", YRg[:, :, cols_], YR[:, :, T_ * 512:(T_ + 1) * 512].rearrange("m p t -> p m t"), (), [("YRg", tti_)])
                if firstg:
                    load_hm_yr(g)
                cnt = 0
                for tti in range(2):
                    cols = slice(tti * 512, (tti + 1) * 512)
                    for n in range(4):
                        c0 = tti * 512 + n * 128
                        sl = n % 2
                        b0 = 0 if sl == 0 else 6
                        for half in range(2):
                            mm(bank(b0 + half), [(HMg[:, kc, c0:c0 + 128], WV[:, kc, half * 512:(half + 1) * 512]) for kc in range(8)],
                               [("HMg", tti), "WV"], [kb(b0 + half)])
                        act(GV[sl][:].rearrange("p (h n) -> p h n", n=512), PS[:, b0:b0 + 2, :], AF.Gelu_apprx_tanh,
                            [kb(b0), kb(b0 + 1)], [("GV", sl)])
                        act(VN[:, n, :], GV[sl][:], AF.Square, [("GV", sl)], [("VN", n), ("SSV", n)], accum=SSV[:, n:n + 1])
                        ts("dve", RSV[:, n:n + 1], SSV[:, n:n + 1], 1.0 / D, EPS, ALU.mult, ALU.add, [("SSV", n)], [("RSV", n)])
                        tt("pool", RSV[:, n:n + 1], RSV[:, n:n + 1], NEGH[:, 0:1], ALU.pow, [("RSV", n), "NEGH"], [("RSV", n)])
                        stt(VN[:, n, :], GV[sl][:], RSV[:, n:n + 1], GVB[:], ALU.mult, ALU.mult,
                            [("GV", sl), ("RSV", n), "GVB"], [("VN", n)])
                    for gi in range(8):
                        slot = gi % 3
                        wload(castg and tti == 0,
                              [(WU[slot][:], w_in[:, 2 * D + gi * 128:2 * D + (gi + 1) * 128].rearrange("(kc p) n -> p kc n", p=128), ("WU", slot))],
                              WU[slot][:].rearrange("p k n -> p (k n)"), WUS[gi], [("WU", slot)], ("WUS", gi))
                        bu = 2 + cnt % 2
                        bm = 4 + cnt % 2
                        sl = cnt % 2
                        cnt += 1
                        mm(bank(bu), [(WU[slot][:, kc, :], HMg[:, kc, cols]) for kc in range(8)],
                           [("WU", slot), ("HMg", tti)], [kb(bu)])
                        act(GUt[sl][:], bank(bu), AF.Gelu_apprx_tanh, [kb(bu)], [("GUt", sl)])

                        def emit_mix(e, gi=gi, bm=bm):
                            e.matmul(PS[:, bm, :].rearrange("p (a b) -> p a b", b=128), lhsT=ONES[0:2, :],
                                     rhs=BSH[0:2, gi, :].unsqueeze(1).broadcast_to([2, 4, 128]), start=True, stop=False)
                            ins = None
                            for n in range(4):
                                ins = e.matmul(PS[:, bm, n * 128:(n + 1) * 128], lhsT=VN[:, n, gi * 128:(gi + 1) * 128],
                                               rhs=WST[:, gi, :], start=False, stop=(n == 3))
                            return ins
                        P.add("pe", emit_mix, ["ONES", "BSH", "WST"] + [("VN", n) for n in range(4)], [kb(bm)])
                        tt("dve", YG[:, gi, cols], GUt[sl][:], bank(bm), ALU.mult, [("GUt", sl), kb(bm)], [("YG", gi, tti)])
                cnt = 0
                for f in range(8):
                    slot = f % 3
                    srcs = (w_in[:, 4 * D + f * 128:4 * D + (f + 1) * 128], w_in[:, 5 * D + f * 128:5 * D + (f + 1) * 128],
                            w_br[:, f * 128:(f + 1) * 128], w_bg[:, f * 128:(f + 1) * 128])
                    wload(castg, [(WM4[slot][:, i4, :, :], src.rearrange("(kc p) n -> p kc n", p=128), ("WM4", slot, i4))
                                   for i4, src in enumerate(srcs)],
                          WM4[slot][:].rearrange("p a k n -> p (a k n)"), WM4S[f], [("WM4", slot, i4) for i4 in range(4)], ("WM4S", f))
                    for tti in range(2):
                        cols = slice(tti * 512, (tti + 1) * 512)
                        par = cnt % 2
                        cnt += 1
                        rhs_src = (HMg, HMg, YRg, YG)
                        rk = ([("HMg", tti)], [("HMg", tti)], [("YRg", tti)], [("YG", gi, tti) for gi in range(8)])
                        for i4 in range(4):
                            b = 2 * i4 + par
                            mm(bank(b), [(WM4[slot][:, i4, kc, :], rhs_src[i4][:, kc, cols]) for kc in range(8)],
                               [("WM4", slot, i4)] + rk[i4], [kb(b)])
                        act(TA[par][:], bank(0 + par), AF.Tanh, [kb(0 + par)], [("TA", par)], scale=0.5)
                        act(TB[par][:], bank(2 + par), AF.Tanh, [kb(2 + par)], [("TB", par)], scale=0.5)
                        stt(M1[par][:], TA[par][:], 1.0, bank(4 + par), ALU.add, ALU.mult, [("TA", par), kb(4 + par)], [("M1", 0)])
                        stt(M2[par][:], TB[par][:], 1.0, bank(6 + par), ALU.add, ALU.mult, [("TB", par), kb(6 + par)], [("M2", 0)])
                        tt("pool", MG[:, f, cols], M1[par][:], M2[par][:], ALU.add, [("M1", 0), ("M2", 0)], [("MG", f, tti)])
                for tti in range(2):
                    T = 2 * g + tti
                    t0 = T * 512
                    cols = slice(tti * 512, (tti + 1) * 512)
                    dma("sp", xg[:, :, cols], XA[:, :, t0:t0 + 512].rearrange("m p t -> p m t"), [("XA", T)],
                        [("xg", m, tti) for m in range(8)])
                cnt = 0
                for m in range(8):
                    slot = m % 3
                    wload(castg, [(WO[slot][:], w_out[:, m * 128:(m + 1) * 128].rearrange("(kc p) n -> p kc n", p=128), ("WU", slot))],
                          WO[slot][:].rearrange("p k n -> p (k n)"), WOS[m], [("WU", slot)], ("WOS", m))
                    if m == 1 and g + 1 < 5:
                        load_hm_yr(g + 1)
                    for tti in range(2):
                        cols = slice(tti * 512, (tti + 1) * 512)
                        b = cnt % 2
                        cnt += 1
                        mm(bank(b), [(WO[slot][:, kc, :], MG[:, kc, cols]) for kc in range(8)],
                           [("WU", slot)] + [("MG", f, tti) for f in range(8)], [kb(b)])
                        stt(xg[:, m, cols], bank(b), gh_ap(cond, 1, m), xg[:, m, cols], ALU.mult, ALU.add,
                            [kb(b), "GH", ("xg", m, tti)], [("xg", m, tti)])
                for tti in range(2):
                    T = 2 * g + tti
                    dma("pool", XA[:, :, T * 512:(T + 1) * 512].rearrange("m p t -> p m t"), xg[:, :, tti * 512:(tti + 1) * 512],
                        [("xg", m, tti) for m in range(8)], [("XA", T)])

        def ff2_group(g, xg):
            cond = 0 if g < 4 else 1
            with contextlib.ExitStack() as ph:
                B = ff_buffers(ph, first=False)
                norm_stats(xg, B)
                for tti in range(2):
                    norm_apply(xg, tti, cond, 2, B, xmod_out=B["xmod"])
                ffn_group(2, cond, 2, xg, B, first=(g == 0))
                YF, YT = B["YF"], B["YT"]
                cnt = 0
                norm_stats(xg, B)
                for tti in range(2):
                    T = 2 * g + tti
                    t0 = T * 512
                    norm_apply(xg, tti, cond, None, B, yf_out=YF)
                    for s in range(4):
                        sl = cnt % 2
                        cnt += 1
                        for half in range(2):
                            b = 2 * (cnt % 2) + half

                            def emit_tr(e, s=s, half=half, b=b):
                                ins = None
                                for mmi in range(4):
                                    m = 4 * half + mmi
                                    ins = e.transpose(out=PS[:, b, mmi * 128:(mmi + 1) * 128], in_=YF[:, m, s * 128:(s + 1) * 128],
                                                      identity=IDN[:])
                                return ins
                            P.add("pe", emit_tr, [("YF", m) for m in range(8)] + ["IDN"], [kb(b)])
                            act(YT[sl][:, half * 512:(half + 1) * 512], bank(b), AF.Copy, [kb(b)], [("YT", sl)])
                        dma("pool", yrows(t0 + s * 128, 128), YT[sl][:], [("YT", sl)], [("yout", T, s)])
                P.barrier(ff_keys() + allps + s2_keys())

        QUART = sb("QUART", [128, 1], F32)
        memset("dve", QUART[:], 0.25, ["QUART"])
        EPSC = sb("EPSC", [128, 1], F32)
        memset("dve", EPSC[:], EPS, ["EPSC"])

        def ffn_tile(which, cond, k_gate, xg, xm, H, B, first, hooks):
            wg, wu, wd = ffw[which]
            GU, WD, SG = B["GU"], B["WD"], B["SG"]
            cnt = B["cnt"]
            for jp in range(NJ // 2):
                for hk in hooks.get(jp, ()):
                    hk()
                slot = cnt[0] % len(GU)
                cnt[0] += 1
                wload(first, [(GU[slot][:, 0, :, :], wg[:, jp * 256:(jp + 1) * 256].rearrange("(kc p) n -> p kc n", p=128), ("GU", slot, 0)),
                              (GU[slot][:, 1, :, :], wu[:, jp * 256:(jp + 1) * 256].rearrange("(kc p) n -> p kc n", p=128), ("GU", slot, 1))],
                      GU[slot][:].rearrange("p a k n -> p (a k n)"), GUS[which][jp], [("GU", slot, 0), ("GU", slot, 1)],
                      ("GUS", which, jp))
                for jj in range(2):
                    j = 2 * jp + jj
                    bg = cnt[1] % 2
                    bu = 2 + cnt[1] % 2
                    sl = cnt[1] % 2
                    cnt[1] += 1
                    xk = [("xm", id(xm), m) for m in range(8)]
                    mm(bank(bg), [(GU[slot][:, 0, kc, jj * 128:(jj + 1) * 128], xm[:, kc, :]) for kc in range(8)],
                       [("GU", slot, 0)] + xk, [kb(bg)])
                    mm(bank(bu), [(GU[slot][:, 1, kc, jj * 128:(jj + 1) * 128], xm[:, kc, :]) for kc in range(8)],
                       [("GU", slot, 1)] + xk, [kb(bu)])
                    act(SG[sl][:], bank(bg), AF.Silu, [kb(bg)], [("SG", sl)])
                    tt("dve", H[:, j, :], SG[sl][:], bank(bu), ALU.mult, [("SG", sl), kb(bu)], [("H", j)])
            for m in range(8):
                slot = cnt[2] % len(WD)
                cnt[2] += 1
                wload(first, [(WD[slot][:], wd[:, m * 128:(m + 1) * 128].rearrange("(kc p) n -> p kc n", p=128), ("WD", slot))],
                      WD[slot][:].rearrange("p k n -> p (k n)"), WDS[which][m], [("WD", slot)], ("WDS", which, m))
                b = 4 + cnt[2] % 2
                mm(bank(b), [(WD[slot][:, j, :], H[:, j, :]) for j in range(NJ)],
                   [("WD", slot)] + [("H", j) for j in range(NJ)], [kb(b)])
                stt(xg[:, m, :], bank(b), gh_ap(cond, k_gate, m), xg[:, m, :], ALU.mult, ALU.add,
                    [kb(b), "GH", ("xg", id(xg), m)], [("xg", id(xg), m)])

        def run_ff_tiles(which, ntiles=10):
            with contextlib.ExitStack() as ph:
                xgT = [sb("xgT", [128, 8, 512], F32, ph) for _ in range(3)]
                xmT = [sb("xmT", [128, 8, 512], BF16, ph) for _ in range(2)]
                HMo = sb("HMo", [128, 8, 512], BF16, ph) if which == 1 else None
                H = sb("Ht", [128, NJ, 512], BF16, ph)
                B = {"GU": [sb("GU", [128, 2, 8, 256], BF16, ph) for _ in range(3 if which == 1 else 4)],
                     "WD": [sb("WD", [128, NJ, 128], BF16, ph) for _ in range(2 if which == 1 else 3)],
                     "SG": [sb("SG", [128, 512], F32, ph) for _ in range(2)], "cnt": [0, 0, 0]}
                SQ = [sb("SQ", [128, 8, 512], BF16, ph) for _ in range(2)]
                LN = sb("LN", [128, 2, 512], F32, ph)
                RS = sb("RS", [128, 2, 512], F32, ph)
                TMP = [sb("TMP", [128, 512], F32, ph) for _ in range(2)]
                if which == 1:
                    XT = [sb("XT", [128, D], F32, ph) for _ in range(4)]
                else:
                    YF = sb("YF", [128, 8, 512], F32, ph)
                    YT = [sb("YT", [128, D], F32, ph) for _ in range(2)]
                tcnt = [0]
                k_in = 0 if which == 1 else 2
                pre_jobs = []
                if which == 1 and PRECAST:
                    STG = [sb("STG", [128, 4096], BF16, ph) for _ in range(2)]
                    pre_jobs = mixer_precast_jobs(STG) + ff_precast_jobs(2, STG)

                def condof(t):
                    return 0 if t < 8 else 1

                def P1(t, part=None):
                    xg = xgT[t % 3]
                    t0 = t * 512
                    if which == 2:
                        if part in (None, 0):
                            dma("sp", xg[:, :, :], XA[:, :, t0:t0 + 512].rearrange("m p t -> p m t"), [("XA", t)],
                                [("xg", id(xg), m) for m in range(8)])
                        return
                    if part in (None, 0):
                        for s in range(4):
                            dma("sp", XT[s][:], xrows(t0 + s * 128, 128), (), [("XT", s)])
                    if part == 0:
                        return
                    for m in range(8):
                        b = 4 + tcnt[0] % 2
                        tcnt[0] += 1

                        def emit_tr(e, m=m, b=b):
                            ins = None
                            for s in range(4):
                                ins = e.transpose(out=PS[:, b, s * 128:(s + 1) * 128], in_=XT[s][:, m * 128:(m + 1) * 128],
                                                  identity=IDN[:])
                            return ins
                        P.add("pe", emit_tr, [("XT", s) for s in range(4)] + ["IDN"], [kb(b)])
                        if condof(t) == 0:
                            pv = bank(b).rearrange("p (a b) -> p a b", b=64)
                            ov = xg[:, m, :].rearrange("p (a b) -> p a b", b=64)
                            if m < 4:
                                tv = TAB[:, m, 8 * t:8 * t + 8].unsqueeze(2).broadcast_to([128, 8, 64])
                            else:
                                tv = TAB[:, m - 4, :].unsqueeze(1).broadcast_to([128, 8, 64])
                            tt("dve", ov, pv, tv, ALU.add, [kb(b), "TAB"], [("xg", id(xg), m)])
                        else:
                            cp("dve", xg[:, m, :], bank(b), [kb(b)], [("xg", id(xg), m)])

                def N_sq(t, w, h=None):
                    xg = xgT[t % 3]
                    for hh in ((0, 1) if h is None else (h,)):
                        ms = range(4 * hh, 4 * hh + 4)
                        act(SQ[w][:, 4 * hh:4 * hh + 4, :], xg[:, 4 * hh:4 * hh + 4, :], AF.Square,
                            [("xg", id(xg), m) for m in ms], [("SQ", w, hh)])

                def N_mm(w):
                    mm(bank(6 + w), [(ONES[:], SQ[w][:, m, :]) for m in range(8)], [("SQ", w, 0), ("SQ", w, 1), "ONES"], [kb(6 + w)])

                def N_ln(w):
                    act(LN[:, w, :], bank(6 + w), AF.Ln, [kb(6 + w), "EPSC"], [("LN", w)], bias=EPSC[:, 0:1], scale=1.0 / D)
                    act(RS[:, w, :], LN[:, w, :], AF.Exp, [("LN", w)], [("RS", w)], scale=-0.5)

                def N_ln2():
                    act(LN[:], PS[:, 6:8, :], AF.Ln, [kb(6), kb(7), "EPSC"], [("LN", 0), ("LN", 1)], bias=EPSC[:, 0:1], scale=1.0 / D)
                    act(RS[:], LN[:], AF.Exp, [("LN", 0), ("LN", 1)], [("RS", 0), ("RS", 1)], scale=-0.5)

                def APPLY(t, w, k, store, h=None):
                    xg = xgT[t % 3]
                    xm = HMo if store else xmT[t % 2]
                    cond = condof(t)
                    for m in (range(8) if h is None else range(4 * h, 4 * h + 4)):
                        sl = m % 2
                        stt(TMP[sl][:], xg[:, m, :], ns_ap(cond, k, m), RS[:, w, :], ALU.mult, ALU.mult,
                            [("xg", id(xg), m), ("RS", w), "NS"], [("TMP", sl)])
                        act(xm[:, m, :], TMP[sl][:], AF.Identity, [("TMP", sl), "MOD"], [("xm", id(xm), m)],
                            bias=sh_ap(cond, k, m))
                    if store and h in (None, 1):
                        t0 = t * 512
                        dma("act", HM[:, :, t0:t0 + 512].rearrange("m p t -> p m t"), xm[:, :, :],
                            [("xm", id(xm), m) for m in range(8)], [("HM", t)])
                        dma("act", XA[:, :, t0:t0 + 512].rearrange("m p t -> p m t"), xg[:, :, :],
                            [("xg", id(xg), m) for m in range(8)], [("XA", t)])

                def FIN_dve(t):
                    xg = xgT[t % 3]
                    for m in range(8):
                        stt(YF[:, m, :], xg[:, m, :], VT[:, R_NF + m:R_NF + m + 1], RS[:, 0, :], ALU.mult, ALU.mult,
                            [("xg", id(xg), m), ("RS", 0), "VT"], [("YF", m)])

                def FIN_tr(t, srange):
                    t0 = t * 512
                    for s_ in srange:
                        sl = tcnt[0] % 2
                        tcnt[0] += 1
                        for half in range(2):
                            b = 4 + half

                            def emit_tr(e, s_=s_, half=half, b=b):
                                ins = None
                                for mmi in range(4):
                                    m = 4 * half + mmi
                                    ins = e.transpose(out=PS[:, b, mmi * 128:(mmi + 1) * 128], in_=YF[:, m, s_ * 128:(s_ + 1) * 128],
                                                      identity=IDN[:])
                                return ins
                            P.add("pe", emit_tr, [("YF", m) for m in range(8)] + ["IDN"], [kb(b)])
                            act(YT[sl][:, half * 512:(half + 1) * 512], bank(b), AF.Copy, [kb(b)], [("YT", sl)])
                        dma("pool", yrows(t0 + s_ * 128, 128), YT[sl][:], [("YT", sl)], [("yout", t, s_)])

                P1(0)
                N_sq(0, 1)
                N_mm(1)
                N_ln(1)
                APPLY(0, 1, k_in, False)
                for t in range(ntiles):
                    hooks = {}
                    both = (t >= 1 and t + 1 < ntiles)
                    if t >= 1:
                        hooks.setdefault(0, []).append(lambda t=t: N_sq(t - 1, 0, 0))
                        hooks.setdefault(1, []).append(lambda t=t: N_sq(t - 1, 0, 1))
                        hooks.setdefault(2, []).append(lambda: N_mm(0))
                    if t + 1 < ntiles:
                        hooks.setdefault(0, []).append(lambda t=t: P1(t + 1, 0))
                        hooks.setdefault(2, []).append(lambda t=t: P1(t + 1, 1))
                        hooks.setdefault(3, []).append(lambda t=t: N_sq(t + 1, 1, 0))
                        hooks.setdefault(4, []).append(lambda t=t: N_sq(t + 1, 1, 1))
                        hooks.setdefault(5, []).append(lambda: N_mm(1))
                    if both:
                        hooks.setdefault(6, []).append(N_ln2)
                    elif t >= 1:
                        hooks.setdefault(6, []).append(lambda: N_ln(0))
                    else:
                        hooks.setdefault(6, []).append(lambda: N_ln(1))
                    if t >= 1:
                        if which == 1:
                            hooks.setdefault(7, []).append(lambda t=t: APPLY(t - 1, 0, 1, True, 0))
                            hooks.setdefault(8, []).append(lambda t=t: APPLY(t - 1, 0, 1, True, 1))
                        else:
                            hooks.setdefault(7, []).append(lambda t=t: FIN_dve(t - 1))
                            hooks.setdefault(8, []).append(lambda t=t: FIN_tr(t - 1, (0, 1)))
                            hooks.setdefault(10, []).append(lambda t=t: FIN_tr(t - 1, (2, 3)))
                    if t + 1 < ntiles:
                        hooks.setdefault(9, []).append(lambda t=t: APPLY(t + 1, 1, k_in, False, 0))
                        hooks.setdefault(10, []).append(lambda t=t: APPLY(t + 1, 1, k_in, False, 1))
                    if t >= 1:
                        for jp_ in (1, 3, 5, 8, 10):
                            if pre_jobs:
                                hooks.setdefault(jp_, []).append(pre_jobs.pop(0))
                    ffn_tile(which, condof(t), k_in, xgT[t % 3], xmT[t % 2], H, B, t == 0 and not (which == 2 and PRECAST), hooks)
                t = ntiles - 1
                N_sq(t, 0)
                N_mm(0)
                N_ln(0)
                if which == 1:
                    APPLY(t, 0, 1, True)
                else:
                    FIN_dve(t)
                    FIN_tr(t, (0, 1, 2, 3))
                P.barrier()

        def run_ff1(groups):
            with contextlib.ExitStack() as ph:
                xg = sb("xg", [128, 8, 1024], F32, ph)
                B = ff_buffers(ph, first=True)
                for g in groups:
                    ff1_group(g, xg, B)
                P.barrier(ff_keys() + allps)

        def run_s2(groups):
            with contextlib.ExitStack() as ph:
                xg = sb("xg", [128, 8, 1024], F32, ph)
                for g in groups:
                    s2_group(g, xg, ph)
                P.barrier()

        stages = stop_after
        run_ff_tiles(1)
        if stages != "ff1":
            s1_group(0)
            s1_group(1)
            if stages != "s1":
                run_s2([0, 1, 2, 3, 4])
                if stages != "s2":
                    run_ff_tiles(2)

        P.build(nc, top, {"sp": 16, "act": 4, "pool": 12, "pe": 1, "dve": 1})
    nc._prog_stats = dict(n_ops=len(P.ops))
    return nc


def make_in_maps(inputs):
    f = lambda a: np.ascontiguousarray(np.asarray(a, dtype=np.float32))
    x_prompt, x_sample = f(inputs["x_prompt"]), f(inputs["x_sample"])
    c, c_ctx = f(inputs["c"]), f(inputs["c_ctx"])
    sf, sbw = f(inputs["state_rnn_fwd"]), f(inputs["state_rnn_bwd"])
    shared = {
        "ident": np.eye(128, dtype=np.float32),
        "gvb": np.ascontiguousarray(np.broadcast_to(f(inputs["gmlp_norm"])[0][None, :], (128, D))),
        "bs": f(inputs["b_s"])[0].reshape(1, D),
        "w_mod": f(inputs["w_mod"])[0],
        "ff1_gate": f(inputs["ff1_gate"])[0], "ff1_up": f(inputs["ff1_up"])[0], "ff1_down": f(inputs["ff1_down"])[0],
        "ff2_gate": f(inputs["ff2_gate"])[0], "ff2_up": f(inputs["ff2_up"])[0], "ff2_down": f(inputs["ff2_down"])[0],
        "w_in": f(inputs["w_in"])[0], "w_r": f(inputs["w_r"])[0], "w_i": f(inputs["w_i"])[0], "w_s": f(inputs["w_s"])[0],
        "w_br": f(inputs["w_br"])[0], "w_bg": f(inputs["w_bg"])[0], "w_out": f(inputs["w_out"])[0],
    }
    base = np.zeros((256, 128), np.float32)
    base[R_C1:R_C1 + 8] = c_ctx.reshape(8, 128)
    base[R_BMOD:R_BMOD + 72] = f(inputs["b_mod"])[0].reshape(72, 128)
    base[R_N1:R_N1 + 8] = f(inputs["norm1"])[0].reshape(8, 128)
    base[R_N2:R_N2 + 8] = f(inputs["norm2"])[0].reshape(8, 128)
    base[R_N3:R_N3 + 8] = f(inputs["norm3"])[0].reshape(8, 128)
    base[R_NF:R_NF + 8] = f(inputs["norm_f"]).reshape(8, 128)
    base[R_CW:R_CW + 32] = f(inputs["conv_w"])[0].reshape(32, 128)
    base[R_CB:R_CB + 8] = f(inputs["conv_b"])[0].reshape(8, 128)
    base[R_BR:R_BR + 16] = f(inputs["b_r"])[0].reshape(16, 128)
    base[R_BI:R_BI + 16] = f(inputs["b_i"])[0].reshape(16, 128)
    base[R_LAM:R_LAM + 16] = f(inputs["lam"])[0].reshape(16, 128)
    maps = []
    for b in range(8):
        v = base.copy()
        v[R_C0:R_C0 + 8] = c[b].reshape(8, 128)
        v[R_SF:R_SF + 8] = sf[b, 0].reshape(8, 128)
        v[R_SB:R_SB + 8] = sbw[b, 0].reshape(8, 128)
        m = dict(shared)
        m["xs"] = x_sample[b]
        m["xp"] = x_prompt[4 * b:4 * b + 4].reshape(1024, D)
        m["vecs"] = v
        maps.append(m)
    return maps


def kernel(**inputs):
    maps = make_in_maps(inputs)
    nc = build_nc()
    res = run_bass_kernel_spmd(nc, maps, core_ids=list(range(8)))
    y_prompt = np.zeros((32, 256, D), np.float32)
    y_sample = np.zeros((8, 4096, D), np.float32)
    nsf = np.zeros((32, 1, D), np.float32)
    nsb = np.zeros((32, 1, D), np.float32)
    for b in range(8):
        r = res.results[b]
        y_sample[b] = np.asarray(r["ys"], dtype=np.float32)
        y_prompt[4 * b:4 * b + 4] = np.asarray(r["yp"], dtype=np.float32).reshape(4, 256, D)
        nsf[4 * b:4 * b + 4, 0] = np.asarray(r["stf"], dtype=np.float32)
        nsb[4 * b:4 * b + 4, 0] = np.asarray(r["stb"], dtype=np.float32)
    return (y_prompt, y_sample, nsf, nsb)
```

```python
import contextlib
import math
import numpy as np
import concourse.bass as bass
import concourse.mybir as mybir
from concourse.bass_utils import run_bass_kernel_spmd

F32 = mybir.dt.float32
BF16 = mybir.dt.bfloat16
I32 = mybir.dt.int32
AF = mybir.ActivationFunctionType
ALU = mybir.AluOpType

D = 1024
DFF = 2816
NJ = DFF // 128
NTOK = 5120
S1_ORDER = 1
PRECAST = True
HZ_ALL = True
EPS = 1e-6

R_C0, R_C1, R_BMOD, R_N1, R_N2, R_N3, R_NF = 0, 8, 16, 88, 96, 104, 112
R_CW, R_CB, R_BR, R_BI, R_LAM, R_SF, R_SB = 128, 160, 168, 184, 200, 216, 224


class Prog:
    ENG = ("pe", "act", "dve", "pool", "sp")

    def __init__(self):
        self.ops = []
        self.last_w = {}
        self.readers = {}
        self.fence = None
        self.last_eng = {}
        self.pend_dma = []
        self.hz = HZ_ALL

    @contextlib.contextmanager
    def hazard(self):
        old = self.hz
        self.hz = True
        try:
            yield
        finally:
            self.hz = old

    def add(self, eng, emit, reads=(), writes=(), dma=False):
        i = len(self.ops)
        deps = set()
        lw = self.last_w
        rd = self.readers
        for k in reads:
            j = lw.get(k)
            if j is not None:
                deps.add(j)
        for k in writes:
            j = lw.get(k)
            if j is not None:
                deps.add(j)
            r = rd.get(k)
            if r:
                deps.update(r[0].values())
                deps.update(r[1])
        deps.discard(i)
        if self.fence is not None:
            deps.add(self.fence)
        if dma:
            self.pend_dma.append(i)
        else:
            self.last_eng[eng] = i
        for k in reads:
            r = rd.get(k)
            if r is None:
                r = rd[k] = ({}, [])
            if dma:
                r[1].append(i)
            else:
                r[0][eng] = i
        for k in writes:
            lw[k] = i
            rd[k] = ({}, [])
        self.ops.append(dict(eng=eng, emit=emit, dma=dma, deps=deps, hz=self.hz))

    def barrier(self, keys=None):
        i = len(self.ops)
        deps = set(self.last_eng.values()) | set(self.pend_dma)
        if self.fence is not None:
            deps.add(self.fence)
        self.ops.append(dict(eng="sp", emit=lambda e: e.nop(), dma=False, deps=deps, hz=False))
        self.fence = i
        self.last_eng = {"sp": i}
        self.pend_dma = []
        self.last_w = {}
        self.readers = {}

    def build(self, nc, stack, dma_ring):
        ops = self.ops
        n_dma = {e: 0 for e in self.ENG}
        dma_ops = {e: [] for e in self.ENG}
        for i, op in enumerate(ops):
            if op["dma"]:
                e = op["eng"]
                n = n_dma[e]
                K = dma_ring[e]
                op["dma_n"] = n
                if n >= K:
                    op["deps"].add(dma_ops[e][n - K])
                dma_ops[e].append(i)
                n_dma[e] += 1
        need = [False] * len(ops)
        for i, op in enumerate(ops):
            nd = set()
            for d in op["deps"]:
                p = ops[d]
                if (not p["dma"]) and (not op["dma"]) and p["eng"] == op["eng"] and not (op["hz"] and op["eng"] != "pe"):
                    continue
                nd.add(d)
                need[d] = True
            op["deps"] = nd
        esem = {e: stack.enter_context(nc.semaphore("s_" + e)) for e in self.ENG}
        dsem = {e: [stack.enter_context(nc.semaphore("d_%s%d" % (e, j))) for j in range(dma_ring[e])]
                for e in self.ENG if n_dma[e] > 0}
        cnt = {e: 0 for e in self.ENG}
        for i, op in enumerate(ops):
            if op["dma"]:
                e = op["eng"]
                n = op["dma_n"]
                K = dma_ring[e]
                op["sig"] = (dsem[e][n % K], 16 * (n // K + 1), 16)
            elif need[i]:
                e = op["eng"]
                cnt[e] += 1
                op["sig"] = (esem[e], cnt[e], 1)
            else:
                op["sig"] = None
        block = stack.enter_context(nc.Block())

        def run(engname, eng):
            waited = {}
            for i, op in enumerate(ops):
                if op["eng"] != engname:
                    continue
                for d in sorted(op["deps"]):
                    sem, val, _ = ops[d]["sig"]
                    key = id(sem)
                    if waited.get(key, 0) >= val:
                        continue
                    waited[key] = val
                    eng.wait_ge(sem, val)
                ins = op["emit"](eng)
                if op["sig"] is not None:
                    sem, val, inc = op["sig"]
                    ins.then_inc(sem, inc)
            if engname in dsem:
                n = n_dma[engname]
                K = dma_ring[engname]
                for j in range(min(n, K)):
                    last_n = ((n - 1 - j) // K) * K + j
                    val = 16 * (last_n // K + 1)
                    if waited.get(id(dsem[engname][j]), 0) < val:
                        eng.wait_ge(dsem[engname][j], val)

        @block.tensor
        def _(e):
            run("pe", e)

        @block.scalar
        def _(e):
            run("act", e)

        @block.vector
        def _(e):
            run("dve", e)

        @block.gpsimd
        def _(e):
            run("pool", e)

        @block.sync
        def _(e):
            run("sp", e)


def build_nc(debug=False, stop_after=None):
    nc = bass.Bass("TRN2", target_bir_lowering=False)
    P = Prog()

    def din(name, shape):
        return nc.dram_tensor(name, list(shape), F32, kind="ExternalInput").ap()

    def dout(name, shape, dt=F32):
        return nc.dram_tensor(name, list(shape), dt, kind="ExternalOutput").ap()

    xs = din("xs", [4096, D])
    xp = din("xp", [1024, D])
    vecs = din("vecs", [256, 128])
    ident_d = din("ident", [128, 128])
    gvb_d = din("gvb", [128, D])
    bs_d = din("bs", [1, D])
    w_mod = din("w_mod", [D, 9 * D])
    ffw = {1: (din("ff1_gate", [D, DFF]), din("ff1_up", [D, DFF]), din("ff1_down", [DFF, D])),
           2: (din("ff2_gate", [D, DFF]), din("ff2_up", [D, DFF]), din("ff2_down", [DFF, D]))}
    w_in = din("w_in", [D, 6 * D])
    w_r = din("w_r", [2, 16, 64, 64])
    w_i = din("w_i", [2, 16, 64, 64])
    w_s = din("w_s", [8, 128, 128])
    w_br = din("w_br", [D, D])
    w_bg = din("w_bg", [D, D])
    w_out = din("w_out", [D, D])
    ys = dout("ys", [4096, D])
    yp = dout("yp", [1024, D])
    stf_d = dout("stf", [4, D])
    stb_d = dout("stb", [4, D])
    skind = "ExternalOutput" if debug else "Internal"
    XA = nc.dram_tensor("XA", [8, 128, NTOK], F32, kind=skind).ap()
    HM = nc.dram_tensor("HM", [8, 128, NTOK], BF16, kind=skind).ap()
    YR = nc.dram_tensor("YR", [8, 128, NTOK], BF16, kind=skind).ap()
    if debug:
        DBG = nc.dram_tensor("DBG", [128, 2048], F32, kind="ExternalOutput").ap()
    GUS = {w: nc.dram_tensor("GUS%d" % w, [NJ // 2, 128, 2 * 8 * 256], BF16).ap() for w in (1, 2)}
    WDS = {w: nc.dram_tensor("WDS%d" % w, [8, 128, NJ * 128], BF16).ap() for w in (1, 2)}
    WXS = nc.dram_tensor("WXS", [8, 128, 2 * 8 * 128], BF16).ap()
    WVS = nc.dram_tensor("WVS", [128, 8 * D], BF16).ap()
    WUS = nc.dram_tensor("WUS", [8, 128, 8 * 128], BF16).ap()
    WM4S = nc.dram_tensor("WM4S", [8, 128, 4 * 8 * 128], BF16).ap()
    WOS = nc.dram_tensor("WOS", [8, 128, 8 * 128], BF16).ap()
    WSTS = nc.dram_tensor("WSTS", [128, 8 * 128], BF16).ap()
    BSHS = nc.dram_tensor("BSHS", [2, 8 * 128], BF16).ap()

    def xrows(t0, n):
        return xs[t0:t0 + n, :] if t0 < 4096 else xp[t0 - 4096:t0 - 4096 + n, :]

    def yrows(t0, n):
        return ys[t0:t0 + n, :] if t0 < 4096 else yp[t0 - 4096:t0 - 4096 + n, :]

    def dma(eng, out, in_, r, w):
        P.add(eng, lambda e: e.dma_start(out=out, in_=in_), r, w, dma=True)

    def wload(first, parts, flat, scr, keys, skey):
        if first:
            for (dst, src, k) in parts:
                dma("pool", dst, src, (), [k])
            dma("pool", scr, flat, keys, [skey])
        else:
            dma("sp", flat, scr, [skey], keys)

    def precast_job(stg, nelem, parts, scr, skey):
        for (dst, src) in parts:
            dma("pool", dst, src, (), [("STG", id(stg))])
        src_v = stg[:, 0:nelem]
        if len(scr.shape) == 3:
            src_v = src_v.rearrange("p (a b) -> p a b", a=scr.shape[1])
        dma("pool", scr, src_v, [("STG", id(stg))], [skey])

    def blk(src):
        return src.rearrange("(kc p) n -> p kc n", p=128)

    def mixer_precast_jobs(STG):
        jobs = []
        cnt = [0]

        def nxt():
            st = STG[cnt[0] % len(STG)]
            cnt[0] += 1
            return st
        for c in range(8):
            def j(c=c):
                st = nxt()
                v = st[:, 0:2048].rearrange("p (a k n) -> p a k n", a=2, k=8)
                precast_job(st, 2048, [(v[:, 0], blk(w_in[:, c * 128:(c + 1) * 128])),
                                       (v[:, 1], blk(w_in[:, D + c * 128:D + (c + 1) * 128]))], WXS[c], ("WXS", c))
            jobs.append(j)
        for h in range(2):
            def j(h=h):
                st = nxt()
                v = st[:, :].rearrange("p (k n) -> p k n", k=8)
                precast_job(st, 4096, [(v, blk(w_in[:, 3 * D + h * 512:3 * D + (h + 1) * 512]))],
                            WVS.rearrange("p (k n) -> p k n", k=8)[:, :, h * 512:(h + 1) * 512], "WVS")
            jobs.append(j)
        for g0 in range(0, 8, 4):
            def j(g0=g0):
                st = nxt()
                v = st[:, :].rearrange("p (g k n) -> p g k n", g=4, k=8)
                precast_job(st, 4096, [(v[:, i], blk(w_in[:, 2 * D + (g0 + i) * 128:2 * D + (g0 + i + 1) * 128])) for i in range(4)],
                            WUS[g0:g0 + 4].rearrange("g p e -> p g e"), ("WUS", g0))
            jobs.append(j)
        for f in range(8):
            def j(f=f):
                st = nxt()
                v = st[:, :].rearrange("p (a k n) -> p a k n", a=4, k=8)
                srcs = (w_in[:, 4 * D + f * 128:4 * D + (f + 1) * 128], w_in[:, 5 * D + f * 128:5 * D + (f + 1) * 128],
                        w_br[:, f * 128:(f + 1) * 128], w_bg[:, f * 128:(f + 1) * 128])
                precast_job(st, 4096, [(v[:, i], blk(srcs[i])) for i in range(4)], WM4S[f], ("WM4S", f))
            jobs.append(j)
        for m0 in range(0, 8, 4):
            def j(m0=m0):
                st = nxt()
                v = st[:, :].rearrange("p (g k n) -> p g k n", g=4, k=8)
                precast_job(st, 4096, [(v[:, i], blk(w_out[:, (m0 + i) * 128:(m0 + i + 1) * 128])) for i in range(4)],
                            WOS[m0:m0 + 4].rearrange("g p e -> p g e"), ("WOS", m0))
            jobs.append(j)
        return jobs

    def ff_precast_jobs(which, STG):
        wg, wu, wd = ffw[which]
        jobs = []
        cnt = [0]

        def nxt():
            st = STG[cnt[0] % len(STG)]
            cnt[0] += 1
            return st
        for jp in range(NJ // 2):
            def j(jp=jp):
                st = nxt()
                v = st[:, :].rearrange("p (a k n) -> p a k n", a=2, k=8)
                precast_job(st, 4096, [(v[:, 0], blk(wg[:, jp * 256:(jp + 1) * 256])), (v[:, 1], blk(wu[:, jp * 256:(jp + 1) * 256]))],
                            GUS[which][jp], ("GUS", which, jp))
            jobs.append(j)
        for m in range(8):
            def j(m=m):
                st = nxt()
                v = st[:, 0:NJ * 128].rearrange("p (k n) -> p k n", k=NJ)
                precast_job(st, NJ * 128, [(v, blk(wd[:, m * 128:(m + 1) * 128]))], WDS[which][m], ("WDS", which, m))
            jobs.append(j)
        return jobs

    def mm(ps_ap, pairs, r, w, first_start=True):
        def emit(e):
            n = len(pairs)
            ins = None
            for i, (l, rh) in enumerate(pairs):
                ins = e.matmul(ps_ap, lhsT=l, rhs=rh, start=(first_start and i == 0), stop=(i == n - 1))
            return ins
        P.add("pe", emit, r, w)

    def act(out, in_, func, r, w, bias=None, scale=None, accum=None):
        def emit(e):
            kw = {}
            if bias is not None:
                kw["bias"] = bias
            if scale is not None:
                kw["scale"] = scale
            if accum is not None:
                kw["accum_out"] = accum
            return e.activation(out=out, in_=in_, func=func, **kw)
        P.add("act", emit, r, w)

    def tt(eng, out, in0, in1, op, r, w):
        P.add(eng, lambda e: e.tensor_tensor(out=out, in0=in0, in1=in1, op=op), r, w)

    def ts(eng, out, in0, s1, s2, op0, op1, r, w):
        if op1 is None:
            P.add(eng, lambda e: e.tensor_scalar(out=out, in0=in0, scalar1=s1, scalar2=None, op0=op0), r, w)
        else:
            P.add(eng, lambda e: e.tensor_scalar(out=out, in0=in0, scalar1=s1, scalar2=s2, op0=op0, op1=op1), r, w)

    def stt(out, in0, scalar, in1, op0, op1, r, w):
        P.add("dve", lambda e: e.scalar_tensor_tensor(out=out, in0=in0, scalar=scalar, in1=in1, op0=op0, op1=op1), r, w)

    def cp(eng, out, in_, r, w):
        P.add(eng, lambda e: e.tensor_copy(out=out, in_=in_), r, w)

    def memset(eng, ap, val, w):
        P.add(eng, lambda e: e.memset(ap, val), (), w)

    with contextlib.ExitStack() as top:
        uid = [0]

        def sb(name, shape, dt, st=None):
            uid[0] += 1
            return (st or top).enter_context(nc.sbuf_tensor("%s_%d" % (name, uid[0]), list(shape), dt))

        PS = top.enter_context(nc.psum_tensor("PS", [128, 8, 512], F32))

        def bank(b):
            return PS[:, b, :]

        def kb(b):
            return ("ps", b)

        IDN = sb("IDN", [128, 128], F32)
        VT = sb("VT", [128, 256], F32)
        MOD = sb("MOD", [128, 2, 72], F32)
        NS = sb("NS", [128, 2, 3, 8], F32)
        GH = sb("GH", [128, 2, 3, 8], F32)
        LL = sb("LL", [128, 16], F32)
        L4 = sb("L4", [128, 16], F32)
        L8 = sb("L8", [128, 16], F32)
        HBR = sb("HBR", [128, 16], F32)
        HBI = sb("HBI", [128, 16], F32)
        TAB = sb("TAB", [128, 4, 64], F32)
        ONES = sb("ONES", [128, 128], BF16)
        NEGH = sb("NEGH", [128, 1], F32)
        BD = sb("BD", [128, 2, 2, 8, 128], BF16)
        STF = sb("STF", [128, 32], F32)
        STB = sb("STB", [128, 32], F32)
        SSV = sb("SSV", [128, 4], F32)

        def sh_ap(cond, k, m):
            return MOD[:, cond, (3 * k) * 8 + m:(3 * k) * 8 + m + 1]

        def ns_ap(cond, k, m):
            return NS[:, cond, k, m:m + 1]

        def gh_ap(cond, k, m):
            return GH[:, cond, k, m:m + 1]

        with contextlib.ExitStack() as ph, P.hazard():
            V0 = sb("V0", [128, 128], F32, ph)
            V1 = sb("V1", [128, 128], F32, ph)
            SB2 = sb("SB2", [128, 8, 2], BF16, ph)
            WM = [sb("WM%d" % i, [128, 8, D], BF16, ph) for i in range(2)]
            WSL = sb("WSL", [128, 8, 128], F32, ph)
            BS0 = sb("BS0", [1, D], F32, ph)
            BSHI = sb("BSHI", [1, D], BF16, ph)
            WST = sb("WSTset", [128, 8, 128], BF16, ph)
            BSLO = sb("BSLO", [1, D], BF16, ph)
            SIG = sb("SIG", [128, 16], F32, ph)
            QI = sb("QI", [128, 2], I32, ph)
            QF = sb("QF", [128, 2], F32, ph)
            FR = sb("FR", [128, 2], F32, ph)
            RI = sb("RI", [128, 64], I32, ph)
            RV = sb("RV", [128, 64], F32, ph)
            ANG = sb("ANG", [128, 4, 64], F32, ph)
            TQ = sb("TQ", [128, 4, 64], F32, ph)
            KI = sb("KI", [128, 4, 64], I32, ph)
            KF = sb("KF", [128, 4, 64], F32, ph)
            CM = sb("CM", [128, 4, 64], F32, ph)

            dma("sp", IDN[:], ident_d[:, :], (), ["IDN"])
            dma("sp", V0[:], vecs[0:128, :], (), ["V0"])
            dma("sp", V1[:], vecs[128:256, :], (), ["V1"])
            dma("sp", BS0[:], bs_d[:, :], (), ["BS0"])
            dma("sp", WSL[:], w_s.rearrange("g p q -> p g q"), (), ["WSL"])
            memset("pool", ONES[:], 1.0, ["ONES"])
            memset("pool", NEGH[:], -0.5, ["NEGH"])
            memset("pool", BD[:], 0.0, ["BD"])
            memset("dve", STF[:], 0.0, ["STF"])
            memset("dve", STB[:], 0.0, ["STB"])
            P.add("pe", lambda e: e.transpose(out=PS[:, 7, 0:128], in_=V0[:], identity=IDN[:]), ["V0", "IDN"], [kb(7)])
            P.add("pe", lambda e: e.transpose(out=PS[:, 7, 128:256], in_=V1[:], identity=IDN[:]), ["V1", "IDN"], [kb(7)])
            cp("dve", VT[:], PS[:, 7, 0:256], [kb(7)], ["VT"])
            for cond in range(2):
                act(SB2[:, :, cond], VT[:, 8 * cond:8 * cond + 8], AF.Silu, ["VT"], ["SB2"])
            act(SIG[:], VT[:, R_LAM:R_LAM + 16], AF.Sigmoid, ["VT"], ["SIG"])
            act(LL[:], SIG[:], AF.Ln, ["SIG"], ["LL"])
            ts("dve", L4[:], LL[:], 4.0, None, ALU.mult, None, ["LL"], ["L4"])
            ts("dve", L8[:], LL[:], 8.0, None, ALU.mult, None, ["LL"], ["L8"])
            ts("dve", HBR[:], VT[:, R_BR:R_BR + 16], 0.5, None, ALU.mult, None, ["VT"], ["HBR"])
            ts("dve", HBI[:], VT[:, R_BI:R_BI + 16], 0.5, None, ALU.mult, None, ["VT"], ["HBI"])
            for g in range(8):
                P.add("pe", lambda e, g=g: e.transpose(out=PS[:, 5, (g % 4) * 128:(g % 4) * 128 + 128], in_=WSL[:, g, :],
                                                       identity=IDN[:]), ["WSL", "IDN"], [kb(5)])
                cp("dve", WST[:, g, :], PS[:, 5, (g % 4) * 128:(g % 4) * 128 + 128], [kb(5)], ["WST"])
            cp("dve", BSHI[:], BS0[:], ["BS0"], ["BSHI"])
            tt("dve", BSLO[:], BS0[:], BSHI[:], ALU.subtract, ["BS0", "BSHI"], ["BSLO"])
            dma("sp", BSHS[0:1, :], BSHI[0:1, :], ["BSHI"], ["BSHS"])
            dma("sp", BSHS[1:2, :], BSLO[0:1, :], ["BSLO"], ["BSHS"])
            dma("sp", WSTS[:, :], WST[:].rearrange("p g q -> p (g q)"), ["WST"], ["WSTS"])
            P.add("pool", lambda e: e.iota(QI[:], [[128, 2]], base=0, channel_multiplier=1), (), ["QI"])
            P.add("pool", lambda e: e.iota(RI[:], [[1, 64]], base=0, channel_multiplier=0), (), ["RI"])
            cp("dve", QF[:], QI[:], ["QI"], ["QF"])
            cp("dve", RV[:], RI[:], ["RI"], ["RV"])
            act(FR[:], QF[:], AF.Exp, ["QF"], ["FR"], scale=-math.log(10000.0) / 256.0)
            for q2 in range(2):
                ts("dve", ANG[:, q2, :], RV[:], FR[:, q2:q2 + 1], None, ALU.mult, None, ["RV", "FR"], ["ANG"])
            ts("dve", ANG[:, 2:4, :], ANG[:, 0:2, :], math.pi / 2, None, ALU.add, None, ["ANG"], ["ANG"])
            ts("dve", TQ[:], ANG[:], 1.0 / (2 * math.pi), 0.5, ALU.mult, ALU.add, ["ANG"], ["TQ"])
            cp("dve", KI[:], TQ[:], ["TQ"], ["KI"])
            cp("dve", KF[:], KI[:], ["KI"], ["KF"])
            tt("dve", CM[:], KF[:], TQ[:], ALU.is_gt, ["KF", "TQ"], ["CM"])
            tt("dve", KF[:], KF[:], CM[:], ALU.subtract, ["KF", "CM"], ["KF"])
            stt(ANG[:], KF[:], -2 * math.pi, ANG[:], ALU.mult, ALU.add, ["KF", "ANG"], ["ANG"])
            ts("dve", ANG[:], ANG[:], 3.14159, -3.14159, ALU.min, ALU.max, ["ANG"], ["ANG"])
            act(TAB[:], ANG[:], AF.Sin, ["ANG"], ["TAB"])
            WF = [sb("WF", [128, 8, D], F32, ph) for _ in range(2)]
            WMo = [sb("WMo", [128, 8, D], BF16, ph) for _ in range(2)]
            hwn = 0
            for i in (0, 1, 2, 3, 5, 6, 7, 4, 8):
                src_i = w_mod[:, i * D:(i + 1) * D].rearrange("(kc p) n -> p kc n", p=128)
                if i == 4:
                    slot = 0
                    wt = WM[slot]
                    wkey = ("WM", slot)
                    dma("pool", wt[:], src_i, (), [wkey])
                else:
                    slot = hwn % 2
                    hwn += 1
                    wt = WMo[slot]
                    wkey = ("WMo", slot)
                    dma("sp", WF[slot][:], src_i, (), [("WF", slot)])
                    cp("dve", wt[:, 0:4, :], WF[slot][:, 0:4, :], [("WF", slot)], [wkey])
                    act(wt[:, 4:8, :], WF[slot][:, 4:8, :], AF.Copy, [("WF", slot)], [wkey])

                def emit_mod(e, i=i, wt=wt):
                    ins = None
                    for m in range(8):
                        for kc in range(8):
                            ins = e.matmul(PS[:, 6, (i * 8 + m) * 2:(i * 8 + m) * 2 + 2],
                                           lhsT=wt[:, kc, m * 128:(m + 1) * 128], rhs=SB2[:, kc, :],
                                           start=(kc == 0), stop=(kc == 7))
                    return ins
                P.add("pe", emit_mod, [wkey, "SB2"], [kb(6)])
            for d in range(2):
                for kind, wsrc in enumerate((w_r, w_i)):
                    v = wsrc[d].rearrange("(c two) i j -> two i c j", two=2)
                    for h in range(2):
                        dma("pool", BD[64 * h:64 * h + 64, d, kind, :, 64 * h:64 * h + 64], v[h], ["BD"], ["BD"])
            psmod = PS[:, 6, 0:144].rearrange("p (r c) -> p r c", c=2)
            for cond in range(2):
                tt("dve", MOD[:, cond, :], psmod[:, :, cond], VT[:, R_BMOD:R_BMOD + 72], ALU.add, [kb(6), "VT"], ["MOD"])
            for cond in range(2):
                for k in range(3):
                    stt(NS[:, cond, k, :], MOD[:, cond, (3 * k + 1) * 8:(3 * k + 1) * 8 + 8], 1.0,
                        VT[:, R_N1 + 8 * k:R_N1 + 8 * k + 8], ALU.add, ALU.mult, ["MOD", "VT"], ["NS"])
                    ts("dve", GH[:, cond, k, :], MOD[:, cond, (3 * k + 2) * 8:(3 * k + 2) * 8 + 8], 0.5, None,
                       ALU.mult, None, ["MOD"], ["GH"])
            if debug:
                dma("sp", DBG[:, 0:256], VT[:], ["VT"], ["DBG"])
                dma("sp", DBG[:, 256:400], MOD[:].rearrange("p a b -> p (a b)"), ["MOD"], ["DBG"])
                dma("sp", DBG[:, 400:656], TAB[:].rearrange("p a b -> p (a b)"), ["TAB"], ["DBG"])
                dma("sp", DBG[:, 656:672], LL[:], ["LL"], ["DBG"])
            P.barrier(["IDN", "VT", "MOD", "NS", "GH", "LL", "L4", "L8", "HBR", "HBI", "TAB", "ONES", "NEGH", "GVB",
                       "BSH", "WST", "BD", "STF", "STB"] + [kb(b) for b in range(8)])

        consts = ["IDN", "VT", "MOD", "NS", "GH", "L4", "L8", "HBR", "HBI", "TAB", "ONES", "NEGH", "GVB", "BSH", "WST", "BD"]

        def norm_stats(xg, B):
            SQ, LNT, RS = B["SQ"], B["MS"], B["RS"]
            for tti in range(2):
                cols = slice(tti * 512, (tti + 1) * 512)
                xk = [("xg", m, tti) for m in range(8)]
                act(SQ[:], xg[:, :, cols], AF.Square, xk, ["SQ"])
                mm(bank(6 + tti), [(ONES[:], SQ[:, m, :]) for m in range(8)], ["SQ", "ONES"], [kb(6 + tti)])
            act(LNT[:], PS[:, 6:8, :], AF.Ln, [kb(6), kb(7), "EPSC"], ["MS"], bias=EPSC[:, 0:1], scale=1.0 / D)
            act(RS[:], LNT[:], AF.Exp, ["MS"], ["RS"], scale=-0.5)

        def norm_apply(xg, tti, cond, k, B, xmod_out=None, yf_out=None):
            cols = slice(tti * 512, (tti + 1) * 512)
            RS, TMP = B["RS"], B["TMP"]
            for m in range(8):
                if xmod_out is not None:
                    sl = m % 2
                    stt(TMP[sl][:], xg[:, m, cols], ns_ap(cond, k, m), RS[:, tti, :], ALU.mult, ALU.mult,
                        [("xg", m, tti), "RS", "NS"], [("TMP", sl)])
                    act(xmod_out[:, m, cols], TMP[sl][:], AF.Identity, [("TMP", sl), "MOD"], [("xmod", m, tti)],
                        bias=sh_ap(cond, k, m))
                else:
                    stt(yf_out[:, m, :], xg[:, m, cols], VT[:, R_NF + m:R_NF + m + 1], RS[:, tti, :], ALU.mult, ALU.mult,
                        [("xg", m, tti), "RS", "VT"], [("YF", m)])

        def ffn_group(which, cond, k_gate, xg, B, first=True):
            wg, wu, wd = ffw[which]
            xmod, H, GU, WD, SG = B["xmod"], B["H"], B["GU"], B["WD"], B["SG"]
            cnt = 0
            for jp in range(NJ // 2):
                slot = jp % 3
                wload(first, [(GU[slot][:, 0, :, :], wg[:, jp * 256:(jp + 1) * 256].rearrange("(kc p) n -> p kc n", p=128), ("GU", slot, 0)),
                              (GU[slot][:, 1, :, :], wu[:, jp * 256:(jp + 1) * 256].rearrange("(kc p) n -> p kc n", p=128), ("GU", slot, 1))],
                      GU[slot][:].rearrange("p a k n -> p (a k n)"), GUS[which][jp], [("GU", slot, 0), ("GU", slot, 1)],
                      ("GUS", which, jp))
                for jj in range(2):
                    j = 2 * jp + jj
                    for tti in range(2):
                        cols = slice(tti * 512, (tti + 1) * 512)
                        bg = cnt % 2
                        bu = 2 + cnt % 2
                        cnt += 1
                        xk = [("xmod", m, tti) for m in range(8)]
                        mm(bank(bg), [(GU[slot][:, 0, kc, jj * 128:(jj + 1) * 128], xmod[:, kc, cols]) for kc in range(8)],
                           [("GU", slot, 0)] + xk, [kb(bg)])
                        mm(bank(bu), [(GU[slot][:, 1, kc, jj * 128:(jj + 1) * 128], xmod[:, kc, cols]) for kc in range(8)],
                           [("GU", slot, 1)] + xk, [kb(bu)])
                        sl = cnt % 2
                        act(SG[sl][:], bank(bg), AF.Silu, [kb(bg)], [("SG", sl)])
                        tt("dve", H[:, j, cols], SG[sl][:], bank(bu), ALU.mult, [("SG", sl), kb(bu)], [("H", j, tti)])
            cnt = 0
            for m in range(8):
                slot = m % 2
                wload(first, [(WD[slot][:], wd[:, m * 128:(m + 1) * 128].rearrange("(kc p) n -> p kc n", p=128), ("WD", slot))],
                      WD[slot][:].rearrange("p k n -> p (k n)"), WDS[which][m], [("WD", slot)], ("WDS", which, m))
                for tti in range(2):
                    cols = slice(tti * 512, (tti + 1) * 512)
                    b = 4 + cnt % 2
                    cnt += 1
                    mm(bank(b), [(WD[slot][:, j, :], H[:, j, cols]) for j in range(NJ)],
                       [("WD", slot)] + [("H", j, tti) for j in range(NJ)], [kb(b)])
                    stt(xg[:, m, cols], bank(b), gh_ap(cond, k_gate, m), xg[:, m, cols], ALU.mult, ALU.add,
                        [kb(b), "GH", ("xg", m, tti)], [("xg", m, tti)])

        def ff_buffers(ph, first):
            B = {}
            B["xmod"] = sb("xmod", [128, 8, 1024], BF16, ph)
            B["H"] = sb("H", [128, NJ, 1024], BF16, ph)
            B["GU"] = [sb("GU%d" % i, [128, 2, 8, 256], BF16, ph) for i in range(3)]
            B["WD"] = [sb("WD%d" % i, [128, NJ, 128], BF16, ph) for i in range(2)]
            B["SG"] = [sb("SG%d" % i, [128, 512], F32, ph) for i in range(2)]
            B["SQ"] = sb("SQ", [128, 8, 512], BF16, ph)
            B["MS"] = sb("MS", [128, 2, 512], F32, ph)
            B["RS"] = sb("RS", [128, 2, 512], F32, ph)
            B["TMP"] = [sb("TMP%d" % i, [128, 512], F32, ph) for i in range(2)]
            if first:
                B["XT"] = [sb("XT%d" % i, [128, D], F32, ph) for i in range(4)]
            else:
                B["YF"] = sb("YF", [128, 8, 512], F32, ph)
                B["YT"] = [sb("YT%d" % i, [128, D], F32, ph) for i in range(2)]
            return B

        def ff_keys():
            ks = [("xmod", m, t) for m in range(8) for t in range(2)] + [("H", j, t) for j in range(NJ) for t in range(2)]
            ks += [("GU", s, i) for s in range(3) for i in range(2)] + [("WD", s) for s in range(2)]
            ks += [("SG", 0), ("SG", 1), "SQ", "MS", "RS", ("TMP", 0), ("TMP", 1)]
            ks += [("XT", i) for i in range(4)] + [("YF", m) for m in range(8)] + [("YT", 0), ("YT", 1)]
            ks += [("xg", m, t) for m in range(8) for t in range(2)]
            return ks

        allps = [kb(b) for b in range(8)]

        def ff1_group(g, xg, B):
            cond = 0 if g < 4 else 1
            XT = B["XT"]
            for tti in range(2):
                T = 2 * g + tti
                t0 = T * 512
                cols = slice(tti * 512, (tti + 1) * 512)
                for s in range(4):
                    dma("sp", XT[s][:], xrows(t0 + s * 128, 128), (), [("XT", s)])
                for m in range(8):
                    b = 4 + m % 4

                    def emit_tr(e, m=m, b=b):
                        ins = None
                        for s in range(4):
                            ins = e.transpose(out=PS[:, b, s * 128:(s + 1) * 128], in_=XT[s][:, m * 128:(m + 1) * 128],
                                              identity=IDN[:])
                        return ins
                    P.add("pe", emit_tr, [("XT", s) for s in range(4)] + ["IDN"], [kb(b)])
                    if cond == 0:
                        pv = bank(b).rearrange("p (a b) -> p a b", b=64)
                        ov = xg[:, m, cols].rearrange("p (a b) -> p a b", b=64)
                        if m < 4:
                            tv = TAB[:, m, 8 * T:8 * T + 8].unsqueeze(2).broadcast_to([128, 8, 64])
                        else:
                            tv = TAB[:, m - 4, :].unsqueeze(1).broadcast_to([128, 8, 64])
                        tt("dve", ov, pv, tv, ALU.add, [kb(b), "TAB"], [("xg", m, tti)])
                    else:
                        cp("dve", xg[:, m, cols], bank(b), [kb(b)], [("xg", m, tti)])
            norm_stats(xg, B)
            for tti in range(2):
                norm_apply(xg, tti, cond, 0, B, xmod_out=B["xmod"])
            ffn_group(1, cond, 0, xg, B, first=(g == 0))
            norm_stats(xg, B)
            for tti in range(2):
                T = 2 * g + tti
                t0 = T * 512
                cols = slice(tti * 512, (tti + 1) * 512)
                norm_apply(xg, tti, cond, 1, B, xmod_out=B["xmod"])
                dma("sp", HM[:, :, t0:t0 + 512].rearrange("m p t -> p m t"), B["xmod"][:, :, cols],
                    [("xmod", m, tti) for m in range(8)], [("HM", T)])
                dma("sp", XA[:, :, t0:t0 + 512].rearrange("m p t -> p m t"), xg[:, :, cols],
                    [("xg", m, tti) for m in range(8)], [("XA", T)])

        def s1_group(sg):
            if sg == 0:
                T0, nt, nseq, L, cond = 0, 8, 1, 4096, 0
            else:
                T0, nt, nseq, L, cond = 8, 2, 4, 256, 1
            Ts = nt * 512
            spt = 512 // L if L < 512 else 1
            with contextlib.ExitStack() as ph:
                HMr = [sb("HMr", [128, 8, 512], BF16, ph) for _ in range(3)]
                XR = sb("XR", [128, nseq * (L + 3)], F32, ph)
                XC = sb("XC", [128, nseq, L], F32, ph)
                XCB = sb("XCB", [128, Ts], BF16, ph)
                AB = [[sb("ABI", [128, nseq, L], F32, ph) for _ in range(3)] for _ in range(2)]
                GGf = [sb("GGf", [128, Ts], BF16, ph) for _ in range(2)]
                YRt = [sb("YRt", [128, 512], BF16, ph) for _ in range(4)]
                WX = [sb("WX", [128, 2, 8, 128], BF16, ph) for _ in range(2)]
                TH = [sb("TH", [128, 512], F32, ph) for _ in range(2)]
                XR3 = XR[:, :].rearrange("p (s l) -> p s l", l=L + 3)
                XCf = XC[:].rearrange("p s l -> p (s l)")
                flat = lambda t3: t3[:].rearrange("p s l -> p (s l)")
                tk = lambda name: [(name, t) for t in range(nt)]
                memset("dve", XR3[:, :, 0:2], 0.0, ["XRhalo"])
                memset("dve", XR3[:, :, L + 2:L + 3], 0.0, ["XRhalo"])
                cnt = [0, 0, 0]

                def load_wx(c_):
                    sl_ = c_ % 2
                    wload(sg == 0 and not PRECAST, [(WX[sl_][:, 0, :, :], w_in[:, c_ * 128:(c_ + 1) * 128].rearrange("(kc p) n -> p kc n", p=128), ("WX", sl_, 0)),
                                    (WX[sl_][:, 1, :, :], w_in[:, D + c_ * 128:D + (c_ + 1) * 128].rearrange("(kc p) n -> p kc n", p=128), ("WX", sl_, 1))],
                          WX[sl_][:].rearrange("p a k n -> p (a k n)"), WXS[c_], [("WX", sl_, 0), ("WX", sl_, 1)], ("WXS", c_))

                def prep_tile(c, t):
                    slot = c % 2
                    GGc = GGf[c % 2]
                    T = T0 + t
                    cols = slice(t * 512, (t + 1) * 512)
                    hs = cnt[1] % 3
                    cnt[1] += 1
                    dma("sp", HMr[hs][:], HM[:, :, T * 512:(T + 1) * 512].rearrange("m p t -> p m t"), [("HM", T)], [("HMr", hs)])
                    bx = cnt[0] % 2
                    bgr = 6 + cnt[0] % 2
                    cnt[0] += 1
                    mm(bank(bx), [(WX[slot][:, 0, kc, :], HMr[hs][:, kc, :]) for kc in range(8)],
                       [("WX", slot, 0), ("HMr", hs)], [kb(bx)])
                    mm(bank(bgr), [(WX[slot][:, 1, kc, :], HMr[hs][:, kc, :]) for kc in range(8)],
                       [("WX", slot, 1), ("HMr", hs)], [kb(bgr)])
                    if L >= 512:
                        ov = XR3[:, 0, 2 + t * 512:2 + (t + 1) * 512]
                        iv = bank(bx)
                    else:
                        ov = XR3[:, t * spt:(t + 1) * spt, 2:2 + L]
                        iv = bank(bx).rearrange("p (s l) -> p s l", l=L)
                    act(ov, iv, AF.Copy, [kb(bx)], [("XR", t)])
                    act(GGc[:, cols], bank(bgr), AF.Copy, [kb(bgr)], [("GGf", c % 2, t)])

                def stage_gelu(c):
                    GGc = GGf[c % 2]
                    gk = [("GGf", c % 2, t) for t in range(nt)]
                    act(GGc[:], GGc[:], AF.Gelu_apprx_tanh, gk, gk)

                def stage_V(c):
                    ts("dve", XC[:], XR3[:, :, 0:L], VT[:, R_CW + c:R_CW + c + 1], VT[:, R_CB + c:R_CB + c + 1],
                       ALU.mult, ALU.add, tk("XR") + ["XRhalo", "VT"], tk("XC"))
                    for k in range(1, 4):
                        stt(XC[:], XR3[:, :, k:k + L], VT[:, R_CW + 8 * k + c:R_CW + 8 * k + c + 1], XC[:],
                            ALU.mult, ALU.add, tk("XR") + tk("XC") + ["VT", "XRhalo"], tk("XC"))
                    act(XCB[:], XCf, AF.Copy, tk("XC"), tk("XCB"))

                def stage_G_act(c, d, prep_c=None):
                    dc = d * 8 + c
                    A_, B_, I_ = AB[d]
                    Af, Bf, If = flat(A_), flat(B_), flat(I_)
                    for t in range(nt):
                        cols = slice(t * 512, (t + 1) * 512)
                        br = 2 + cnt[2] % 2
                        bi = 4 + cnt[2] % 2
                        sl = cnt[2] % 2
                        cnt[2] += 1
                        mm(bank(br), [(BD[:, d, 0, c, :], XCB[:, cols])], ["BD", ("XCB", t)], [kb(br)])
                        mm(bank(bi), [(BD[:, d, 1, c, :], XCB[:, cols])], ["BD", ("XCB", t)], [kb(bi)])
                        act(TH[sl][:], bank(br), AF.Tanh, [kb(br), "HBR"], [("TH", sl)], bias=HBR[:, dc:dc + 1], scale=0.5)
                        act(Af[:, cols], TH[sl][:], AF.Exp, [("TH", sl), "L4"], [("A", d, t)],
                            bias=L4[:, dc:dc + 1], scale=L4[:, dc:dc + 1])
                        tt("dve", Bf[:, cols], Af[:, cols], Af[:, cols], ALU.mult, [("A", d, t)], [("B", d, t)])
                        act(If[:, cols], bank(bi), AF.Tanh, [kb(bi), "HBI"], [("I", d, t)], bias=HBI[:, dc:dc + 1], scale=0.5)
                        if prep_c is not None:
                            prep_tile(prep_c, t)
                    if prep_c is not None and prep_c + 1 < 8:
                        load_wx(prep_c + 1)

                def stage_sqrt(c, d):
                    Bf = flat(AB[d][1])
                    ka = lambda n: [(n, d, t) for t in range(nt)]
                    act(Bf, Bf, AF.Sqrt, ka("B") + ["QUART"], ka("B"), bias=QUART[:, 0:1], scale=-0.25)

                def stage_G_dve1(c, d):
                    A_, B_, I_ = AB[d]
                    If = flat(I_)
                    ka = lambda n: [(n, d, t) for t in range(nt)]
                    stt(If, If, 1.0, XCf, ALU.add, ALU.mult, ka("I") + tk("XC"), ka("I"))

                def stage_G_dve2(c, d):
                    A_, B_, I_ = AB[d]
                    Af, Bf, If = flat(A_), flat(B_), flat(I_)
                    ka = lambda n: [(n, d, t) for t in range(nt)]
                    tt("dve", Bf, Bf, If, ALU.mult, ka("B") + ka("I"), ka("B"))
                    with P.hazard():
                        for s_ in range(nseq):
                            if sg == 0:
                                r0 = R_SF if d == 0 else R_SB
                                init = VT[:, r0 + c:r0 + c + 1]
                            else:
                                init = 0.0
                            if d == 0:
                                P.add("dve", lambda e, s_=s_, init=init, A_=A_, B_=B_: e.tensor_tensor_scan(
                                    out=B_[:, s_, :], data0=A_[:, s_, :], data1=B_[:, s_, :], initial=init,
                                    op0=ALU.mult, op1=ALU.add), ka("A") + ka("B") + ["VT"], ka("B"))
                            else:
                                P.add("dve", lambda e, s_=s_, init=init, A_=A_, B_=B_: e.tensor_tensor_scan(
                                    out=B_[:, s_, ::-1], data0=A_[:, s_, ::-1], data1=B_[:, s_, ::-1], initial=init,
                                    op0=ALU.mult, op1=ALU.add), ka("A") + ka("B") + ["VT"], ka("B"))
                        if sg == 1:
                            if d == 0:
                                cp("dve", STF[:, :].rearrange("p (s c) -> p s c", c=8)[:, :, c], B_[:, :, L - 1], ka("B"), ["STF"])
                            else:
                                cp("dve", STB[:, :].rearrange("p (s c) -> p s c", c=8)[:, :, c], B_[:, :, 0], ka("B"), ["STB"])

                def stage_ADD(c):
                    B0f, B1f = flat(AB[0][1]), flat(AB[1][1])
                    with P.hazard():
                        tt("dve", B0f, B0f, B1f, ALU.add, [("B", 0, t) for t in range(nt)] + [("B", 1, t) for t in range(nt)],
                           [("B", 0, t) for t in range(nt)])

                def stage_Y(c):
                    B0f = flat(AB[0][1])
                    GGc = GGf[c % 2]
                    for t in range(nt):
                        cols = slice(t * 512, (t + 1) * 512)
                        ys_ = t % 4
                        tt("dve", YRt[ys_][:], GGc[:, cols], B0f[:, cols], ALU.mult, [("GGf", c % 2, t), ("B", 0, t)], [("YRt", ys_)])
                        t0 = (T0 + t) * 512
                        dma("pool", YR[c, :, t0:t0 + 512], YRt[ys_][:], [("YRt", ys_)], [("YR", sg, c, t)])

                pre2 = []
                if False and sg == 1 and PRECAST:
                    STG2 = [sb("STG2", [128, 4096], BF16, ph) for _ in range(3)]
                    pre2 = ff_precast_jobs(2, STG2)
                load_wx(0)
                for t in range(nt):
                    prep_tile(0, t)
                load_wx(1)
                stage_V(0)
                for c in range(8):
                    for _ in range(3):
                        if pre2:
                            pre2.pop(0)()
                    nxt = c + 1 if c + 1 < 8 else None
                    if sg == 0:
                        stage_G_act(c, 0)
                        stage_sqrt(c, 0)
                        stage_G_dve1(c, 0)
                        stage_G_dve2(c, 0)
                        stage_G_act(c, 1, nxt)
                        stage_sqrt(c, 1)
                    else:
                        stage_G_act(c, 0)
                        stage_G_act(c, 1, nxt)
                        stage_sqrt(c, 0)
                        stage_sqrt(c, 1)
                        stage_G_dve1(c, 0)
                        stage_G_dve2(c, 0)
                    stage_gelu(c)
                    stage_G_dve1(c, 1)
                    if nxt is not None:
                        stage_V(nxt)
                    stage_G_dve2(c, 1)
                    stage_ADD(c)
                    stage_Y(c)
                if sg == 1:
                    STT_ = sb("STT_", [32, 128], F32, ph)
                    for nm, src, dst in (("f", STF, stf_d), ("b", STB, stb_d)):
                        P.add("pe", lambda e, src=src: e.transpose(out=PS[0:32, 0, 0:128], in_=src[:, :], identity=IDN[:]),
                              ["STF", "STB", "IDN"], [kb(0)])
                        cp("dve", STT_[:], PS[0:32, 0, 0:128], [kb(0)], ["STT_"])
                        dma("pool", dst.rearrange("s (c p) -> (s c) p", p=128), STT_[:], ["STT_"], [("st", nm)])
                P.barrier()

        def s2_keys():
            ks = [("HMg", t) for t in range(2)] + [("YRg", t) for t in range(2)] + ["WV", ("WU", 0), ("WU", 1)]
            ks += [("GV", 0), ("GV", 1), "JUNK", "SSV", "RSV"] + [("VN", n) for n in range(4)]
            ks += [("GUt", 0), ("GUt", 1)] + [("YG", g, t) for g in range(8) for t in range(2)]
            ks += [("MG", f, t) for f in range(8) for t in range(2)] + [("WM4", s, i) for s in range(2) for i in range(4)]
            ks += [("TA", 0), ("TA", 1), ("TB", 0), ("TB", 1), ("M1", 0), ("M1", 1), ("M2", 0), ("M2", 1), ("WO", 0), ("WO", 1)]
            return ks

        S2B = {}

        def s2_group(g, xg, ph_outer):
            cond = 0 if g < 4 else 1
            firstg = (len(S2B) == 0)
            castg = firstg and not PRECAST

            def sbm(name, shape, dt):
                if name not in S2B:
                    S2B[name] = sb(name, shape, dt, ph_outer)
                return S2B[name]
            if True:
                ph = None
                HMg = sbm("HMg", [128, 8, 1024], BF16)
                YRg = sbm("YRg", [128, 8, 1024], BF16)
                WV = sbm("WV", [128, 8, D], BF16)
                GVB = sbm("GVB", [128, D], F32)
                BSH = sbm("BSH", [2, 8, 128], BF16)
                WST = sbm("WST", [128, 8, 128], BF16)
                if firstg:
                    dma("sp", GVB[:], gvb_d[:, :], (), ["GVB"])
                    dma("sp", BSH[:].rearrange("o g p -> o (g p)"), BSHS[:, :], (), ["BSH"])
                    dma("sp", WST[:].rearrange("p g q -> p (g q)"), WSTS[:, :], (), ["WST"])
                WU = [sbm("WU%d" % i, [128, 8, 128], BF16) for i in range(3)]
                GV = [sbm("GV%d" % i, [128, D], F32) for i in range(2)]
                RSV = sbm("RSV", [128, 4], F32)
                VN = sbm("VN", [128, 4, D], BF16)
                GUt = [sbm("GUt%d" % i, [128, 512], F32) for i in range(2)]
                YG = sbm("YG", [128, 8, 1024], BF16)
                MG = sbm("MG", [128, 8, 1024], BF16)
                WM4 = [sbm("WM4%d" % i, [128, 4, 8, 128], BF16) for i in range(3)]
                TA = [sbm("TA%d" % i, [128, 512], F32) for i in range(2)]
                TB = [sbm("TB%d" % i, [128, 512], F32) for i in range(2)]
                M1 = [sbm("M1%d" % i, [128, 512], F32) for i in range(2)]
                M2 = [sbm("M2%d" % i, [128, 512], F32) for i in range(2)]
                WO = WU
                if firstg:
                    wload(castg, [(WV[:], w_in[:, 3 * D:4 * D].rearrange("(kc p) n -> p kc n", p=128), "WV")],
                          WV[:].rearrange("p k n -> p (k n)"), WVS, ["WV"], "WVS")
                def load_hm_yr(g_):
                    for tti_ in range(2):
                        T_ = 2 * g_ + tti_
                        cols_ = slice(tti_ * 512, (tti_ + 1) * 512)
                        dma("sp", HMg[:, :, cols_], HM[:, :, T_ * 512:(T_ + 1) * 512].rearrange("m p t -> p m t"), [("HM", T_)], [("HMg", tti_)])
                    for tti_ in range(2):
                        T_ = 2 * g_ + tti_
                        cols_ = slice(tti_ * 512, (tti_ + 1) * 512)
                        dma("sp", YRg[:, :, cols_], YR[:, :, T_ * 512:(T_ + 1) * 512].rearrange("m p t -> p m t"), (), [("YRg", tti_)])
                if firstg:
                    load_hm_yr(g)
                cnt = 0
                for tti in range(2):
                    cols = slice(tti * 512, (tti + 1) * 512)
                    for n in range(4):
                        c0 = tti * 512 + n * 128
                        sl = n % 2
                        b0 = 0 if sl == 0 else 6
                        for half in range(2):
                            mm(bank(b0 + half), [(HMg[:, kc, c0:c0 + 128], WV[:, kc, half * 512:(half + 1) * 512]) for kc in range(8)],
                               [("HMg", tti), "WV"], [kb(b0 + half)])
                        act(GV[sl][:].rearrange("p (h n) -> p h n", n=512), PS[:, b0:b0 + 2, :], AF.Gelu_apprx_tanh,
                            [kb(b0), kb(b0 + 1)], [("GV", sl)])
                        act(VN[:, n, :], GV[sl][:], AF.Square, [("GV", sl)], [("VN", n), ("SSV", n)], accum=SSV[:, n:n + 1])
                        ts("dve", RSV[:, n:n + 1], SSV[:, n:n + 1], 1.0 / D, EPS, ALU.mult, ALU.add, [("SSV", n)], [("RSV", n)])
                        tt("pool", RSV[:, n:n + 1], RSV[:, n:n + 1], NEGH[:, 0:1], ALU.pow, [("RSV", n), "NEGH"], [("RSV", n)])
                        stt(VN[:, n, :], GV[sl][:], RSV[:, n:n + 1], GVB[:], ALU.mult, ALU.mult,
                            [("GV", sl), ("RSV", n), "GVB"], [("VN", n)])
                    for gi in range(8):
                        slot = gi % 3
                        wload(castg and tti == 0,
                              [(WU[slot][:], w_in[:, 2 * D + gi * 128:2 * D + (gi + 1) * 128].rearrange("(kc p) n -> p kc n", p=128), ("WU", slot))],
                              WU[slot][:].rearrange("p k n -> p (k n)"), WUS[gi], [("WU", slot)], ("WUS", gi))
                        bu = 2 + cnt % 2
                        bm = 4 + cnt % 2
                        sl = cnt % 2
                        cnt += 1
                        mm(bank(bu), [(WU[slot][:, kc, :], HMg[:, kc, cols]) for kc in range(8)],
                           [("WU", slot), ("HMg", tti)], [kb(bu)])
                        act(GUt[sl][:], bank(bu), AF.Gelu_apprx_tanh, [kb(bu)], [("GUt", sl)])

                        def emit_mix(e, gi=gi, bm=bm):
                            e.matmul(PS[:, bm, :].rearrange("p (a b) -> p a b", b=128), lhsT=ONES[0:2, :],
                                     rhs=BSH[0:2, gi, :].unsqueeze(1).broadcast_to([2, 4, 128]), start=True, stop=False)
                            ins = None
                            for n in range(4):
                                ins = e.matmul(PS[:, bm, n * 128:(n + 1) * 128], lhsT=VN[:, n, gi * 128:(gi + 1) * 128],
                                               rhs=WST[:, gi, :], start=False, stop=(n == 3))
                            return ins
                        P.add("pe", emit_mix, ["ONES", "BSH", "WST"] + [("VN", n) for n in range(4)], [kb(bm)])
                        tt("dve", YG[:, gi, cols], GUt[sl][:], bank(bm), ALU.mult, [("GUt", sl), kb(bm)], [("YG", gi, tti)])
                cnt = 0
                for f in range(8):
                    slot = f % 3
                    srcs = (w_in[:, 4 * D + f * 128:4 * D + (f + 1) * 128], w_in[:, 5 * D + f * 128:5 * D + (f + 1) * 128],
                            w_br[:, f * 128:(f + 1) * 128], w_bg[:, f * 128:(f + 1) * 128])
                    wload(castg, [(WM4[slot][:, i4, :, :], src.rearrange("(kc p) n -> p kc n", p=128), ("WM4", slot, i4))
                                   for i4, src in enumerate(srcs)],
                          WM4[slot][:].rearrange("p a k n -> p (a k n)"), WM4S[f], [("WM4", slot, i4) for i4 in range(4)], ("WM4S", f))
                    for tti in range(2):
                        cols = slice(tti * 512, (tti + 1) * 512)
                        par = cnt % 2
                        cnt += 1
                        rhs_src = (HMg, HMg, YRg, YG)
                        rk = ([("HMg", tti)], [("HMg", tti)], [("YRg", tti)], [("YG", gi, tti) for gi in range(8)])
                        for i4 in range(4):
                            b = 2 * i4 + par
                            mm(bank(b), [(WM4[slot][:, i4, kc, :], rhs_src[i4][:, kc, cols]) for kc in range(8)],
                               [("WM4", slot, i4)] + rk[i4], [kb(b)])
                        act(TA[par][:], bank(0 + par), AF.Tanh, [kb(0 + par)], [("TA", par)], scale=0.5)
                        act(TB[par][:], bank(2 + par), AF.Tanh, [kb(2 + par)], [("TB", par)], scale=0.5)
                        stt(M1[par][:], TA[par][:], 1.0, bank(4 + par), ALU.add, ALU.mult, [("TA", par), kb(4 + par)], [("M1", par)])
                        stt(M2[par][:], TB[par][:], 1.0, bank(6 + par), ALU.add, ALU.mult, [("TB", par), kb(6 + par)], [("M2", par)])
                        tt("pool", MG[:, f, cols], M1[par][:], M2[par][:], ALU.add, [("M1", par), ("M2", par)], [("MG", f, tti)])
                for tti in range(2):
                    T = 2 * g + tti
                    t0 = T * 512
                    cols = slice(tti * 512, (tti + 1) * 512)
                    dma("sp", xg[:, :, cols], XA[:, :, t0:t0 + 512].rearrange("m p t -> p m t"), [("XA", T)],
                        [("xg", m, tti) for m in range(8)])
                cnt = 0
                for m in range(8):
                    slot = m % 3
                    wload(castg, [(WO[slot][:], w_out[:, m * 128:(m + 1) * 128].rearrange("(kc p) n -> p kc n", p=128), ("WU", slot))],
                          WO[slot][:].rearrange("p k n -> p (k n)"), WOS[m], [("WU", slot)], ("WOS", m))
                    if m == 1 and g + 1 < 5:
                        load_hm_yr(g + 1)
                    for tti in range(2):
                        cols = slice(tti * 512, (tti + 1) * 512)
                        b = cnt % 2
                        cnt += 1
                        mm(bank(b), [(WO[slot][:, kc, :], MG[:, kc, cols]) for kc in range(8)],
                           [("WU", slot)] + [("MG", f, tti) for f in range(8)], [kb(b)])
                        stt(xg[:, m, cols], bank(b), gh_ap(cond, 1, m), xg[:, m, cols], ALU.mult, ALU.add,
                            [kb(b), "GH", ("xg", m, tti)], [("xg", m, tti)])
                for tti in range(2):
                    T = 2 * g + tti
                    dma("pool", XA[:, :, T * 512:(T + 1) * 512].rearrange("m p t -> p m t"), xg[:, :, tti * 512:(tti + 1) * 512],
                        [("xg", m, tti) for m in range(8)], [("XA", T)])

        def ff2_group(g, xg):
            cond = 0 if g < 4 else 1
            with contextlib.ExitStack() as ph:
                B = ff_buffers(ph, first=False)
                norm_stats(xg, B)
                for tti in range(2):
                    norm_apply(xg, tti, cond, 2, B, xmod_out=B["xmod"])
                ffn_group(2, cond, 2, xg, B, first=(g == 0))
                YF, YT = B["YF"], B["YT"]
                cnt = 0
                norm_stats(xg, B)
                for tti in range(2):
                    T = 2 * g + tti
                    t0 = T * 512
                    norm_apply(xg, tti, cond, None, B, yf_out=YF)
                    for s in range(4):
                        sl = cnt % 2
                        cnt += 1
                        for half in range(2):
                            b = 2 * (cnt % 2) + half

                            def emit_tr(e, s=s, half=half, b=b):
                                ins = None
                                for mmi in range(4):
                                    m = 4 * half + mmi
                                    ins = e.transpose(out=PS[:, b, mmi * 128:(mmi + 1) * 128], in_=YF[:, m, s * 128:(s + 1) * 128],
                                                      identity=IDN[:])
                                return ins
                            P.add("pe", emit_tr, [("YF", m) for m in range(8)] + ["IDN"], [kb(b)])
                            act(YT[sl][:, half * 512:(half + 1) * 512], bank(b), AF.Copy, [kb(b)], [("YT", sl)])
                        dma("pool", yrows(t0 + s * 128, 128), YT[sl][:], [("YT", sl)], [("yout", T, s)])
                P.barrier(ff_keys() + allps + s2_keys())

        QUART = sb("QUART", [128, 1], F32)
        memset("dve", QUART[:], 0.25, ["QUART"])
        EPSC = sb("EPSC", [128, 1], F32)
        memset("dve", EPSC[:], EPS, ["EPSC"])

        def ffn_tile(which, cond, k_gate, xg, xm, H, B, first, hooks):
            wg, wu, wd = ffw[which]
            GU, WD, SG = B["GU"], B["WD"], B["SG"]
            cnt = B["cnt"]
            for jp in range(NJ // 2):
                for hk in hooks.get(jp, ()):
                    hk()
                slot = cnt[0] % len(GU)
                cnt[0] += 1
                wload(first, [(GU[slot][:, 0, :, :], wg[:, jp * 256:(jp + 1) * 256].rearrange("(kc p) n -> p kc n", p=128), ("GU", slot, 0)),
                              (GU[slot][:, 1, :, :], wu[:, jp * 256:(jp + 1) * 256].rearrange("(kc p) n -> p kc n", p=128), ("GU", slot, 1))],
                      GU[slot][:].rearrange("p a k n -> p (a k n)"), GUS[which][jp], [("GU", slot, 0), ("GU", slot, 1)],
                      ("GUS", which, jp))
                for jj in range(2):
                    j = 2 * jp + jj
                    bg = cnt[1] % 2
                    bu = 2 + cnt[1] % 2
                    sl = cnt[1] % 2
                    cnt[1] += 1
                    xk = [("xm", id(xm), m) for m in range(8)]
                    mm(bank(bg), [(GU[slot][:, 0, kc, jj * 128:(jj + 1) * 128], xm[:, kc, :]) for kc in range(8)],
                       [("GU", slot, 0)] + xk, [kb(bg)])
                    mm(bank(bu), [(GU[slot][:, 1, kc, jj * 128:(jj + 1) * 128], xm[:, kc, :]) for kc in range(8)],
                       [("GU", slot, 1)] + xk, [kb(bu)])
                    act(SG[sl][:], bank(bg), AF.Silu, [kb(bg)], [("SG", sl)])
                    tt("dve", H[:, j, :], SG[sl][:], bank(bu), ALU.mult, [("SG", sl), kb(bu)], [("H", j)])
            for m in range(8):
                slot = cnt[2] % len(WD)
                cnt[2] += 1
                wload(first, [(WD[slot][:], wd[:, m * 128:(m + 1) * 128].rearrange("(kc p) n -> p kc n", p=128), ("WD", slot))],
                      WD[slot][:].rearrange("p k n -> p (k n)"), WDS[which][m], [("WD", slot)], ("WDS", which, m))
                b = 4 + cnt[2] % 2
                mm(bank(b), [(WD[slot][:, j, :], H[:, j, :]) for j in range(NJ)],
                   [("WD", slot)] + [("H", j) for j in range(NJ)], [kb(b)])
                stt(xg[:, m, :], bank(b), gh_ap(cond, k_gate, m), xg[:, m, :], ALU.mult, ALU.add,
                    [kb(b), "GH", ("xg", id(xg), m)], [("xg", id(xg), m)])

        def run_ff_tiles(which, ntiles=10):
            with contextlib.ExitStack() as ph:
                xgT = [sb("xgT", [128, 8, 512], F32, ph) for _ in range(3)]
                xmT = [sb("xmT", [128, 8, 512], BF16, ph) for _ in range(2)]
                HMo = sb("HMo", [128, 8, 512], BF16, ph) if which == 1 else None
                H = sb("Ht", [128, NJ, 512], BF16, ph)
                B = {"GU": [sb("GU", [128, 2, 8, 256], BF16, ph) for _ in range(3 if which == 1 else 4)],
                     "WD": [sb("WD", [128, NJ, 128], BF16, ph) for _ in range(2 if which == 1 else 3)],
                     "SG": [sb("SG", [128, 512], F32, ph) for _ in range(2)], "cnt": [0, 0, 0]}
                SQ = [sb("SQ", [128, 8, 512], BF16, ph) for _ in range(2)]
                LN = sb("LN", [128, 2, 512], F32, ph)
                RS = sb("RS", [128, 2, 512], F32, ph)
                TMP = [sb("TMP", [128, 512], F32, ph) for _ in range(2)]
                if which == 1:
                    XT = [sb("XT", [128, D], F32, ph) for _ in range(4)]
                else:
                    YF = sb("YF", [128, 8, 512], F32, ph)
                    YT = [sb("YT", [128, D], F32, ph) for _ in range(2)]
                tcnt = [0]
                k_in = 0 if which == 1 else 2
                pre_jobs = []
                if which == 1 and PRECAST:
                    STG = [sb("STG", [128, 4096], BF16, ph) for _ in range(2)]
                    pre_jobs = mixer_precast_jobs(STG) + ff_precast_jobs(2, STG)

                def condof(t):
                    return 0 if t < 8 else 1

                def P1(t, part=None):
                    xg = xgT[t % 3]
                    t0 = t * 512
                    if which == 2:
                        if part in (None, 0):
                            dma("sp", xg[:, :, :], XA[:, :, t0:t0 + 512].rearrange("m p t -> p m t"), [("XA", t)],
                                [("xg", id(xg), m) for m in range(8)])
                        return
                    if part in (None, 0):
                        for s in range(4):
                            dma("sp", XT[s][:], xrows(t0 + s * 128, 128), (), [("XT", s)])
                    if part == 0:
                        return
                    for m in range(8):
                        b = 4 + tcnt[0] % 2
                        tcnt[0] += 1

                        def emit_tr(e, m=m, b=b):
                            ins = None
                            for s in range(4):
                                ins = e.transpose(out=PS[:, b, s * 128:(s + 1) * 128], in_=XT[s][:, m * 128:(m + 1) * 128],
                                                  identity=IDN[:])
                            return ins
                        P.add("pe", emit_tr, [("XT", s) for s in range(4)] + ["IDN"], [kb(b)])
                        if condof(t) == 0:
                            pv = bank(b).rearrange("p (a b) -> p a b", b=64)
                            ov = xg[:, m, :].rearrange("p (a b) -> p a b", b=64)
                            if m < 4:
                                tv = TAB[:, m, 8 * t:8 * t + 8].unsqueeze(2).broadcast_to([128, 8, 64])
                            else:
                                tv = TAB[:, m - 4, :].unsqueeze(1).broadcast_to([128, 8, 64])
                            tt("dve", ov, pv, tv, ALU.add, [kb(b), "TAB"], [("xg", id(xg), m)])
                        else:
                            cp("dve", xg[:, m, :], bank(b), [kb(b)], [("xg", id(xg), m)])

                def N_sq(t, w, h=None):
                    xg = xgT[t % 3]
                    for hh in ((0, 1) if h is None else (h,)):
                        ms = range(4 * hh, 4 * hh + 4)
                        act(SQ[w][:, 4 * hh:4 * hh + 4, :], xg[:, 4 * hh:4 * hh + 4, :], AF.Square,
                            [("xg", id(xg), m) for m in ms], [("SQ", w, hh)])

                def N_mm(w):
                    mm(bank(6 + w), [(ONES[:], SQ[w][:, m, :]) for m in range(8)], [("SQ", w, 0), ("SQ", w, 1), "ONES"], [kb(6 + w)])

                def N_ln(w):
                    act(LN[:, w, :], bank(6 + w), AF.Ln, [kb(6 + w), "EPSC"], [("LN", w)], bias=EPSC[:, 0:1], scale=1.0 / D)
                    act(RS[:, w, :], LN[:, w, :], AF.Exp, [("LN", w)], [("RS", w)], scale=-0.5)

                def N_ln2():
                    act(LN[:], PS[:, 6:8, :], AF.Ln, [kb(6), kb(7), "EPSC"], [("LN", 0), ("LN", 1)], bias=EPSC[:, 0:1], scale=1.0 / D)
                    act(RS[:], LN[:], AF.Exp, [("LN", 0), ("LN", 1)], [("RS", 0), ("RS", 1)], scale=-0.5)

                def APPLY(t, w, k, store, h=None):
                    xg = xgT[t % 3]
                    xm = HMo if store else xmT[t % 2]
                    cond = condof(t)
                    for m in (range(8) if h is None else range(4 * h, 4 * h + 4)):
                        sl = m % 2
                        stt(TMP[sl][:], xg[:, m, :], ns_ap(cond, k, m), RS[:, w, :], ALU.mult, ALU.mult,
                            [("xg", id(xg), m), ("RS", w), "NS"], [("TMP", sl)])
                        act(xm[:, m, :], TMP[sl][:], AF.Identity, [("TMP", sl), "MOD"], [("xm", id(xm), m)],
                            bias=sh_ap(cond, k, m))
                    if store and h in (None, 1):
                        t0 = t * 512
                        dma("act", HM[:, :, t0:t0 + 512].rearrange("m p t -> p m t"), xm[:, :, :],
                            [("xm", id(xm), m) for m in range(8)], [("HM", t)])
                        dma("act", XA[:, :, t0:t0 + 512].rearrange("m p t -> p m t"), xg[:, :, :],
                            [("xg", id(xg), m) for m in range(8)], [("XA", t)])

                def FIN_dve(t):
                    xg = xgT[t % 3]
                    for m in range(8):
                        stt(YF[:, m, :], xg[:, m, :], VT[:, R_NF + m:R_NF + m + 1], RS[:, 0, :], ALU.mult, ALU.mult,
                            [("xg", id(xg), m), ("RS", 0), "VT"], [("YF", m)])

                def FIN_tr(t, srange):
                    t0 = t * 512
                    for s_ in srange:
                        sl = tcnt[0] % 2
                        tcnt[0] += 1
                        for half in range(2):
                            b = 4 + half

                            def emit_tr(e, s_=s_, half=half, b=b):
                                ins = None
                                for mmi in range(4):
                                    m = 4 * half + mmi
                                    ins = e.transpose(out=PS[:, b, mmi * 128:(mmi + 1) * 128], in_=YF[:, m, s_ * 128:(s_ + 1) * 128],
                                                      identity=IDN[:])
                                return ins
                            P.add("pe", emit_tr, [("YF", m) for m in range(8)] + ["IDN"], [kb(b)])
                            act(YT[sl][:, half * 512:(half + 1) * 512], bank(b), AF.Copy, [kb(b)], [("YT", sl)])
                        dma("pool", yrows(t0 + s_ * 128, 128), YT[sl][:], [("YT", sl)], [("yout", t, s_)])

                P1(0)
                N_sq(0, 1)
                N_mm(1)
                N_ln(1)
                APPLY(0, 1, k_in, False)
                for t in range(ntiles):
                    hooks = {}
                    both = (t >= 1 and t + 1 < ntiles)
                    if t >= 1:
                        hooks.setdefault(0, []).append(lambda t=t: N_sq(t - 1, 0, 0))
                        hooks.setdefault(1, []).append(lambda t=t: N_sq(t - 1, 0, 1))
                        hooks.setdefault(2, []).append(lambda: N_mm(0))
                    if t + 1 < ntiles:
                        hooks.setdefault(0, []).append(lambda t=t: P1(t + 1, 0))
                        hooks.setdefault(2, []).append(lambda t=t: P1(t + 1, 1))
                        hooks.setdefault(3, []).append(lambda t=t: N_sq(t + 1, 1, 0))
                        hooks.setdefault(4, []).append(lambda t=t: N_sq(t + 1, 1, 1))
                        hooks.setdefault(5, []).append(lambda: N_mm(1))
                    if both:
                        hooks.setdefault(6, []).append(N_ln2)
                    elif t >= 1:
                        hooks.setdefault(6, []).append(lambda: N_ln(0))
                    else:
                        hooks.setdefault(6, []).append(lambda: N_ln(1))
                    if t >= 1:
                        if which == 1:
                            hooks.setdefault(7, []).append(lambda t=t: APPLY(t - 1, 0, 1, True, 0))
                            hooks.setdefault(8, []).append(lambda t=t: APPLY(t - 1, 0, 1, True, 1))
                        else:
                            hooks.setdefault(7, []).append(lambda t=t: FIN_dve(t - 1))
                            hooks.setdefault(8, []).append(lambda t=t: FIN_tr(t - 1, (0, 1)))
                            hooks.setdefault(10, []).append(lambda t=t: FIN_tr(t - 1, (2, 3)))
                    if t + 1 < ntiles:
                        hooks.setdefault(9, []).append(lambda t=t: APPLY(t + 1, 1, k_in, False, 0))
                        hooks.setdefault(10, []).append(lambda t=t: APPLY(t + 1, 1, k_in, False, 1))
                    if t >= 1:
                        for jp_ in (1, 3, 5, 8, 10):
                            if pre_jobs:
                                hooks.setdefault(jp_, []).append(pre_jobs.pop(0))
                    ffn_tile(which, condof(t), k_in, xgT[t % 3], xmT[t % 2], H, B, t == 0 and not (which == 2 and PRECAST), hooks)
                t = ntiles - 1
                N_sq(t, 0)
                N_mm(0)
                N_ln(0)
                if which == 1:
                    APPLY(t, 0, 1, True)
                else:
                    FIN_dve(t)
                    FIN_tr(t, (0, 1, 2, 3))
                P.barrier()

        def run_ff1(groups):
            with contextlib.ExitStack() as ph:
                xg = sb("xg", [128, 8, 1024], F32, ph)
                B = ff_buffers(ph, first=True)
                for g in groups:
                    ff1_group(g, xg, B)
                P.barrier(ff_keys() + allps)

        def run_s2(groups):
            with contextlib.ExitStack() as ph:
                xg = sb("xg", [128, 8, 1024], F32, ph)
                for g in groups:
                    s2_group(g, xg, ph)
                P.barrier()

        stages = stop_after
        run_ff_tiles(1)
        if stages != "ff1":
            s1_group(0)
            s1_group(1)
            if stages != "s1":
                run_s2([0, 1, 2, 3, 4])
                if stages != "s2":
                    run_ff_tiles(2)

        P.build(nc, top, {"sp": 16, "act": 4, "pool": 12, "pe": 1, "dve": 1})
    nc._prog_stats = dict(n_ops=len(P.ops))
    return nc


def make_in_maps(inputs):
    f = lambda a: np.ascontiguousarray(np.asarray(a, dtype=np.float32))
    x_prompt, x_sample = f(inputs["x_prompt"]), f(inputs["x_sample"])
    c, c_ctx = f(inputs["c"]), f(inputs["c_ctx"])
    sf, sbw = f(inputs["state_rnn_fwd"]), f(inputs["state_rnn_bwd"])
    shared = {
        "ident": np.eye(128, dtype=np.float32),
        "gvb": np.ascontiguousarray(np.broadcast_to(f(inputs["gmlp_norm"])[0][None, :], (128, D))),
        "bs": f(inputs["b_s"])[0].reshape(1, D),
        "w_mod": f(inputs["w_mod"])[0],
        "ff1_gate": f(inputs["ff1_gate"])[0], "ff1_up": f(inputs["ff1_up"])[0], "ff1_down": f(inputs["ff1_down"])[0],
        "ff2_gate": f(inputs["ff2_gate"])[0], "ff2_up": f(inputs["ff2_up"])[0], "ff2_down": f(inputs["ff2_down"])[0],
        "w_in": f(inputs["w_in"])[0], "w_r": f(inputs["w_r"])[0], "w_i": f(inputs["w_i"])[0], "w_s": f(inputs["w_s"])[0],
        "w_br": f(inputs["w_br"])[0], "w_bg": f(inputs["w_bg"])[0], "w_out": f(inputs["w_out"])[0],
    }
    base = np.zeros((256, 128), np.float32)
    base[R_C1:R_C1 + 8] = c_ctx.reshape(8, 128)
    base[R_BMOD:R_BMOD + 72] = f(inputs["b_mod"])[0].reshape(72, 128)
    base[R_N1:R_N1 + 8] = f(inputs["norm1"])[0].reshape(8, 128)
    base[R_N2:R_N2 + 8] = f(inputs["norm2"])[0].reshape(8, 128)
    base[R_N3:R_N3 + 8] = f(inputs["norm3"])[0].reshape(8, 128)
    base[R_NF:R_NF + 8] = f(inputs["norm_f"]).reshape(8, 128)
    base[R_CW:R_CW + 32] = f(inputs["conv_w"])[0].reshape(32, 128)
    base[R_CB:R_CB + 8] = f(inputs["conv_b"])[0].reshape(8, 128)
    base[R_BR:R_BR + 16] = f(inputs["b_r"])[0].reshape(16, 128)
    base[R_BI:R_BI + 16] = f(inputs["b_i"])[0].reshape(16, 128)
    base[R_LAM:R_LAM + 16] = f(inputs["lam"])[0].reshape(16, 128)
    maps = []
    for b in range(8):
        v = base.copy()
        v[R_C0:R_C0 + 8] = c[b].reshape(8, 128)
        v[R_SF:R_SF + 8] = sf[b, 0].reshape(8, 128)
        v[R_SB:R_SB + 8] = sbw[b, 0].reshape(8, 128)
        m = dict(shared)
        m["xs"] = x_sample[b]
        m["xp"] = x_prompt[4 * b:4 * b + 4].reshape(1024, D)
        m["vecs"] = v
        maps.append(m)
    return maps


def kernel(**inputs):
    maps = make_in_maps(inputs)
    nc = build_nc()
    res = run_bass_kernel_spmd(nc, maps, core_ids=list(range(8)))
    y_prompt = np.zeros((32, 256, D), np.float32)
    y_sample = np.zeros((8, 4096, D), np.float32)
    nsf = np.zeros((32, 1, D), np.float32)
    nsb = np.zeros((32, 1, D), np.float32)
    for b in range(8):
        r = res.results[b]
        y_sample[b] = np.asarray(r["ys"], dtype=np.float32)
        y_prompt[4 * b:4 * b + 4] = np.asarray(r["yp"], dtype=np.float32).reshape(4, 256, D)
        nsf[4 * b:4 * b + 4, 0] = np.asarray(r["stf"], dtype=np.float32)
        nsb[4 * b:4 * b + 4, 0] = np.asarray(r["stb"], dtype=np.float32)
    return (y_prompt, y_sample, nsf, nsb)
```

```python
import contextlib
import math
import numpy as np
import concourse.bass as bass
import concourse.mybir as mybir
from concourse.bass_utils import run_bass_kernel_spmd

F32 = mybir.dt.float32
BF16 = mybir.dt.bfloat16
I32 = mybir.dt.int32
AF = mybir.ActivationFunctionType
ALU = mybir.AluOpType

D = 1024
DFF = 2816
NJ = DFF // 128
NTOK = 5120
S1_ORDER = 1
PRECAST = True
HZ_ALL = True
EPS = 1e-6

R_C0, R_C1, R_BMOD, R_N1, R_N2, R_N3, R_NF = 0, 8, 16, 88, 96, 104, 112
R_CW, R_CB, R_BR, R_BI, R_LAM, R_SF, R_SB = 128, 160, 168, 184, 200, 216, 224


class Prog:
    ENG = ("pe", "act", "dve", "pool", "sp")

    def __init__(self):
        self.ops = []
        self.last_w = {}
        self.readers = {}
        self.fence = None
        self.last_eng = {}
        self.pend_dma = []
        self.hz = HZ_ALL

    @contextlib.contextmanager
    def hazard(self):
        old = self.hz
        self.hz = True
        try:
            yield
        finally:
            self.hz = old

    def add(self, eng, emit, reads=(), writes=(), dma=False):
        i = len(self.ops)
        deps = set()
        lw = self.last_w
        rd = self.readers
        for k in reads:
            j = lw.get(k)
            if j is not None:
                deps.add(j)
        for k in writes:
            j = lw.get(k)
            if j is not None:
                deps.add(j)
            r = rd.get(k)
            if r:
                deps.update(r[0].values())
                deps.update(r[1])
        deps.discard(i)
        if self.fence is not None:
            deps.add(self.fence)
        if dma:
            self.pend_dma.append(i)
        else:
            self.last_eng[eng] = i
        for k in reads:
            r = rd.get(k)
            if r is None:
                r = rd[k] = ({}, [])
            if dma:
                r[1].append(i)
            else:
                r[0][eng] = i
        for k in writes:
            lw[k] = i
            rd[k] = ({}, [])
        self.ops.append(dict(eng=eng, emit=emit, dma=dma, deps=deps, hz=self.hz))

    def barrier(self, keys=None):
        i = len(self.ops)
        deps = set(self.last_eng.values()) | set(self.pend_dma)
        if self.fence is not None:
            deps.add(self.fence)
        self.ops.append(dict(eng="sp", emit=lambda e: e.nop(), dma=False, deps=deps, hz=False))
        self.fence = i
        self.last_eng = {"sp": i}
        self.pend_dma = []
        self.last_w = {}
        self.readers = {}

    def build(self, nc, stack, dma_ring):
        ops = self.ops
        n_dma = {e: 0 for e in self.ENG}
        dma_ops = {e: [] for e in self.ENG}
        for i, op in enumerate(ops):
            if op["dma"]:
                e = op["eng"]
                n = n_dma[e]
                K = dma_ring[e]
                op["dma_n"] = n
                if n >= K:
                    op["deps"].add(dma_ops[e][n - K])
                dma_ops[e].append(i)
                n_dma[e] += 1
        need = [False] * len(ops)
        for i, op in enumerate(ops):
            nd = set()
            for d in op["deps"]:
                p = ops[d]
                if (not p["dma"]) and (not op["dma"]) and p["eng"] == op["eng"] and not (op["hz"] and op["eng"] != "pe"):
                    continue
                nd.add(d)
                need[d] = True
            op["deps"] = nd
        esem = {e: stack.enter_context(nc.semaphore("s_" + e)) for e in self.ENG}
        dsem = {e: [stack.enter_context(nc.semaphore("d_%s%d" % (e, j))) for j in range(dma_ring[e])]
                for e in self.ENG if n_dma[e] > 0}
        cnt = {e: 0 for e in self.ENG}
        for i, op in enumerate(ops):
            if op["dma"]:
                e = op["eng"]
                n = op["dma_n"]
                K = dma_ring[e]
                op["sig"] = (dsem[e][n % K], 16 * (n // K + 1), 16)
            elif need[i]:
                e = op["eng"]
                cnt[e] += 1
                op["sig"] = (esem[e], cnt[e], 1)
            else:
                op["sig"] = None
        block = stack.enter_context(nc.Block())

        def run(engname, eng):
            waited = {}
            for i, op in enumerate(ops):
                if op["eng"] != engname:
                    continue
                for d in sorted(op["deps"]):
                    sem, val, _ = ops[d]["sig"]
                    key = id(sem)
                    if waited.get(key, 0) >= val:
                        continue
                    waited[key] = val
                    eng.wait_ge(sem, val)
                ins = op["emit"](eng)
                if op["sig"] is not None:
                    sem, val, inc = op["sig"]
                    ins.then_inc(sem, inc)
            if engname in dsem:
                n = n_dma[engname]
                K = dma_ring[engname]
                for j in range(min(n, K)):
                    last_n = ((n - 1 - j) // K) * K + j
                    val = 16 * (last_n // K + 1)
                    if waited.get(id(dsem[engname][j]), 0) < val:
                        eng.wait_ge(dsem[engname][j], val)

        @block.tensor
        def _(e):
            run("pe", e)

        @block.scalar
        def _(e):
            run("act", e)

        @block.vector
        def _(e):
            run("dve", e)

        @block.gpsimd
        def _(e):
            run("pool", e)

        @block.sync
        def _(e):
            run("sp", e)


def build_nc(debug=False, stop_after=None):
    nc = bass.Bass("TRN2", target_bir_lowering=False)
    P = Prog()

    def din(name, shape):
        return nc.dram_tensor(name, list(shape), F32, kind="ExternalInput").ap()

    def dout(name, shape, dt=F32):
        return nc.dram_tensor(name, list(shape), dt, kind="ExternalOutput").ap()

    xs = din("xs", [4096, D])
    xp = din("xp", [1024, D])
    vecs = din("vecs", [256, 128])
    ident_d = din("ident", [128, 128])
    gvb_d = din("gvb", [128, D])
    bs_d = din("bs", [1, D])
    w_mod = din("w_mod", [D, 9 * D])
    ffw = {1: (din("ff1_gate", [D, DFF]), din("ff1_up", [D, DFF]), din("ff1_down", [DFF, D])),
           2: (din("ff2_gate", [D, DFF]), din("ff2_up", [D, DFF]), din("ff2_down", [DFF, D]))}
    w_in = din("w_in", [D, 6 * D])
    w_r = din("w_r", [2, 16, 64, 64])
    w_i = din("w_i", [2, 16, 64, 64])
    w_s = din("w_s", [8, 128, 128])
    w_br = din("w_br", [D, D])
    w_bg = din("w_bg", [D, D])
    w_out = din("w_out", [D, D])
    ys = dout("ys", [4096, D])
    yp = dout("yp", [1024, D])
    stf_d = dout("stf", [4, D])
    stb_d = dout("stb", [4, D])
    skind = "ExternalOutput" if debug else "Internal"
    XA = nc.dram_tensor("XA", [8, 128, NTOK], F32, kind=skind).ap()
    HM = nc.dram_tensor("HM", [8, 128, NTOK], BF16, kind=skind).ap()
    YR = nc.dram_tensor("YR", [8, 128, NTOK], BF16, kind=skind).ap()
    if debug:
        DBG = nc.dram_tensor("DBG", [128, 2048], F32, kind="ExternalOutput").ap()
    GUS = {w: nc.dram_tensor("GUS%d" % w, [NJ // 2, 128, 2 * 8 * 256], BF16).ap() for w in (1, 2)}
    WDS = {w: nc.dram_tensor("WDS%d" % w, [8, 128, NJ * 128], BF16).ap() for w in (1, 2)}
    WXS = nc.dram_tensor("WXS", [8, 128, 2 * 8 * 128], BF16).ap()
    WVS = nc.dram_tensor("WVS", [128, 8 * D], BF16).ap()
    WUS = nc.dram_tensor("WUS", [8, 128, 8 * 128], BF16).ap()
    WM4S = nc.dram_tensor("WM4S", [8, 128, 4 * 8 * 128], BF16).ap()
    WOS = nc.dram_tensor("WOS", [8, 128, 8 * 128], BF16).ap()
    WSTS = nc.dram_tensor("WSTS", [128, 8 * 128], BF16).ap()
    BSHS = nc.dram_tensor("BSHS", [2, 8 * 128], BF16).ap()

    def xrows(t0, n):
        return xs[t0:t0 + n, :] if t0 < 4096 else xp[t0 - 4096:t0 - 4096 + n, :]

    def yrows(t0, n):
        return ys[t0:t0 + n, :] if t0 < 4096 else yp[t0 - 4096:t0 - 4096 + n, :]

    def dma(eng, out, in_, r, w):
        P.add(eng, lambda e: e.dma_start(out=out, in_=in_), r, w, dma=True)

    def wload(first, parts, flat, scr, keys, skey):
        if first:
            for (dst, src, k) in parts:
                dma("pool", dst, src, (), [k])
            dma("pool", scr, flat, keys, [skey])
        else:
            dma("sp", flat, scr, [skey], keys)

    def precast_job(stg, nelem, parts, scr, skey):
        for (dst, src) in parts:
            dma("pool", dst, src, (), [("STG", id(stg))])
        src_v = stg[:, 0:nelem]
        if len(scr.shape) == 3:
            src_v = src_v.rearrange("p (a b) -> p a b", a=scr.shape[1])
        dma("pool", scr, src_v, [("STG", id(stg))], [skey])

    def blk(src):
        return src.rearrange("(kc p) n -> p kc n", p=128)

    def mixer_precast_jobs(STG):
        jobs = []
        cnt = [0]

        def nxt():
            st = STG[cnt[0] % len(STG)]
            cnt[0] += 1
            return st
        for c in range(8):
            def j(c=c):
                st = nxt()
                v = st[:, 0:2048].rearrange("p (a k n) -> p a k n", a=2, k=8)
                precast_job(st, 2048, [(v[:, 0], blk(w_in[:, c * 128:(c + 1) * 128])),
                                       (v[:, 1], blk(w_in[:, D + c * 128:D + (c + 1) * 128]))], WXS[c], ("WXS", c))
            jobs.append(j)
        for h in range(2):
            def j(h=h):
                st = nxt()
                v = st[:, :].rearrange("p (k n) -> p k n", k=8)
                precast_job(st, 4096, [(v, blk(w_in[:, 3 * D + h * 512:3 * D + (h + 1) * 512]))],
                            WVS.rearrange("p (k n) -> p k n", k=8)[:, :, h * 512:(h + 1) * 512], "WVS")
            jobs.append(j)
        for g0 in range(0, 8, 4):
            def j(g0=g0):
                st = nxt()
                v = st[:, :].rearrange("p (g k n) -> p g k n", g=4, k=8)
                precast_job(st, 4096, [(v[:, i], blk(w_in[:, 2 * D + (g0 + i) * 128:2 * D + (g0 + i + 1) * 128])) for i in range(4)],
                            WUS[g0:g0 + 4].rearrange("g p e -> p g e"), ("WUS", g0))
            jobs.append(j)
        for f in range(8):
            def j(f=f):
                st = nxt()
                v = st[:, :].rearrange("p (a k n) -> p a k n", a=4, k=8)
                srcs = (w_in[:, 4 * D + f * 128:4 * D + (f + 1) * 128], w_in[:, 5 * D + f * 128:5 * D + (f + 1) * 128],
                        w_br[:, f * 128:(f + 1) * 128], w_bg[:, f * 128:(f + 1) * 128])
                precast_job(st, 4096, [(v[:, i], blk(srcs[i])) for i in range(4)], WM4S[f], ("WM4S", f))
            jobs.append(j)
        for m0 in range(0, 8, 4):
            def j(m0=m0):
                st = nxt()
                v = st[:, :].rearrange("p (g k n) -> p g k n", g=4, k=8)
                precast_job(st, 4096, [(v[:, i], blk(w_out[:, (m0 + i) * 128:(m0 + i + 1) * 128])) for i in range(4)],
                            WOS[m0:m0 + 4].rearrange("g p e -> p g e"), ("WOS", m0))
            jobs.append(j)
        return jobs

    def ff_precast_jobs(which, STG):
        wg, wu, wd = ffw[which]
        jobs = []
        cnt = [0]

        def nxt():
            st = STG[cnt[0] % len(STG)]
            cnt[0] += 1
            return st
        for jp in range(NJ // 2):
            def j(jp=jp):
                st = nxt()
                v = st[:, :].rearrange("p (a k n) -> p a k n", a=2, k=8)
                precast_job(st, 4096, [(v[:, 0], blk(wg[:, jp * 256:(jp + 1) * 256])), (v[:, 1], blk(wu[:, jp * 256:(jp + 1) * 256]))],
                            GUS[which][jp], ("GUS", which, jp))
            jobs.append(j)
        for m in range(8):
            def j(m=m):
                st = nxt()
                v = st[:, 0:NJ * 128].rearrange("p (k n) -> p k n", k=NJ)
                precast_job(st, NJ * 128, [(v, blk(wd[:, m * 128:(m + 1) * 128]))], WDS[which][m], ("WDS", which, m))
            jobs.append(j)
        return jobs

    def mm(ps_ap, pairs, r, w, first_start=True):
        def emit(e):
            n = len(pairs)
            ins = None
            for i, (l, rh) in enumerate(pairs):
                ins = e.matmul(ps_ap, lhsT=l, rhs=rh, start=(first_start and i == 0), stop=(i == n - 1))
            return ins
        P.add("pe", emit, r, w)

    def act(out, in_, func, r, w, bias=None, scale=None, accum=None):
        def emit(e):
            kw = {}
            if bias is not None:
                kw["bias"] = bias
            if scale is not None:
                kw["scale"] = scale
            if accum is not None:
                kw["accum_out"] = accum
            return e.activation(out=out, in_=in_, func=func, **kw)
        P.add("act", emit, r, w)

    def tt(eng, out, in0, in1, op, r, w):
        P.add(eng, lambda e: e.tensor_tensor(out=out, in0=in0, in1=in1, op=op), r, w)

    def ts(eng, out, in0, s1, s2, op0, op1, r, w):
        if op1 is None:
            P.add(eng, lambda e: e.tensor_scalar(out=out, in0=in0, scalar1=s1, scalar2=None, op0=op0), r, w)
        else:
            P.add(eng, lambda e: e.tensor_scalar(out=out, in0=in0, scalar1=s1, scalar2=s2, op0=op0, op1=op1), r, w)

    def stt(out, in0, scalar, in1, op0, op1, r, w):
        P.add("dve", lambda e: e.scalar_tensor_tensor(out=out, in0=in0, scalar=scalar, in1=in1, op0=op0, op1=op1), r, w)

    def cp(eng, out, in_, r, w):
        P.add(eng, lambda e: e.tensor_copy(out=out, in_=in_), r, w)

    def memset(eng, ap, val, w):
        P.add(eng, lambda e: e.memset(ap, val), (), w)

    with contextlib.ExitStack() as top:
        uid = [0]

        def sb(name, shape, dt, st=None):
            uid[0] += 1
            return (st or top).enter_context(nc.sbuf_tensor("%s_%d" % (name, uid[0]), list(shape), dt))

        PS = top.enter_context(nc.psum_tensor("PS", [128, 8, 512], F32))

        def bank(b):
            return PS[:, b, :]

        def kb(b):
            return ("ps", b)

        IDN = sb("IDN", [128, 128], F32)
        VT = sb("VT", [128, 256], F32)
        MOD = sb("MOD", [128, 2, 72], F32)
        NS = sb("NS", [128, 2, 3, 8], F32)
        GH = sb("GH", [128, 2, 3, 8], F32)
        LL = sb("LL", [128, 16], F32)
        L4 = sb("L4", [128, 16], F32)
        L8 = sb("L8", [128, 16], F32)
        HBR = sb("HBR", [128, 16], F32)
        HBI = sb("HBI", [128, 16], F32)
        TAB = sb("TAB", [128, 4, 64], F32)
        ONES = sb("ONES", [128, 128], BF16)
        NEGH = sb("NEGH", [128, 1], F32)
        BD = sb("BD", [128, 2, 2, 8, 128], BF16)
        STF = sb("STF", [128, 32], F32)
        STB = sb("STB", [128, 32], F32)
        SSV = sb("SSV", [128, 4], F32)

        def sh_ap(cond, k, m):
            return MOD[:, cond, (3 * k) * 8 + m:(3 * k) * 8 + m + 1]

        def ns_ap(cond, k, m):
            return NS[:, cond, k, m:m + 1]

        def gh_ap(cond, k, m):
            return GH[:, cond, k, m:m + 1]

        with contextlib.ExitStack() as ph, P.hazard():
            V0 = sb("V0", [128, 128], F32, ph)
            V1 = sb("V1", [128, 128], F32, ph)
            SB2 = sb("SB2", [128, 8, 2], BF16, ph)
            WM = [sb("WM%d" % i, [128, 8, D], BF16, ph) for i in range(2)]
            WSL = sb("WSL", [128, 8, 128], F32, ph)
            BS0 = sb("BS0", [1, D], F32, ph)
            BSHI = sb("BSHI", [1, D], BF16, ph)
            WST = sb("WSTset", [128, 8, 128], BF16, ph)
            BSLO = sb("BSLO", [1, D], BF16, ph)
            SIG = sb("SIG", [128, 16], F32, ph)
            QI = sb("QI", [128, 2], I32, ph)
            QF = sb("QF", [128, 2], F32, ph)
            FR = sb("FR", [128, 2], F32, ph)
            RI = sb("RI", [128, 64], I32, ph)
            RV = sb("RV", [128, 64], F32, ph)
            ANG = sb("ANG", [128, 4, 64], F32, ph)
            TQ = sb("TQ", [128, 4, 64], F32, ph)
            KI = sb("KI", [128, 4, 64], I32, ph)
            KF = sb("KF", [128, 4, 64], F32, ph)
            CM = sb("CM", [128, 4, 64], F32, ph)

            dma("sp", IDN[:], ident_d[:, :], (), ["IDN"])
            dma("sp", V0[:], vecs[0:128, :], (), ["V0"])
            dma("sp", V1[:], vecs[128:256, :], (), ["V1"])
            dma("sp", BS0[:], bs_d[:, :], (), ["BS0"])
            dma("sp", WSL[:], w_s.rearrange("g p q -> p g q"), (), ["WSL"])
            memset("pool", ONES[:], 1.0, ["ONES"])
            memset("pool", NEGH[:], -0.5, ["NEGH"])
            memset("pool", BD[:], 0.0, ["BD"])
            memset("dve", STF[:], 0.0, ["STF"])
            memset("dve", STB[:], 0.0, ["STB"])
            P.add("pe", lambda e: e.transpose(out=PS[:, 7, 0:128], in_=V0[:], identity=IDN[:]), ["V0", "IDN"], [kb(7)])
            P.add("pe", lambda e: e.transpose(out=PS[:, 7, 128:256], in_=V1[:], identity=IDN[:]), ["V1", "IDN"], [kb(7)])
            cp("dve", VT[:], PS[:, 7, 0:256], [kb(7)], ["VT"])
            for cond in range(2):
                act(SB2[:, :, cond], VT[:, 8 * cond:8 * cond + 8], AF.Silu, ["VT"], ["SB2"])
            act(SIG[:], VT[:, R_LAM:R_LAM + 16], AF.Sigmoid, ["VT"], ["SIG"])
            act(LL[:], SIG[:], AF.Ln, ["SIG"], ["LL"])
            ts("dve", L4[:], LL[:], 4.0, None, ALU.mult, None, ["LL"], ["L4"])
            ts("dve", L8[:], LL[:], 8.0, None, ALU.mult, None, ["LL"], ["L8"])
            ts("dve", HBR[:], VT[:, R_BR:R_BR + 16], 0.5, None, ALU.mult, None, ["VT"], ["HBR"])
            ts("dve", HBI[:], VT[:, R_BI:R_BI + 16], 0.5, None, ALU.mult, None, ["VT"], ["HBI"])
            for g in range(8):
                P.add("pe", lambda e, g=g: e.transpose(out=PS[:, 5, (g % 4) * 128:(g % 4) * 128 + 128], in_=WSL[:, g, :],
                                                       identity=IDN[:]), ["WSL", "IDN"], [kb(5)])
                cp("dve", WST[:, g, :], PS[:, 5, (g % 4) * 128:(g % 4) * 128 + 128], [kb(5)], ["WST"])
            cp("dve", BSHI[:], BS0[:], ["BS0"], ["BSHI"])
            tt("dve", BSLO[:], BS0[:], BSHI[:], ALU.subtract, ["BS0", "BSHI"], ["BSLO"])
            dma("sp", BSHS[0:1, :], BSHI[0:1, :], ["BSHI"], ["BSHS"])
            dma("sp", BSHS[1:2, :], BSLO[0:1, :], ["BSLO"], ["BSHS"])
            dma("sp", WSTS[:, :], WST[:].rearrange("p g q -> p (g q)"), ["WST"], ["WSTS"])
            P.add("pool", lambda e: e.iota(QI[:], [[128, 2]], base=0, channel_multiplier=1), (), ["QI"])
            P.add("pool", lambda e: e.iota(RI[:], [[1, 64]], base=0, channel_multiplier=0), (), ["RI"])
            cp("dve", QF[:], QI[:], ["QI"], ["QF"])
            cp("dve", RV[:], RI[:], ["RI"], ["RV"])
            act(FR[:], QF[:], AF.Exp, ["QF"], ["FR"], scale=-math.log(10000.0) / 256.0)
            for q2 in range(2):
                ts("dve", ANG[:, q2, :], RV[:], FR[:, q2:q2 + 1], None, ALU.mult, None, ["RV", "FR"], ["ANG"])
            ts("dve", ANG[:, 2:4, :], ANG[:, 0:2, :], math.pi / 2, None, ALU.add, None, ["ANG"], ["ANG"])
            ts("dve", TQ[:], ANG[:], 1.0 / (2 * math.pi), 0.5, ALU.mult, ALU.add, ["ANG"], ["TQ"])
            cp("dve", KI[:], TQ[:], ["TQ"], ["KI"])
            cp("dve", KF[:], KI[:], ["KI"], ["KF"])
            tt("dve", CM[:], KF[:], TQ[:], ALU.is_gt, ["KF", "TQ"], ["CM"])
            tt("dve", KF[:], KF[:], CM[:], ALU.subtract, ["KF", "CM"], ["KF"])
            stt(ANG[:], KF[:], -2 * math.pi, ANG[:], ALU.mult, ALU.add, ["KF", "ANG"], ["ANG"])
            ts("dve", ANG[:], ANG[:], 3.14159, -3.14159, ALU.min, ALU.max, ["ANG"], ["ANG"])
            act(TAB[:], ANG[:], AF.Sin, ["ANG"], ["TAB"])
            WF = [sb("WF", [128, 8, D], F32, ph) for _ in range(2)]
            WMo = [sb("WMo", [128, 8, D], BF16, ph) for _ in range(2)]
            hwn = 0
            for i in (0, 1, 2, 3, 5, 6, 7, 4, 8):
                src_i = w_mod[:, i * D:(i + 1) * D].rearrange("(kc p) n -> p kc n", p=128)
                if i == 4:
                    slot = 0
                    wt = WM[slot]
                    wkey = ("WM", slot)
                    dma("pool", wt[:], src_i, (), [wkey])
                else:
                    slot = hwn % 2
                    hwn += 1
                    wt = WMo[slot]
                    wkey = ("WMo", slot)
                    dma("sp", WF[slot][:], src_i, (), [("WF", slot)])
                    cp("dve", wt[:, 0:4, :], WF[slot][:, 0:4, :], [("WF", slot)], [wkey])
                    act(wt[:, 4:8, :], WF[slot][:, 4:8, :], AF.Copy, [("WF", slot)], [wkey])

                def emit_mod(e, i=i, wt=wt):
                    ins = None
                    for m in range(8):
                        for kc in range(8):
                            ins = e.matmul(PS[:, 6, (i * 8 + m) * 2:(i * 8 + m) * 2 + 2],
                                           lhsT=wt[:, kc, m * 128:(m + 1) * 128], rhs=SB2[:, kc, :],
                                           start=(kc == 0), stop=(kc == 7))
                    return ins
                P.add("pe", emit_mod, [wkey, "SB2"], [kb(6)])
            for d in range(2):
                for kind, wsrc in enumerate((w_r, w_i)):
                    v = wsrc[d].rearrange("(c two) i j -> two i c j", two=2)
                    for h in range(2):
                        dma("pool", BD[64 * h:64 * h + 64, d, kind, :, 64 * h:64 * h + 64], v[h], ["BD"], ["BD"])
            psmod = PS[:, 6, 0:144].rearrange("p (r c) -> p r c", c=2)
            for cond in range(2):
                tt("dve", MOD[:, cond, :], psmod[:, :, cond], VT[:, R_BMOD:R_BMOD + 72], ALU.add, [kb(6), "VT"], ["MOD"])
            for cond in range(2):
                for k in range(3):
                    stt(NS[:, cond, k, :], MOD[:, cond, (3 * k + 1) * 8:(3 * k + 1) * 8 + 8], 1.0,
                        VT[:, R_N1 + 8 * k:R_N1 + 8 * k + 8], ALU.add, ALU.mult, ["MOD", "VT"], ["NS"])
                    ts("dve", GH[:, cond, k, :], MOD[:, cond, (3 * k + 2) * 8:(3 * k + 2) * 8 + 8], 0.5, None,
                       ALU.mult, None, ["MOD"], ["GH"])
            if debug:
                dma("sp", DBG[:, 0:256], VT[:], ["VT"], ["DBG"])
                dma("sp", DBG[:, 256:400], MOD[:].rearrange("p a b -> p (a b)"), ["MOD"], ["DBG"])
                dma("sp", DBG[:, 400:656], TAB[:].rearrange("p a b -> p (a b)"), ["TAB"], ["DBG"])
                dma("sp", DBG[:, 656:672], LL[:], ["LL"], ["DBG"])
            P.barrier(["IDN", "VT", "MOD", "NS", "GH", "LL", "L4", "L8", "HBR", "HBI", "TAB", "ONES", "NEGH", "GVB",
                       "BSH", "WST", "BD", "STF", "STB"] + [kb(b) for b in range(8)])

        consts = ["IDN", "VT", "MOD", "NS", "GH", "L4", "L8", "HBR", "HBI", "TAB", "ONES", "NEGH", "GVB", "BSH", "WST", "BD"]

        def norm_stats(xg, B):
            SQ, LNT, RS = B["SQ"], B["MS"], B["RS"]
            for tti in range(2):
                cols = slice(tti * 512, (tti + 1) * 512)
                xk = [("xg", m, tti) for m in range(8)]
                act(SQ[:], xg[:, :, cols], AF.Square, xk, ["SQ"])
                mm(bank(6 + tti), [(ONES[:], SQ[:, m, :]) for m in range(8)], ["SQ", "ONES"], [kb(6 + tti)])
            act(LNT[:], PS[:, 6:8, :], AF.Ln, [kb(6), kb(7), "EPSC"], ["MS"], bias=EPSC[:, 0:1], scale=1.0 / D)
            act(RS[:], LNT[:], AF.Exp, ["MS"], ["RS"], scale=-0.5)

        def norm_apply(xg, tti, cond, k, B, xmod_out=None, yf_out=None):
            cols = slice(tti * 512, (tti + 1) * 512)
            RS, TMP = B["RS"], B["TMP"]
            for m in range(8):
                if xmod_out is not None:
                    sl = m % 2
                    stt(TMP[sl][:], xg[:, m, cols], ns_ap(cond, k, m), RS[:, tti, :], ALU.mult, ALU.mult,
                        [("xg", m, tti), "RS", "NS"], [("TMP", sl)])
                    act(xmod_out[:, m, cols], TMP[sl][:], AF.Identity, [("TMP", sl), "MOD"], [("xmod", m, tti)],
                        bias=sh_ap(cond, k, m))
                else:
                    stt(yf_out[:, m, :], xg[:, m, cols], VT[:, R_NF + m:R_NF + m + 1], RS[:, tti, :], ALU.mult, ALU.mult,
                        [("xg", m, tti), "RS", "VT"], [("YF", m)])

        def ffn_group(which, cond, k_gate, xg, B, first=True):
            wg, wu, wd = ffw[which]
            xmod, H, GU, WD, SG = B["xmod"], B["H"], B["GU"], B["WD"], B["SG"]
            cnt = 0
            for jp in range(NJ // 2):
                slot = jp % 3
                wload(first, [(GU[slot][:, 0, :, :], wg[:, jp * 256:(jp + 1) * 256].rearrange("(kc p) n -> p kc n", p=128), ("GU", slot, 0)),
                              (GU[slot][:, 1, :, :], wu[:, jp * 256:(jp + 1) * 256].rearrange("(kc p) n -> p kc n", p=128), ("GU", slot, 1))],
                      GU[slot][:].rearrange("p a k n -> p (a k n)"), GUS[which][jp], [("GU", slot, 0), ("GU", slot, 1)],
                      ("GUS", which, jp))
                for jj in range(2):
                    j = 2 * jp + jj
                    for tti in range(2):
                        cols = slice(tti * 512, (tti + 1) * 512)
                        bg = cnt % 2
                        bu = 2 + cnt % 2
                        cnt += 1
                        xk = [("xmod", m, tti) for m in range(8)]
                        mm(bank(bg), [(GU[slot][:, 0, kc, jj * 128:(jj + 1) * 128], xmod[:, kc, cols]) for kc in range(8)],
                           [("GU", slot, 0)] + xk, [kb(bg)])
                        mm(bank(bu), [(GU[slot][:, 1, kc, jj * 128:(jj + 1) * 128], xmod[:, kc, cols]) for kc in range(8)],
                           [("GU", slot, 1)] + xk, [kb(bu)])
                        sl = cnt % 2
                        act(SG[sl][:], bank(bg), AF.Silu, [kb(bg)], [("SG", sl)])
                        tt("dve", H[:, j, cols], SG[sl][:], bank(bu), ALU.mult, [("SG", sl), kb(bu)], [("H", j, tti)])
            cnt = 0
            for m in range(8):
                slot = m % 2
                wload(first, [(WD[slot][:], wd[:, m * 128:(m + 1) * 128].rearrange("(kc p) n -> p kc n", p=128), ("WD", slot))],
                      WD[slot][:].rearrange("p k n -> p (k n)"), WDS[which][m], [("WD", slot)], ("WDS", which, m))
                for tti in range(2):
                    cols = slice(tti * 512, (tti + 1) * 512)
                    b = 4 + cnt % 2
                    cnt += 1
                    mm(bank(b), [(WD[slot][:, j, :], H[:, j, cols]) for j in range(NJ)],
                       [("WD", slot)] + [("H", j, tti) for j in range(NJ)], [kb(b)])
                    stt(xg[:, m, cols], bank(b), gh_ap(cond, k_gate, m), xg[:, m, cols], ALU.mult, ALU.add,
                        [kb(b), "GH", ("xg", m, tti)], [("xg", m, tti)])

        def ff_buffers(ph, first):
            B = {}
            B["xmod"] = sb("xmod", [128, 8, 1024], BF16, ph)
            B["H"] = sb("H", [128, NJ, 1024], BF16, ph)
            B["GU"] = [sb("GU%d" % i, [128, 2, 8, 256], BF16, ph) for i in range(3)]
            B["WD"] = [sb("WD%d" % i, [128, NJ, 128], BF16, ph) for i in range(2)]
            B["SG"] = [sb("SG%d" % i, [128, 512], F32, ph) for i in range(2)]
            B["SQ"] = sb("SQ", [128, 8, 512], BF16, ph)
            B["MS"] = sb("MS", [128, 2, 512], F32, ph)
            B["RS"] = sb("RS", [128, 2, 512], F32, ph)
            B["TMP"] = [sb("TMP%d" % i, [128, 512], F32, ph) for i in range(2)]
            if first:
                B["XT"] = [sb("XT%d" % i, [128, D], F32, ph) for i in range(4)]
            else:
                B["YF"] = sb("YF", [128, 8, 512], F32, ph)
                B["YT"] = [sb("YT%d" % i, [128, D], F32, ph) for i in range(2)]
            return B

        def ff_keys():
            ks = [("xmod", m, t) for m in range(8) for t in range(2)] + [("H", j, t) for j in range(NJ) for t in range(2)]
            ks += [("GU", s, i) for s in range(3) for i in range(2)] + [("WD", s) for s in range(2)]
            ks += [("SG", 0), ("SG", 1), "SQ", "MS", "RS", ("TMP", 0), ("TMP", 1)]
            ks += [("XT", i) for i in range(4)] + [("YF", m) for m in range(8)] + [("YT", 0), ("YT", 1)]
            ks += [("xg", m, t) for m in range(8) for t in range(2)]
            return ks

        allps = [kb(b) for b in range(8)]

        def ff1_group(g, xg, B):
            cond = 0 if g < 4 else 1
            XT = B["XT"]
            for tti in range(2):
                T = 2 * g + tti
                t0 = T * 512
                cols = slice(tti * 512, (tti + 1) * 512)
                for s in range(4):
                    dma("sp", XT[s][:], xrows(t0 + s * 128, 128), (), [("XT", s)])
                for m in range(8):
                    b = 4 + m % 4

                    def emit_tr(e, m=m, b=b):
                        ins = None
                        for s in range(4):
                            ins = e.transpose(out=PS[:, b, s * 128:(s + 1) * 128], in_=XT[s][:, m * 128:(m + 1) * 128],
                                              identity=IDN[:])
                        return ins
                    P.add("pe", emit_tr, [("XT", s) for s in range(4)] + ["IDN"], [kb(b)])
                    if cond == 0:
                        pv = bank(b).rearrange("p (a b) -> p a b", b=64)
                        ov = xg[:, m, cols].rearrange("p (a b) -> p a b", b=64)
                        if m < 4:
                            tv = TAB[:, m, 8 * T:8 * T + 8].unsqueeze(2).broadcast_to([128, 8, 64])
                        else:
                            tv = TAB[:, m - 4, :].unsqueeze(1).broadcast_to([128, 8, 64])
                        tt("dve", ov, pv, tv, ALU.add, [kb(b), "TAB"], [("xg", m, tti)])
                    else:
                        cp("dve", xg[:, m, cols], bank(b), [kb(b)], [("xg", m, tti)])
            norm_stats(xg, B)
            for tti in range(2):
                norm_apply(xg, tti, cond, 0, B, xmod_out=B["xmod"])
            ffn_group(1, cond, 0, xg, B, first=(g == 0))
            norm_stats(xg, B)
            for tti in range(2):
                T = 2 * g + tti
                t0 = T * 512
                cols = slice(tti * 512, (tti + 1) * 512)
                norm_apply(xg, tti, cond, 1, B, xmod_out=B["xmod"])
                dma("sp", HM[:, :, t0:t0 + 512].rearrange("m p t -> p m t"), B["xmod"][:, :, cols],
                    [("xmod", m, tti) for m in range(8)], [("HM", T)])
                dma("sp", XA[:, :, t0:t0 + 512].rearrange("m p t -> p m t"), xg[:, :, cols],
                    [("xg", m, tti) for m in range(8)], [("XA", T)])

        def s1_group(sg):
            if sg == 0:
                T0, nt, nseq, L, cond = 0, 8, 1, 4096, 0
            else:
                T0, nt, nseq, L, cond = 8, 2, 4, 256, 1
            Ts = nt * 512
            spt = 512 // L if L < 512 else 1
            with contextlib.ExitStack() as ph:
                HMr = [sb("HMr", [128, 8, 512], BF16, ph) for _ in range(3)]
                XR = sb("XR", [128, nseq * (L + 3)], F32, ph)
                XC = sb("XC", [128, nseq, L], F32, ph)
                XCB = sb("XCB", [128, Ts], BF16, ph)
                AB = [[sb("ABI", [128, nseq, L], F32, ph) for _ in range(3)] for _ in range(2)]
                GGf = [sb("GGf", [128, Ts], BF16, ph) for _ in range(2)]
                YRt = [sb("YRt", [128, 512], BF16, ph) for _ in range(4)]
                WX = [sb("WX", [128, 2, 8, 128], BF16, ph) for _ in range(2)]
                TH = [sb("TH", [128, 512], F32, ph) for _ in range(2)]
                XR3 = XR[:, :].rearrange("p (s l) -> p s l", l=L + 3)
                XCf = XC[:].rearrange("p s l -> p (s l)")
                flat = lambda t3: t3[:].rearrange("p s l -> p (s l)")
                tk = lambda name: [(name, t) for t in range(nt)]
                memset("dve", XR3[:, :, 0:2], 0.0, ["XRhalo"])
                memset("dve", XR3[:, :, L + 2:L + 3], 0.0, ["XRhalo"])
                cnt = [0, 0, 0]

                def load_wx(c_):
                    sl_ = c_ % 2
                    wload(sg == 0 and not PRECAST, [(WX[sl_][:, 0, :, :], w_in[:, c_ * 128:(c_ + 1) * 128].rearrange("(kc p) n -> p kc n", p=128), ("WX", sl_, 0)),
                                    (WX[sl_][:, 1, :, :], w_in[:, D + c_ * 128:D + (c_ + 1) * 128].rearrange("(kc p) n -> p kc n", p=128), ("WX", sl_, 1))],
                          WX[sl_][:].rearrange("p a k n -> p (a k n)"), WXS[c_], [("WX", sl_, 0), ("WX", sl_, 1)], ("WXS", c_))

                def prep_tile(c, t):
                    slot = c % 2
                    GGc = GGf[c % 2]
                    T = T0 + t
                    cols = slice(t * 512, (t + 1) * 512)
                    hs = cnt[1] % 3
                    cnt[1] += 1
                    dma("sp", HMr[hs][:], HM[:, :, T * 512:(T + 1) * 512].rearrange("m p t -> p m t"), [("HM", T)], [("HMr", hs)])
                    bx = cnt[0] % 2
                    bgr = 6 + cnt[0] % 2
                    cnt[0] += 1
                    mm(bank(bx), [(WX[slot][:, 0, kc, :], HMr[hs][:, kc, :]) for kc in range(8)],
                       [("WX", slot, 0), ("HMr", hs)], [kb(bx)])
                    mm(bank(bgr), [(WX[slot][:, 1, kc, :], HMr[hs][:, kc, :]) for kc in range(8)],
                       [("WX", slot, 1), ("HMr", hs)], [kb(bgr)])
                    if L >= 512:
                        ov = XR3[:, 0, 2 + t * 512:2 + (t + 1) * 512]
                        iv = bank(bx)
                    else:
                        ov = XR3[:, t * spt:(t + 1) * spt, 2:2 + L]
                        iv = bank(bx).rearrange("p (s l) -> p s l", l=L)
                    act(ov, iv, AF.Copy, [kb(bx)], [("XR", t)])
                    act(GGc[:, cols], bank(bgr), AF.Copy, [kb(bgr)], [("GGf", c % 2, t)])

                def stage_gelu(c):
                    GGc = GGf[c % 2]
                    gk = [("GGf", c % 2, t) for t in range(nt)]
                    act(GGc[:], GGc[:], AF.Gelu_apprx_tanh, gk, gk)

                def stage_V(c):
                    ts("dve", XC[:], XR3[:, :, 0:L], VT[:, R_CW + c:R_CW + c + 1], VT[:, R_CB + c:R_CB + c + 1],
                       ALU.mult, ALU.add, tk("XR") + ["XRhalo", "VT"], tk("XC"))
                    for k in range(1, 4):
                        stt(XC[:], XR3[:, :, k:k + L], VT[:, R_CW + 8 * k + c:R_CW + 8 * k + c + 1], XC[:],
                            ALU.mult, ALU.add, tk("XR") + tk("XC") + ["VT", "XRhalo"], tk("XC"))
                    act(XCB[:], XCf, AF.Copy, tk("XC"), tk("XCB"))

                def stage_G_act(c, d, prep_c=None):
                    dc = d * 8 + c
                    A_, B_, I_ = AB[d]
                    Af, Bf, If = flat(A_), flat(B_), flat(I_)
                    for t in range(nt):
                        cols = slice(t * 512, (t + 1) * 512)
                        br = 2 + cnt[2] % 2
                        bi = 4 + cnt[2] % 2
                        sl = cnt[2] % 2
                        cnt[2] += 1
                        mm(bank(br), [(BD[:, d, 0, c, :], XCB[:, cols])], ["BD", ("XCB", t)], [kb(br)])
                        mm(bank(bi), [(BD[:, d, 1, c, :], XCB[:, cols])], ["BD", ("XCB", t)], [kb(bi)])
                        act(TH[sl][:], bank(br), AF.Tanh, [kb(br), "HBR"], [("TH", sl)], bias=HBR[:, dc:dc + 1], scale=0.5)
                        act(Af[:, cols], TH[sl][:], AF.Exp, [("TH", sl), "L4"], [("A", d, t)],
                            bias=L4[:, dc:dc + 1], scale=L4[:, dc:dc + 1])
                        tt("dve", Bf[:, cols], Af[:, cols], Af[:, cols], ALU.mult, [("A", d, t)], [("B", d, t)])
                        act(If[:, cols], bank(bi), AF.Tanh, [kb(bi), "HBI"], [("I", d, t)], bias=HBI[:, dc:dc + 1], scale=0.5)
                        if prep_c is not None:
                            prep_tile(prep_c, t)
                    if prep_c is not None and prep_c + 1 < 8:
                        load_wx(prep_c + 1)

                def stage_sqrt(c, d):
                    Bf = flat(AB[d][1])
                    ka = lambda n: [(n, d, t) for t in range(nt)]
                    act(Bf, Bf, AF.Sqrt, ka("B") + ["QUART"], ka("B"), bias=QUART[:, 0:1], scale=-0.25)

                def stage_G_dve1(c, d):
                    A_, B_, I_ = AB[d]
                    If = flat(I_)
                    ka = lambda n: [(n, d, t) for t in range(nt)]
                    stt(If, If, 1.0, XCf, ALU.add, ALU.mult, ka("I") + tk("XC"), ka("I"))

                def stage_G_dve2(c, d):
                    A_, B_, I_ = AB[d]
                    Af, Bf, If = flat(A_), flat(B_), flat(I_)
                    ka = lambda n: [(n, d, t) for t in range(nt)]
                    tt("dve", Bf, Bf, If, ALU.mult, ka("B") + ka("I"), ka("B"))
                    with P.hazard():
                        for s_ in range(nseq):
                            if sg == 0:
                                r0 = R_SF if d == 0 else R_SB
                                init = VT[:, r0 + c:r0 + c + 1]
                            else:
                                init = 0.0
                            if d == 0:
                                P.add("dve", lambda e, s_=s_, init=init, A_=A_, B_=B_: e.tensor_tensor_scan(
                                    out=B_[:, s_, :], data0=A_[:, s_, :], data1=B_[:, s_, :], initial=init,
                                    op0=ALU.mult, op1=ALU.add), ka("A") + ka("B") + ["VT"], ka("B"))
                            else:
                                P.add("dve", lambda e, s_=s_, init=init, A_=A_, B_=B_: e.tensor_tensor_scan(
                                    out=B_[:, s_, ::-1], data0=A_[:, s_, ::-1], data1=B_[:, s_, ::-1], initial=init,
                                    op0=ALU.mult, op1=ALU.add), ka("A") + ka("B") + ["VT"], ka("B"))
                        if sg == 1:
                            if d == 0:
                                cp("dve", STF[:, :].rearrange("p (s c) -> p s c", c=8)[:, :, c], B_[:, :, L - 1], ka("B"), ["STF"])
                            else:
                                cp("dve", STB[:, :].rearrange("p (s c) -> p s c", c=8)[:, :, c], B_[:, :, 0], ka("B"), ["STB"])

                def stage_ADD(c):
                    B0f, B1f = flat(AB[0][1]), flat(AB[1][1])
                    with P.hazard():
                        tt("dve", B0f, B0f, B1f, ALU.add, [("B", 0, t) for t in range(nt)] + [("B", 1, t) for t in range(nt)],
                           [("B", 0, t) for t in range(nt)])

                def stage_Y(c):
                    B0f = flat(AB[0][1])
                    GGc = GGf[c % 2]
                    for t in range(nt):
                        cols = slice(t * 512, (t + 1) * 512)
                        ys_ = t % 4
                        tt("dve", YRt[ys_][:], GGc[:, cols], B0f[:, cols], ALU.mult, [("GGf", c % 2, t), ("B", 0, t)], [("YRt", ys_)])
                        t0 = (T0 + t) * 512
                        dma("pool", YR[c, :, t0:t0 + 512], YRt[ys_][:], [("YRt", ys_)], [("YR", sg, c, t)])

                pre2 = []
                if False and sg == 1 and PRECAST:
                    STG2 = [sb("STG2", [128, 4096], BF16, ph) for _ in range(3)]
                    pre2 = ff_precast_jobs(2, STG2)
                load_wx(0)
                for t in range(nt):
                    prep_tile(0, t)
                load_wx(1)
                stage_V(0)
                for c in range(8):
                    for _ in range(3):
                        if pre2:
                            pre2.pop(0)()
                    nxt = c + 1 if c + 1 < 8 else None
                    if sg == 0:
                        stage_G_act(c, 0)
                        stage_sqrt(c, 0)
                        stage_G_dve1(c, 0)
                        stage_G_dve2(c, 0)
                        stage_G_act(c, 1, nxt)
                        stage_sqrt(c, 1)
                    else:
                        stage_G_act(c, 0)
                        stage_G_act(c, 1, nxt)
                        stage_sqrt(c, 0)
                        stage_sqrt(c, 1)
                        stage_G_dve1(c, 0)
                        stage_G_dve2(c, 0)
                    stage_gelu(c)
                    stage_G_dve1(c, 1)
                    if nxt is not None:
                        stage_V(nxt)
                    stage_G_dve2(c, 1)
                    stage_ADD(c)
                    stage_Y(c)
                if sg == 1:
                    STT_ = sb("STT_", [32, 128], F32, ph)
                    for nm, src, dst in (("f", STF, stf_d), ("b", STB, stb_d)):
                        P.add("pe", lambda e, src=src: e.transpose(out=PS[0:32, 0, 0:128], in_=src[:, :], identity=IDN[:]),
                              ["STF", "STB", "IDN"], [kb(0)])
                        cp("dve", STT_[:], PS[0:32, 0, 0:128], [kb(0)], ["STT_"])
                        dma("pool", dst.rearrange("s (c p) -> (s c) p", p=128), STT_[:], ["STT_"], [("st", nm)])
                P.barrier()

        def s2_keys():
            ks = [("HMg", t) for t in range(2)] + [("YRg", t) for t in range(2)] + ["WV", ("WU", 0), ("WU", 1)]
            ks += [("GV", 0), ("GV", 1), "JUNK", "SSV", "RSV"] + [("VN", n) for n in range(4)]
            ks += [("GUt", 0), ("GUt", 1)] + [("YG", g, t) for g in range(8) for t in range(2)]
            ks += [("MG", f, t) for f in range(8) for t in range(2)] + [("WM4", s, i) for s in range(2) for i in range(4)]
            ks += [("TA", 0), ("TA", 1), ("TB", 0), ("TB", 1), ("M1", 0), ("M1", 1), ("M2", 0), ("M2", 1), ("WO", 0), ("WO", 1)]
            return ks

        S2B = {}

        def s2_group(g, xg, ph_outer):
            cond = 0 if g < 4 else 1
            firstg = (len(S2B) == 0)
            castg = firstg and not PRECAST

            def sbm(name, shape, dt):
                if name not in S2B:
                    S2B[name] = sb(name, shape, dt, ph_outer)
                return S2B[name]
            if True:
                ph = None
                HMg = sbm("HMg", [128, 8, 1024], BF16)
                YRg = sbm("YRg", [128, 8, 1024], BF16)
                WV = sbm("WV", [128, 8, D], BF16)
                GVB = sbm("GVB", [128, D], F32)
                BSH = sbm("BSH", [2, 8, 128], BF16)
                WST = sbm("WST", [128, 8, 128], BF16)
                if firstg:
                    dma("sp", GVB[:], gvb_d[:, :], (), ["GVB"])
                    dma("sp", BSH[:].rearrange("o g p -> o (g p)"), BSHS[:, :], (), ["BSH"])
                    dma("sp", WST[:].rearrange("p g q -> p (g q)"), WSTS[:, :], (), ["WST"])
                WU = [sbm("WU%d" % i, [128, 8, 128], BF16) for i in range(3)]
                GV = [sbm("GV%d" % i, [128, D], F32) for i in range(2)]
                RSV = sbm("RSV", [128, 4], F32)
                VN = sbm("VN", [128, 4, D], BF16)
                GUt = [sbm("GUt%d" % i, [128, 512], F32) for i in range(3)]
                YG = sbm("YG", [128, 8, 1024], BF16)
                MG = sbm("MG", [128, 8, 1024], BF16)
                WM4 = [sbm("WM4%d" % i, [128, 4, 8, 128], BF16) for i in range(3)]
                TA = [sbm("TA%d" % i, [128, 512], F32) for i in range(3)]
                TB = [sbm("TB%d" % i, [128, 512], F32) for i in range(3)]
                M1 = [sbm("M1%d" % i, [128, 512], F32) for i in range(2)]
                M2 = [sbm("M2%d" % i, [128, 512], F32) for i in range(2)]
                WO = WU
                if firstg:
                    wload(castg, [(WV[:], w_in[:, 3 * D:4 * D].rearrange("(kc p) n -> p kc n", p=128), "WV")],
                          WV[:].rearrange("p k n -> p (k n)"), WVS, ["WV"], "WVS")
                def load_hm_yr(g_):
                    for tti_ in range(2):
                        T_ = 2 * g_ + tti_
                        cols_ = slice(tti_ * 512, (tti_ + 1) * 512)
                        dma("sp", HMg[:, :, cols_], HM[:, :, T_ * 512:(T_ + 1) * 512].rearrange("m p t -> p m t"), [("HM", T_)], [("HMg", tti_)])
                    for tti_ in range(2):
                        T_ = 2 * g_ + tti_
                        cols_ = slice(tti_ * 512, (tti_ + 1) * 512)
                        dma("sp", YRg[:, :, cols_], YR[:, :, T_ * 512:(T_ + 1) * 512].rearrange("m p t -> p m t"), (), [("YRg", tti_)])
                if firstg:
                    load_hm_yr(g)
                cnt = 0
                for tti in range(2):
                    cols = slice(tti * 512, (tti + 1) * 512)
                    for n in range(4):
                        c0 = tti * 512 + n * 128
                        sl = n % 2
                        b0 = 0 if sl == 0 else 6
                        for half in range(2):
                            mm(bank(b0 + half), [(HMg[:, kc, c0:c0 + 128], WV[:, kc, half * 512:(half + 1) * 512]) for kc in range(8)],
                               [("HMg", tti), "WV"], [kb(b0 + half)])
                        act(GV[sl][:].rearrange("p (h n) -> p h n", n=512), PS[:, b0:b0 + 2, :], AF.Gelu_apprx_tanh,
                            [kb(b0), kb(b0 + 1)], [("GV", sl)])
                        act(VN[:, n, :], GV[sl][:], AF.Square, [("GV", sl)], [("VN", n), ("SSV", n)], accum=SSV[:, n:n + 1])
                        ts("dve", RSV[:, n:n + 1], SSV[:, n:n + 1], 1.0 / D, EPS, ALU.mult, ALU.add, [("SSV", n)], [("RSV", n)])
                        tt("pool", RSV[:, n:n + 1], RSV[:, n:n + 1], NEGH[:, 0:1], ALU.pow, [("RSV", n), "NEGH"], [("RSV", n)])
                        stt(VN[:, n, :], GV[sl][:], RSV[:, n:n + 1], GVB[:], ALU.mult, ALU.mult,
                            [("GV", sl), ("RSV", n), "GVB"], [("VN", n)])
                    for gi in range(8):
                        slot = gi % 3
                        wload(castg and tti == 0,
                              [(WU[slot][:], w_in[:, 2 * D + gi * 128:2 * D + (gi + 1) * 128].rearrange("(kc p) n -> p kc n", p=128), ("WU", slot))],
                              WU[slot][:].rearrange("p k n -> p (k n)"), WUS[gi], [("WU", slot)], ("WUS", gi))
                        bu = 2 + cnt % 2
                        bm = 4 + cnt % 2
                        sl = cnt % 3
                        cnt += 1
                        mm(bank(bu), [(WU[slot][:, kc, :], HMg[:, kc, cols]) for kc in range(8)],
                           [("WU", slot), ("HMg", tti)], [kb(bu)])
                        act(GUt[sl][:], bank(bu), AF.Gelu_apprx_tanh, [kb(bu)], [("GUt", sl)])

                        def emit_mix(e, gi=gi, bm=bm):
                            e.matmul(PS[:, bm, :].rearrange("p (a b) -> p a b", b=128), lhsT=ONES[0:2, :],
                                     rhs=BSH[0:2, gi, :].unsqueeze(1).broadcast_to([2, 4, 128]), start=True, stop=False)
                            ins = None
                            for n in range(4):
                                ins = e.matmul(PS[:, bm, n * 128:(n + 1) * 128], lhsT=VN[:, n, gi * 128:(gi + 1) * 128],
                                               rhs=WST[:, gi, :], start=False, stop=(n == 3))
                            return ins
                        P.add("pe", emit_mix, ["ONES", "BSH", "WST"] + [("VN", n) for n in range(4)], [kb(bm)])
                        tt("dve", YG[:, gi, cols], GUt[sl][:], bank(bm), ALU.mult, [("GUt", sl), kb(bm)], [("YG", gi, tti)])
                cnt = 0
                for f in range(8):
                    slot = f % 3
                    srcs = (w_in[:, 4 * D + f * 128:4 * D + (f + 1) * 128], w_in[:, 5 * D + f * 128:5 * D + (f + 1) * 128],
                            w_br[:, f * 128:(f + 1) * 128], w_bg[:, f * 128:(f + 1) * 128])
                    wload(castg, [(WM4[slot][:, i4, :, :], src.rearrange("(kc p) n -> p kc n", p=128), ("WM4", slot, i4))
                                   for i4, src in enumerate(srcs)],
                          WM4[slot][:].rearrange("p a k n -> p (a k n)"), WM4S[f], [("WM4", slot, i4) for i4 in range(4)], ("WM4S", f))
                    for tti in range(2):
                        cols = slice(tti * 512, (tti + 1) * 512)
                        par = cnt % 2
                        tp3 = cnt % 3
                        cnt += 1
                        rhs_src = (HMg, HMg, YRg, YG)
                        rk = ([("HMg", tti)], [("HMg", tti)], [("YRg", tti)], [("YG", gi, tti) for gi in range(8)])
                        for i4 in range(4):
                            b = 2 * i4 + par
                            mm(bank(b), [(WM4[slot][:, i4, kc, :], rhs_src[i4][:, kc, cols]) for kc in range(8)],
                               [("WM4", slot, i4)] + rk[i4], [kb(b)])
                        act(TA[tp3][:], bank(0 + par), AF.Tanh, [kb(0 + par)], [("TA", tp3)], scale=0.5)
                        act(TB[tp3][:], bank(2 + par), AF.Tanh, [kb(2 + par)], [("TB", tp3)], scale=0.5)
                        stt(M1[par][:], TA[tp3][:], 1.0, bank(4 + par), ALU.add, ALU.mult, [("TA", tp3), kb(4 + par)], [("M1", par)])
                        stt(M2[par][:], TB[tp3][:], 1.0, bank(6 + par), ALU.add, ALU.mult, [("TB", tp3), kb(6 + par)], [("M2", par)])
                        tt("pool", MG[:, f, cols], M1[par][:], M2[par][:], ALU.add, [("M1", par), ("M2", par)], [("MG", f, tti)])
                for tti in range(2):
                    T = 2 * g + tti
                    t0 = T * 512
                    cols = slice(tti * 512, (tti + 1) * 512)
                    dma("sp", xg[:, :, cols], XA[:, :, t0:t0 + 512].rearrange("m p t -> p m t"), [("XA", T)],
                        [("xg", m, tti) for m in range(8)])
                cnt = 0
                for m in range(8):
                    slot = m % 3
                    wload(castg, [(WO[slot][:], w_out[:, m * 128:(m + 1) * 128].rearrange("(kc p) n -> p kc n", p=128), ("WU", slot))],
                          WO[slot][:].rearrange("p k n -> p (k n)"), WOS[m], [("WU", slot)], ("WOS", m))
                    if m == 1 and g + 1 < 5:
                        load_hm_yr(g + 1)
                    for tti in range(2):
                        cols = slice(tti * 512, (tti + 1) * 512)
                        b = cnt % 2
                        cnt += 1
                        mm(bank(b), [(WO[slot][:, kc, :], MG[:, kc, cols]) for kc in range(8)],
                           [("WU", slot)] + [("MG", f, tti) for f in range(8)], [kb(b)])
                        stt(xg[:, m, cols], bank(b), gh_ap(cond, 1, m), xg[:, m, cols], ALU.mult, ALU.add,
                            [kb(b), "GH", ("xg", m, tti)], [("xg", m, tti)])
                for tti in range(2):
                    T = 2 * g + tti
                    dma("pool", XA[:, :, T * 512:(T + 1) * 512].rearrange("m p t -> p m t"), xg[:, :, tti * 512:(tti + 1) * 512],
                        [("xg", m, tti) for m in range(8)], [("XA", T)])

        def ff2_group(g, xg):
            cond = 0 if g < 4 else 1
            with contextlib.ExitStack() as ph:
                B = ff_buffers(ph, first=False)
                norm_stats(xg, B)
                for tti in range(2):
                    norm_apply(xg, tti, cond, 2, B, xmod_out=B["xmod"])
                ffn_group(2, cond, 2, xg, B, first=(g == 0))
                YF, YT = B["YF"], B["YT"]
                cnt = 0
                norm_stats(xg, B)
                for tti in range(2):
                    T = 2 * g + tti
                    t0 = T * 512
                    norm_apply(xg, tti, cond, None, B, yf_out=YF)
                    for s in range(4):
                        sl = cnt % 2
                        cnt += 1
                        for half in range(2):
                            b = 2 * (cnt % 2) + half

                            def emit_tr(e, s=s, half=half, b=b):
                                ins = None
                                for mmi in range(4):
                                    m = 4 * half + mmi
                                    ins = e.transpose(out=PS[:, b, mmi * 128:(mmi + 1) * 128], in_=YF[:, m, s * 128:(s + 1) * 128],
                                                      identity=IDN[:])
                                return ins
                            P.add("pe", emit_tr, [("YF", m) for m in range(8)] + ["IDN"], [kb(b)])
                            act(YT[sl][:, half * 512:(half + 1) * 512], bank(b), AF.Copy, [kb(b)], [("YT", sl)])
                        dma("pool", yrows(t0 + s * 128, 128), YT[sl][:], [("YT", sl)], [("yout", T, s)])
                P.barrier(ff_keys() + allps + s2_keys())

        QUART = sb("QUART", [128, 1], F32)
        memset("dve", QUART[:], 0.25, ["QUART"])
        EPSC = sb("EPSC", [128, 1], F32)
        memset("dve", EPSC[:], EPS, ["EPSC"])

        def ffn_tile(which, cond, k_gate, xg, xm, H, B, first, hooks):
            wg, wu, wd = ffw[which]
            GU, WD, SG = B["GU"], B["WD"], B["SG"]
            cnt = B["cnt"]
            for jp in range(NJ // 2):
                for hk in hooks.get(jp, ()):
                    hk()
                slot = cnt[0] % len(GU)
                cnt[0] += 1
                wload(first, [(GU[slot][:, 0, :, :], wg[:, jp * 256:(jp + 1) * 256].rearrange("(kc p) n -> p kc n", p=128), ("GU", slot, 0)),
                              (GU[slot][:, 1, :, :], wu[:, jp * 256:(jp + 1) * 256].rearrange("(kc p) n -> p kc n", p=128), ("GU", slot, 1))],
                      GU[slot][:].rearrange("p a k n -> p (a k n)"), GUS[which][jp], [("GU", slot, 0), ("GU", slot, 1)],
                      ("GUS", which, jp))
                for jj in range(2):
                    j = 2 * jp + jj
                    bg = cnt[1] % 2
                    bu = 2 + cnt[1] % 2
                    sl = cnt[1] % 2
                    cnt[1] += 1
                    xk = [("xm", id(xm), m) for m in range(8)]
                    mm(bank(bg), [(GU[slot][:, 0, kc, jj * 128:(jj + 1) * 128], xm[:, kc, :]) for kc in range(8)],
                       [("GU", slot, 0)] + xk, [kb(bg)])
                    mm(bank(bu), [(GU[slot][:, 1, kc, jj * 128:(jj + 1) * 128], xm[:, kc, :]) for kc in range(8)],
                       [("GU", slot, 1)] + xk, [kb(bu)])
                    act(SG[sl][:], bank(bg), AF.Silu, [kb(bg)], [("SG", sl)])
                    tt("dve", H[:, j, :], SG[sl][:], bank(bu), ALU.mult, [("SG", sl), kb(bu)], [("H", j)])
            for m in range(8):
                slot = cnt[2] % len(WD)
                cnt[2] += 1
                wload(first, [(WD[slot][:], wd[:, m * 128:(m + 1) * 128].rearrange("(kc p) n -> p kc n", p=128), ("WD", slot))],
                      WD[slot][:].rearrange("p k n -> p (k n)"), WDS[which][m], [("WD", slot)], ("WDS", which, m))
                b = 4 + cnt[2] % 2
                mm(bank(b), [(WD[slot][:, j, :], H[:, j, :]) for j in range(NJ)],
                   [("WD", slot)] + [("H", j) for j in range(NJ)], [kb(b)])
                stt(xg[:, m, :], bank(b), gh_ap(cond, k_gate, m), xg[:, m, :], ALU.mult, ALU.add,
                    [kb(b), "GH", ("xg", id(xg), m)], [("xg", id(xg), m)])

        def run_ff_tiles(which, ntiles=10):
            with contextlib.ExitStack() as ph:
                xgT = [sb("xgT", [128, 8, 512], F32, ph) for _ in range(3)]
                xmT = [sb("xmT", [128, 8, 512], BF16, ph) for _ in range(2)]
                HMo = sb("HMo", [128, 8, 512], BF16, ph) if which == 1 else None
                H = sb("Ht", [128, NJ, 512], BF16, ph)
                B = {"GU": [sb("GU", [128, 2, 8, 256], BF16, ph) for _ in range(3 if which == 1 else 4)],
                     "WD": [sb("WD", [128, NJ, 128], BF16, ph) for _ in range(2 if which == 1 else 3)],
                     "SG": [sb("SG", [128, 512], F32, ph) for _ in range(2)], "cnt": [0, 0, 0]}
                SQ = [sb("SQ", [128, 8, 512], BF16, ph) for _ in range(2)]
                LN = sb("LN", [128, 2, 512], F32, ph)
                RS = sb("RS", [128, 2, 512], F32, ph)
                TMP = [sb("TMP", [128, 512], F32, ph) for _ in range(2)]
                if which == 1:
                    XT = [sb("XT", [128, D], F32, ph) for _ in range(4)]
                else:
                    YF = sb("YF", [128, 8, 512], F32, ph)
                    YT = [sb("YT", [128, D], F32, ph) for _ in range(2)]
                tcnt = [0]
                k_in = 0 if which == 1 else 2
                pre_jobs = []
                if which == 1 and PRECAST:
                    STG = [sb("STG", [128, 4096], BF16, ph) for _ in range(2)]
                    pre_jobs = mixer_precast_jobs(STG) + ff_precast_jobs(2, STG)

                def condof(t):
                    return 0 if t < 8 else 1

                def P1(t, part=None):
                    xg = xgT[t % 3]
                    t0 = t * 512
                    if which == 2:
                        if part in (None, 0):
                            dma("sp", xg[:, :, :], XA[:, :, t0:t0 + 512].rearrange("m p t -> p m t"), [("XA", t)],
                                [("xg", id(xg), m) for m in range(8)])
                        return
                    if part in (None, 0):
                        for s in range(4):
                            dma("sp", XT[s][:], xrows(t0 + s * 128, 128), (), [("XT", s)])
                    if part == 0:
                        return
                    for m in range(8):
                        b = 4 + tcnt[0] % 2
                        tcnt[0] += 1

                        def emit_tr(e, m=m, b=b):
                            ins = None
                            for s in range(4):
                                ins = e.transpose(out=PS[:, b, s * 128:(s + 1) * 128], in_=XT[s][:, m * 128:(m + 1) * 128],
                                                  identity=IDN[:])
                            return ins
                        P.add("pe", emit_tr, [("XT", s) for s in range(4)] + ["IDN"], [kb(b)])
                        if condof(t) == 0:
                            pv = bank(b).rearrange("p (a b) -> p a b", b=64)
                            ov = xg[:, m, :].rearrange("p (a b) -> p a b", b=64)
                            if m < 4:
                                tv = TAB[:, m, 8 * t:8 * t + 8].unsqueeze(2).broadcast_to([128, 8, 64])
                            else:
                                tv = TAB[:, m - 4, :].unsqueeze(1).broadcast_to([128, 8, 64])
                            tt("dve", ov, pv, tv, ALU.add, [kb(b), "TAB"], [("xg", id(xg), m)])
                        else:
                            cp("dve", xg[:, m, :], bank(b), [kb(b)], [("xg", id(xg), m)])

                def N_sq(t, w, h=None):
                    xg = xgT[t % 3]
                    for hh in ((0, 1) if h is None else (h,)):
                        ms = range(4 * hh, 4 * hh + 4)
                        act(SQ[w][:, 4 * hh:4 * hh + 4, :], xg[:, 4 * hh:4 * hh + 4, :], AF.Square,
                            [("xg", id(xg), m) for m in ms], [("SQ", w, hh)])

                def N_mm(w):
                    mm(bank(6 + w), [(ONES[:], SQ[w][:, m, :]) for m in range(8)], [("SQ", w, 0), ("SQ", w, 1), "ONES"], [kb(6 + w)])

                def N_ln(w):
                    act(LN[:, w, :], bank(6 + w), AF.Ln, [kb(6 + w), "EPSC"], [("LN", w)], bias=EPSC[:, 0:1], scale=1.0 / D)
                    act(RS[:, w, :], LN[:, w, :], AF.Exp, [("LN", w)], [("RS", w)], scale=-0.5)

                def N_ln2():
                    act(LN[:], PS[:, 6:8, :], AF.Ln, [kb(6), kb(7), "EPSC"], [("LN", 0), ("LN", 1)], bias=EPSC[:, 0:1], scale=1.0 / D)
                    act(RS[:], LN[:], AF.Exp, [("LN", 0), ("LN", 1)], [("RS", 0), ("RS", 1)], scale=-0.5)

                def APPLY(t, w, k, store, h=None):
                    xg = xgT[t % 3]
                    xm = HMo if store else xmT[t % 2]
                    cond = condof(t)
                    for m in (range(8) if h is None else range(4 * h, 4 * h + 4)):
                        sl = m % 2
                        stt(TMP[sl][:], xg[:, m, :], ns_ap(cond, k, m), RS[:, w, :], ALU.mult, ALU.mult,
                            [("xg", id(xg), m), ("RS", w), "NS"], [("TMP", sl)])
                        act(xm[:, m, :], TMP[sl][:], AF.Identity, [("TMP", sl), "MOD"], [("xm", id(xm), m)],
                            bias=sh_ap(cond, k, m))
                    if store and h in (None, 1):
                        t0 = t * 512
                        dma("act", HM[:, :, t0:t0 + 512].rearrange("m p t -> p m t"), xm[:, :, :],
                            [("xm", id(xm), m) for m in range(8)], [("HM", t)])
                        dma("act", XA[:, :, t0:t0 + 512].rearrange("m p t -> p m t"), xg[:, :, :],
                            [("xg", id(xg), m) for m in range(8)], [("XA", t)])

                def FIN_dve(t):
                    xg = xgT[t % 3]
                    for m in range(8):
                        stt(YF[:, m, :], xg[:, m, :], VT[:, R_NF + m:R_NF + m + 1], RS[:, 0, :], ALU.mult, ALU.mult,
                            [("xg", id(xg), m), ("RS", 0), "VT"], [("YF", m)])

                def FIN_tr(t, srange):
                    t0 = t * 512
                    for s_ in srange:
                        sl = tcnt[0] % 2
                        tcnt[0] += 1
                        for half in range(2):
                            b = 4 + half

                            def emit_tr(e, s_=s_, half=half, b=b):
                                ins = None
                                for mmi in range(4):
                                    m = 4 * half + mmi
                                    ins = e.transpose(out=PS[:, b, mmi * 128:(mmi + 1) * 128], in_=YF[:, m, s_ * 128:(s_ + 1) * 128],
                                                      identity=IDN[:])
                                return ins
                            P.add("pe", emit_tr, [("YF", m) for m in range(8)] + ["IDN"], [kb(b)])
                            act(YT[sl][:, half * 512:(half + 1) * 512], bank(b), AF.Copy, [kb(b)], [("YT", sl)])
                        dma("pool", yrows(t0 + s_ * 128, 128), YT[sl][:], [("YT", sl)], [("yout", t, s_)])

                P1(0)
                N_sq(0, 1)
                N_mm(1)
                N_ln(1)
                APPLY(0, 1, k_in, False)
                for t in range(ntiles):
                    hooks = {}
                    both = (t >= 1 and t + 1 < ntiles)
                    if t >= 1:
                        hooks.setdefault(0, []).append(lambda t=t: N_sq(t - 1, 0, 0))
                        hooks.setdefault(1, []).append(lambda t=t: N_sq(t - 1, 0, 1))
                        hooks.setdefault(2, []).append(lambda: N_mm(0))
                    if t + 1 < ntiles:
                        hooks.setdefault(0, []).append(lambda t=t: P1(t + 1, 0))
                        hooks.setdefault(2, []).append(lambda t=t: P1(t + 1, 1))
                        hooks.setdefault(3, []).append(lambda t=t: N_sq(t + 1, 1, 0))
                        hooks.setdefault(4, []).append(lambda t=t: N_sq(t + 1, 1, 1))
                        hooks.setdefault(5, []).append(lambda: N_mm(1))
                    if both:
                        hooks.setdefault(6, []).append(N_ln2)
                    elif t >= 1:
                        hooks.setdefault(6, []).append(lambda: N_ln(0))
                    else:
                        hooks.setdefault(6, []).append(lambda: N_ln(1))
                    if t >= 1:
                        if which == 1:
                            hooks.setdefault(7, []).append(lambda t=t: APPLY(t - 1, 0, 1, True, 0))
                            hooks.setdefault(8, []).append(lambda t=t: APPLY(t - 1, 0, 1, True, 1))
                        else:
                            hooks.setdefault(7, []).append(lambda t=t: FIN_dve(t - 1))
                            hooks.setdefault(8, []).append(lambda t=t: FIN_tr(t - 1, (0, 1)))
                            hooks.setdefault(10, []).append(lambda t=t: FIN_tr(t - 1, (2, 3)))
                    if t + 1 < ntiles:
                        hooks.setdefault(9, []).append(lambda t=t: APPLY(t + 1, 1, k_in, False, 0))
                        hooks.setdefault(10, []).append(lambda t=t: APPLY(t + 1, 1, k_in, False, 1))
                    if t >= 1:
                        for jp_ in (1, 3, 5, 8, 10):
                            if pre_jobs:
                                hooks.setdefault(jp_, []).append(pre_jobs.pop(0))
                    ffn_tile(which, condof(t), k_in, xgT[t % 3], xmT[t % 2], H, B, t == 0 and not (which == 2 and PRECAST), hooks)
                t = ntiles - 1
                N_sq(t, 0)
                N_mm(0)
                N_ln(0)
                if which == 1:
                    APPLY(t, 0, 1, True)
                else:
                    FIN_dve(t)
                    FIN_tr(t, (0, 1, 2, 3))
                P.barrier()

        def run_ff1(groups):
            with contextlib.ExitStack() as ph:
                xg = sb("xg", [128, 8, 1024], F32, ph)
                B = ff_buffers(ph, first=True)
                for g in groups:
                    ff1_group(g, xg, B)
                P.barrier(ff_keys() + allps)

        def run_s2(groups):
            with contextlib.ExitStack() as ph:
                xg = sb("xg", [128, 8, 1024], F32, ph)
                for g in groups:
                    s2_group(g, xg, ph)
                P.barrier()

        stages = stop_after
        run_ff_tiles(1)
        if stages != "ff1":
            s1_group(0)
            s1_group(1)
            if stages != "s1":
                run_s2([0, 1, 2, 3, 4])
                if stages != "s2":
                    run_ff_tiles(2)

        P.build(nc, top, {"sp": 16, "act": 4, "pool": 12, "pe": 1, "dve": 1})
    nc._prog_stats = dict(n_ops=len(P.ops))
    return nc


def make_in_maps(inputs):
    f = lambda a: np.ascontiguousarray(np.asarray(a, dtype=np.float32))
    x_prompt, x_sample = f(inputs["x_prompt"]), f(inputs["x_sample"])
    c, c_ctx = f(inputs["c"]), f(inputs["c_ctx"])
    sf, sbw = f(inputs["state_rnn_fwd"]), f(inputs["state_rnn_bwd"])
    shared = {
        "ident": np.eye(128, dtype=np.float32),
        "gvb": np.ascontiguousarray(np.broadcast_to(f(inputs["gmlp_norm"])[0][None, :], (128, D))),
        "bs": f(inputs["b_s"])[0].reshape(1, D),
        "w_mod": f(inputs["w_mod"])[0],
        "ff1_gate": f(inputs["ff1_gate"])[0], "ff1_up": f(inputs["ff1_up"])[0], "ff1_down": f(inputs["ff1_down"])[0],
        "ff2_gate": f(inputs["ff2_gate"])[0], "ff2_up": f(inputs["ff2_up"])[0], "ff2_down": f(inputs["ff2_down"])[0],
        "w_in": f(inputs["w_in"])[0], "w_r": f(inputs["w_r"])[0], "w_i": f(inputs["w_i"])[0], "w_s": f(inputs["w_s"])[0],
        "w_br": f(inputs["w_br"])[0], "w_bg": f(inputs["w_bg"])[0], "w_out": f(inputs["w_out"])[0],
    }
    base = np.zeros((256, 128), np.float32)
    base[R_C1:R_C1 + 8] = c_ctx.reshape(8, 128)
    base[R_BMOD:R_BMOD + 72] = f(inputs["b_mod"])[0].reshape(72, 128)
    base[R_N1:R_N1 + 8] = f(inputs["norm1"])[0].reshape(8, 128)
    base[R_N2:R_N2 + 8] = f(inputs["norm2"])[0].reshape(8, 128)
    base[R_N3:R_N3 + 8] = f(inputs["norm3"])[0].reshape(8, 128)
    base[R_NF:R_NF + 8] = f(inputs["norm_f"]).reshape(8, 128)
    base[R_CW:R_CW + 32] = f(inputs["conv_w"])[0].reshape(32, 128)
    base[R_CB:R_CB + 8] = f(inputs["conv_b"])[0].reshape(8, 128)
    base[R_BR:R_BR + 16] = f(inputs["b_r"])[0].reshape(16, 128)
    base[R_BI:R_BI + 16] = f(inputs["b_i"])[0].reshape(16, 128)
    base[R_LAM:R_LAM + 16] = f(inputs["lam"])[0].reshape(16, 128)
    maps = []
    for b in range(8):
        v = base.copy()
        v[R_C0:R_C0 + 8] = c[b].reshape(8, 128)
        v[R_SF:R_SF + 8] = sf[b, 0].reshape(8, 128)
        v[R_SB:R_SB + 8] = sbw[b, 0].reshape(8, 128)
        m = dict(shared)
        m["xs"] = x_sample[b]
        m["xp"] = x_prompt[4 * b:4 * b + 4].reshape(1024, D)
        m["vecs"] = v
        maps.append(m)
    return maps


def kernel(**inputs):
    maps = make_in_maps(inputs)
    nc = build_nc()
    res = run_bass_kernel_spmd(nc, maps, core_ids=list(range(8)))
    y_prompt = np.zeros((32, 256, D), np.float32)
    y_sample = np.zeros((8, 4096, D), np.float32)
    nsf = np.zeros((32, 1, D), np.float32)
    nsb = np.zeros((32, 1, D), np.float32)
    for b in range(8):
        r = res.results[b]
        y_sample[b] = np.asarray(r["ys"], dtype=np.float32)
        y_prompt[4 * b:4 * b + 4] = np.asarray(r["yp"], dtype=np.float32).reshape(4, 256, D)
        nsf[4 * b:4 * b + 4, 0] = np.asarray(r["stf"], dtype=np.float32)
        nsb[4 * b:4 * b + 4, 0] = np.asarray(r["stb"], dtype=np.float32)
    return (y_prompt, y_sample, nsf, nsb)
```

```python
import contextlib
import math
import numpy as np
import concourse.bass as bass
import concourse.mybir as mybir
from concourse.bass_utils import run_bass_kernel_spmd

F32 = mybir.dt.float32
BF16 = mybir.dt.bfloat16
I32 = mybir.dt.int32
AF = mybir.ActivationFunctionType
ALU = mybir.AluOpType

D = 1024
DFF = 2816
NJ = DFF // 128
NTOK = 5120
S1_ORDER = 1
PRECAST = True
HZ_ALL = True
EPS = 1e-6

R_C0, R_C1, R_BMOD, R_N1, R_N2, R_N3, R_NF = 0, 8, 16, 88, 96, 104, 112
R_CW, R_CB, R_BR, R_BI, R_LAM, R_SF, R_SB = 128, 160, 168, 184, 200, 216, 224


class Prog:
    ENG = ("pe", "act", "dve", "pool", "sp")

    def __init__(self):
        self.ops = []
        self.last_w = {}
        self.readers = {}
        self.fence = None
        self.last_eng = {}
        self.pend_dma = []
        self.hz = HZ_ALL

    @contextlib.contextmanager
    def hazard(self):
        old = self.hz
        self.hz = True
        try:
            yield
        finally:
            self.hz = old

    def add(self, eng, emit, reads=(), writes=(), dma=False):
        i = len(self.ops)
        deps = set()
        lw = self.last_w
        rd = self.readers
        for k in reads:
            j = lw.get(k)
            if j is not None:
                deps.add(j)
        for k in writes:
            j = lw.get(k)
            if j is not None:
                deps.add(j)
            r = rd.get(k)
            if r:
                deps.update(r[0].values())
                deps.update(r[1])
        deps.discard(i)
        if self.fence is not None:
            deps.add(self.fence)
        if dma:
            self.pend_dma.append(i)
        else:
            self.last_eng[eng] = i
        for k in reads:
            r = rd.get(k)
            if r is None:
                r = rd[k] = ({}, [])
            if dma:
                r[1].append(i)
            else:
                r[0][eng] = i
        for k in writes:
            lw[k] = i
            rd[k] = ({}, [])
        self.ops.append(dict(eng=eng, emit=emit, dma=dma, deps=deps, hz=self.hz))

    def barrier(self, keys=None):
        i = len(self.ops)
        deps = set(self.last_eng.values()) | set(self.pend_dma)
        if self.fence is not None:
            deps.add(self.fence)
        self.ops.append(dict(eng="sp", emit=lambda e: e.nop(), dma=False, deps=deps, hz=False))
        self.fence = i
        self.last_eng = {"sp": i}
        self.pend_dma = []
        self.last_w = {}
        self.readers = {}

    def build(self, nc, stack, dma_ring):
        ops = self.ops
        n_dma = {e: 0 for e in self.ENG}
        dma_ops = {e: [] for e in self.ENG}
        for i, op in enumerate(ops):
            if op["dma"]:
                e = op["eng"]
                n = n_dma[e]
                K = dma_ring[e]
                op["dma_n"] = n
                if n >= K:
                    op["deps"].add(dma_ops[e][n - K])
                dma_ops[e].append(i)
                n_dma[e] += 1
        need = [False] * len(ops)
        for i, op in enumerate(ops):
            nd = set()
            for d in op["deps"]:
                p = ops[d]
                if (not p["dma"]) and (not op["dma"]) and p["eng"] == op["eng"] and not (op["hz"] and op["eng"] != "pe"):
                    continue
                nd.add(d)
                need[d] = True
            op["deps"] = nd
        esem = {e: stack.enter_context(nc.semaphore("s_" + e)) for e in self.ENG}
        dsem = {e: [stack.enter_context(nc.semaphore("d_%s%d" % (e, j))) for j in range(dma_ring[e])]
                for e in self.ENG if n_dma[e] > 0}
        cnt = {e: 0 for e in self.ENG}
        for i, op in enumerate(ops):
            if op["dma"]:
                e = op["eng"]
                n = op["dma_n"]
                K = dma_ring[e]
                op["sig"] = (dsem[e][n % K], 16 * (n // K + 1), 16)
            elif need[i]:
                e = op["eng"]
                cnt[e] += 1
                op["sig"] = (esem[e], cnt[e], 1)
            else:
                op["sig"] = None
        block = stack.enter_context(nc.Block())

        def run(engname, eng):
            waited = {}
            for i, op in enumerate(ops):
                if op["eng"] != engname:
                    continue
                for d in sorted(op["deps"]):
                    sem, val, _ = ops[d]["sig"]
                    key = id(sem)
                    if waited.get(key, 0) >= val:
                        continue
                    waited[key] = val
                    eng.wait_ge(sem, val)
                ins = op["emit"](eng)
                if op["sig"] is not None:
                    sem, val, inc = op["sig"]
                    ins.then_inc(sem, inc)
            if engname in dsem:
                n = n_dma[engname]
                K = dma_ring[engname]
                for j in range(min(n, K)):
                    last_n = ((n - 1 - j) // K) * K + j
                    val = 16 * (last_n // K + 1)
                    if waited.get(id(dsem[engname][j]), 0) < val:
                        eng.wait_ge(dsem[engname][j], val)

        @block.tensor
        def _(e):
            run("pe", e)

        @block.scalar
        def _(e):
            run("act", e)

        @block.vector
        def _(e):
            run("dve", e)

        @block.gpsimd
        def _(e):
            run("pool", e)

        @block.sync
        def _(e):
            run("sp", e)


def build_nc(debug=False, stop_after=None):
    nc = bass.Bass("TRN2", target_bir_lowering=False)
    P = Prog()

    def din(name, shape):
        return nc.dram_tensor(name, list(shape), F32, kind="ExternalInput").ap()

    def dout(name, shape, dt=F32):
        return nc.dram_tensor(name, list(shape), dt, kind="ExternalOutput").ap()

    xs = din("xs", [4096, D])
    xp = din("xp", [1024, D])
    vecs = din("vecs", [256, 128])
    ident_d = din("ident", [128, 128])
    gvb_d = din("gvb", [128, D])
    bs_d = din("bs", [1, D])
    w_mod = din("w_mod", [D, 9 * D])
    ffw = {1: (din("ff1_gate", [D, DFF]), din("ff1_up", [D, DFF]), din("ff1_down", [DFF, D])),
           2: (din("ff2_gate", [D, DFF]), din("ff2_up", [D, DFF]), din("ff2_down", [DFF, D]))}
    w_in = din("w_in", [D, 6 * D])
    w_r = din("w_r", [2, 16, 64, 64])
    w_i = din("w_i", [2, 16, 64, 64])
    w_s = din("w_s", [8, 128, 128])
    w_br = din("w_br", [D, D])
    w_bg = din("w_bg", [D, D])
    w_out = din("w_out", [D, D])
    ys = dout("ys", [4096, D])
    yp = dout("yp", [1024, D])
    stf_d = dout("stf", [4, D])
    stb_d = dout("stb", [4, D])
    skind = "ExternalOutput" if debug else "Internal"
    XA = nc.dram_tensor("XA", [8, 128, NTOK], F32, kind=skind).ap()
    HM = nc.dram_tensor("HM", [8, 128, NTOK], BF16, kind=skind).ap()
    YR = nc.dram_tensor("YR", [8, 128, NTOK], BF16, kind=skind).ap()
    if debug:
        DBG = nc.dram_tensor("DBG", [128, 2048], F32, kind="ExternalOutput").ap()
    GUS = {w: nc.dram_tensor("GUS%d" % w, [NJ // 2, 128, 2 * 8 * 256], BF16).ap() for w in (1, 2)}
    WDS = {w: nc.dram_tensor("WDS%d" % w, [8, 128, NJ * 128], BF16).ap() for w in (1, 2)}
    WXS = nc.dram_tensor("WXS", [8, 128, 2 * 8 * 128], BF16).ap()
    WVS = nc.dram_tensor("WVS", [128, 8 * D], BF16).ap()
    WUS = nc.dram_tensor("WUS", [8, 128, 8 * 128], BF16).ap()
    WM4S = nc.dram_tensor("WM4S", [8, 128, 4 * 8 * 128], BF16).ap()
    WOS = nc.dram_tensor("WOS", [8, 128, 8 * 128], BF16).ap()
    WSTS = nc.dram_tensor("WSTS", [128, 8 * 128], BF16).ap()
    BSHS = nc.dram_tensor("BSHS", [2, 8 * 128], BF16).ap()

    def xrows(t0, n):
        return xs[t0:t0 + n, :] if t0 < 4096 else xp[t0 - 4096:t0 - 4096 + n, :]

    def yrows(t0, n):
        return ys[t0:t0 + n, :] if t0 < 4096 else yp[t0 - 4096:t0 - 4096 + n, :]

    def dma(eng, out, in_, r, w):
        P.add(eng, lambda e: e.dma_start(out=out, in_=in_), r, w, dma=True)

    def wload(first, parts, flat, scr, keys, skey):
        if first:
            for (dst, src, k) in parts:
                dma("pool", dst, src, (), [k])
            dma("pool", scr, flat, keys, [skey])
        else:
            dma("sp", flat, scr, [skey], keys)

    def precast_job(stg, nelem, parts, scr, skey):
        for (dst, src) in parts:
            dma("pool", dst, src, (), [("STG", id(stg))])
        src_v = stg[:, 0:nelem]
        if len(scr.shape) == 3:
            src_v = src_v.rearrange("p (a b) -> p a b", a=scr.shape[1])
        dma("pool", scr, src_v, [("STG", id(stg))], [skey])

    def blk(src):
        return src.rearrange("(kc p) n -> p kc n", p=128)

    def mixer_precast_jobs(STG):
        jobs = []
        cnt = [0]

        def nxt():
            st = STG[cnt[0] % len(STG)]
            cnt[0] += 1
            return st
        for c in range(8):
            def j(c=c):
                st = nxt()
                v = st[:, 0:2048].rearrange("p (a k n) -> p a k n", a=2, k=8)
                precast_job(st, 2048, [(v[:, 0], blk(w_in[:, c * 128:(c + 1) * 128])),
                                       (v[:, 1], blk(w_in[:, D + c * 128:D + (c + 1) * 128]))], WXS[c], ("WXS", c))
            jobs.append(j)
        for h in range(2):
            def j(h=h):
                st = nxt()
                v = st[:, :].rearrange("p (k n) -> p k n", k=8)
                precast_job(st, 4096, [(v, blk(w_in[:, 3 * D + h * 512:3 * D + (h + 1) * 512]))],
                            WVS.rearrange("p (k n) -> p k n", k=8)[:, :, h * 512:(h + 1) * 512], "WVS")
            jobs.append(j)
        for g0 in range(0, 8, 4):
            def j(g0=g0):
                st = nxt()
                v = st[:, :].rearrange("p (g k n) -> p g k n", g=4, k=8)
                precast_job(st, 4096, [(v[:, i], blk(w_in[:, 2 * D + (g0 + i) * 128:2 * D + (g0 + i + 1) * 128])) for i in range(4)],
                            WUS[g0:g0 + 4].rearrange("g p e -> p g e"), ("WUS", g0))
            jobs.append(j)
        for f in range(8):
            def j(f=f):
                st = nxt()
                v = st[:, :].rearrange("p (a k n) -> p a k n", a=4, k=8)
                srcs = (w_in[:, 4 * D + f * 128:4 * D + (f + 1) * 128], w_in[:, 5 * D + f * 128:5 * D + (f + 1) * 128],
                        w_br[:, f * 128:(f + 1) * 128], w_bg[:, f * 128:(f + 1) * 128])
                precast_job(st, 4096, [(v[:, i], blk(srcs[i])) for i in range(4)], WM4S[f], ("WM4S", f))
            jobs.append(j)
        for m0 in range(0, 8, 4):
            def j(m0=m0):
                st = nxt()
                v = st[:, :].rearrange("p (g k n) -> p g k n", g=4, k=8)
                precast_job(st, 4096, [(v[:, i], blk(w_out[:, (m0 + i) * 128:(m0 + i + 1) * 128])) for i in range(4)],
                            WOS[m0:m0 + 4].rearrange("g p e -> p g e"), ("WOS", m0))
            jobs.append(j)
        return jobs

    def ff_precast_jobs(which, STG):
        wg, wu, wd = ffw[which]
        jobs = []
        cnt = [0]

        def nxt():
            st = STG[cnt[0] % len(STG)]
            cnt[0] += 1
            return st
        for jp in range(NJ // 2):
            def j(jp=jp):
                st = nxt()
                v = st[:, :].rearrange("p (a k n) -> p a k n", a=2, k=8)
                precast_job(st, 4096, [(v[:, 0], blk(wg[:, jp * 256:(jp + 1) * 256])), (v[:, 1], blk(wu[:, jp * 256:(jp + 1) * 256]))],
                            GUS[which][jp], ("GUS", which, jp))
            jobs.append(j)
        for m in range(8):
            def j(m=m):
                st = nxt()
                v = st[:, 0:NJ * 128].rearrange("p (k n) -> p k n", k=NJ)
                precast_job(st, NJ * 128, [(v, blk(wd[:, m * 128:(m + 1) * 128]))], WDS[which][m], ("WDS", which, m))
            jobs.append(j)
        return jobs

    def mm(ps_ap, pairs, r, w, first_start=True):
        def emit(e):
            n = len(pairs)
            ins = None
            for i, (l, rh) in enumerate(pairs):
                ins = e.matmul(ps_ap, lhsT=l, rhs=rh, start=(first_start and i == 0), stop=(i == n - 1))
            return ins
        P.add("pe", emit, r, w)

    def act(out, in_, func, r, w, bias=None, scale=None, accum=None):
        def emit(e):
            kw = {}
            if bias is not None:
                kw["bias"] = bias
            if scale is not None:
                kw["scale"] = scale
            if accum is not None:
                kw["accum_out"] = accum
            return e.activation(out=out, in_=in_, func=func, **kw)
        P.add("act", emit, r, w)

    def tt(eng, out, in0, in1, op, r, w):
        P.add(eng, lambda e: e.tensor_tensor(out=out, in0=in0, in1=in1, op=op), r, w)

    def ts(eng, out, in0, s1, s2, op0, op1, r, w):
        if op1 is None:
            P.add(eng, lambda e: e.tensor_scalar(out=out, in0=in0, scalar1=s1, scalar2=None, op0=op0), r, w)
        else:
            P.add(eng, lambda e: e.tensor_scalar(out=out, in0=in0, scalar1=s1, scalar2=s2, op0=op0, op1=op1), r, w)

    def stt(out, in0, scalar, in1, op0, op1, r, w):
        P.add("dve", lambda e: e.scalar_tensor_tensor(out=out, in0=in0, scalar=scalar, in1=in1, op0=op0, op1=op1), r, w)

    def cp(eng, out, in_, r, w):
        P.add(eng, lambda e: e.tensor_copy(out=out, in_=in_), r, w)

    def memset(eng, ap, val, w):
        P.add(eng, lambda e: e.memset(ap, val), (), w)

    with contextlib.ExitStack() as top:
        uid = [0]

        def sb(name, shape, dt, st=None):
            uid[0] += 1
            return (st or top).enter_context(nc.sbuf_tensor("%s_%d" % (name, uid[0]), list(shape), dt))

        PS = top.enter_context(nc.psum_tensor("PS", [128, 8, 512], F32))

        def bank(b):
            return PS[:, b, :]

        def kb(b):
            return ("ps", b)

        IDN = sb("IDN", [128, 128], F32)
        VT = sb("VT", [128, 256], F32)
        MOD = sb("MOD", [128, 2, 72], F32)
        NS = sb("NS", [128, 2, 3, 8], F32)
        GH = sb("GH", [128, 2, 3, 8], F32)
        LL = sb("LL", [128, 16], F32)
        L4 = sb("L4", [128, 16], F32)
        L8 = sb("L8", [128, 16], F32)
        HBR = sb("HBR", [128, 16], F32)
        HBI = sb("HBI", [128, 16], F32)
        TAB = sb("TAB", [128, 4, 64], F32)
        ONES = sb("ONES", [128, 128], BF16)
        NEGH = sb("NEGH", [128, 1], F32)
        BD = sb("BD", [128, 2, 2, 8, 128], BF16)
        STF = sb("STF", [128, 32], F32)
        STB = sb("STB", [128, 32], F32)
        SSV = sb("SSV", [128, 4], F32)

        def sh_ap(cond, k, m):
            return MOD[:, cond, (3 * k) * 8 + m:(3 * k) * 8 + m + 1]

        def ns_ap(cond, k, m):
            return NS[:, cond, k, m:m + 1]

        def gh_ap(cond, k, m):
            return GH[:, cond, k, m:m + 1]

        with contextlib.ExitStack() as ph, P.hazard():
            V0 = sb("V0", [128, 128], F32, ph)
            V1 = sb("V1", [128, 128], F32, ph)
            SB2 = sb("SB2", [128, 8, 2], BF16, ph)
            WM = [sb("WM%d" % i, [128, 8, D], BF16, ph) for i in range(2)]
            WSL = sb("WSL", [128, 8, 128], F32, ph)
            BS0 = sb("BS0", [1, D], F32, ph)
            BSHI = sb("BSHI", [1, D], BF16, ph)
            WST = sb("WSTset", [128, 8, 128], BF16, ph)
            BSLO = sb("BSLO", [1, D], BF16, ph)
            SIG = sb("SIG", [128, 16], F32, ph)
            QI = sb("QI", [128, 2], I32, ph)
            QF = sb("QF", [128, 2], F32, ph)
            FR = sb("FR", [128, 2], F32, ph)
            RI = sb("RI", [128, 64], I32, ph)
            RV = sb("RV", [128, 64], F32, ph)
            ANG = sb("ANG", [128, 4, 64], F32, ph)
            TQ = sb("TQ", [128, 4, 64], F32, ph)
            KI = sb("KI", [128, 4, 64], I32, ph)
            KF = sb("KF", [128, 4, 64], F32, ph)
            CM = sb("CM", [128, 4, 64], F32, ph)

            dma("sp", IDN[:], ident_d[:, :], (), ["IDN"])
            dma("sp", V0[:], vecs[0:128, :], (), ["V0"])
            dma("sp", V1[:], vecs[128:256, :], (), ["V1"])
            dma("sp", BS0[:], bs_d[:, :], (), ["BS0"])
            dma("sp", WSL[:], w_s.rearrange("g p q -> p g q"), (), ["WSL"])
            memset("pool", ONES[:], 1.0, ["ONES"])
            memset("pool", NEGH[:], -0.5, ["NEGH"])
            memset("pool", BD[:], 0.0, ["BD"])
            memset("dve", STF[:], 0.0, ["STF"])
            memset("dve", STB[:], 0.0, ["STB"])
            P.add("pe", lambda e: e.transpose(out=PS[:, 7, 0:128], in_=V0[:], identity=IDN[:]), ["V0", "IDN"], [kb(7)])
            P.add("pe", lambda e: e.transpose(out=PS[:, 7, 128:256], in_=V1[:], identity=IDN[:]), ["V1", "IDN"], [kb(7)])
            cp("dve", VT[:], PS[:, 7, 0:256], [kb(7)], ["VT"])
            for cond in range(2):
                act(SB2[:, :, cond], VT[:, 8 * cond:8 * cond + 8], AF.Silu, ["VT"], ["SB2"])
            act(SIG[:], VT[:, R_LAM:R_LAM + 16], AF.Sigmoid, ["VT"], ["SIG"])
            act(LL[:], SIG[:], AF.Ln, ["SIG"], ["LL"])
            ts("dve", L4[:], LL[:], 4.0, None, ALU.mult, None, ["LL"], ["L4"])
            ts("dve", L8[:], LL[:], 8.0, None, ALU.mult, None, ["LL"], ["L8"])
            ts("dve", HBR[:], VT[:, R_BR:R_BR + 16], 0.5, None, ALU.mult, None, ["VT"], ["HBR"])
            ts("dve", HBI[:], VT[:, R_BI:R_BI + 16], 0.5, None, ALU.mult, None, ["VT"], ["HBI"])
            for g in range(8):
                P.add("pe", lambda e, g=g: e.transpose(out=PS[:, 5, (g % 4) * 128:(g % 4) * 128 + 128], in_=WSL[:, g, :],
                                                       identity=IDN[:]), ["WSL", "IDN"], [kb(5)])
                cp("dve", WST[:, g, :], PS[:, 5, (g % 4) * 128:(g % 4) * 128 + 128], [kb(5)], ["WST"])
            cp("dve", BSHI[:], BS0[:], ["BS0"], ["BSHI"])
            tt("dve", BSLO[:], BS0[:], BSHI[:], ALU.subtract, ["BS0", "BSHI"], ["BSLO"])
            dma("sp", BSHS[0:1, :], BSHI[0:1, :], ["BSHI"], ["BSHS"])
            dma("sp", BSHS[1:2, :], BSLO[0:1, :], ["BSLO"], ["BSHS"])
            dma("sp", WSTS[:, :], WST[:].rearrange("p g q -> p (g q)"), ["WST"], ["WSTS"])
            P.add("pool", lambda e: e.iota(QI[:], [[128, 2]], base=0, channel_multiplier=1), (), ["QI"])
            P.add("pool", lambda e: e.iota(RI[:], [[1, 64]], base=0, channel_multiplier=0), (), ["RI"])
            cp("dve", QF[:], QI[:], ["QI"], ["QF"])
            cp("dve", RV[:], RI[:], ["RI"], ["RV"])
            act(FR[:], QF[:], AF.Exp, ["QF"], ["FR"], scale=-math.log(10000.0) / 256.0)
            for q2 in range(2):
                ts("dve", ANG[:, q2, :], RV[:], FR[:, q2:q2 + 1], None, ALU.mult, None, ["RV", "FR"], ["ANG"])
            ts("dve", ANG[:, 2:4, :], ANG[:, 0:2, :], math.pi / 2, None, ALU.add, None, ["ANG"], ["ANG"])
            ts("dve", TQ[:], ANG[:], 1.0 / (2 * math.pi), 0.5, ALU.mult, ALU.add, ["ANG"], ["TQ"])
            cp("dve", KI[:], TQ[:], ["TQ"], ["KI"])
            cp("dve", KF[:], KI[:], ["KI"], ["KF"])
            tt("dve", CM[:], KF[:], TQ[:], ALU.is_gt, ["KF", "TQ"], ["CM"])
            tt("dve", KF[:], KF[:], CM[:], ALU.subtract, ["KF", "CM"], ["KF"])
            stt(ANG[:], KF[:], -2 * math.pi, ANG[:], ALU.mult, ALU.add, ["KF", "ANG"], ["ANG"])
            ts("dve", ANG[:], ANG[:], 3.14159, -3.14159, ALU.min, ALU.max, ["ANG"], ["ANG"])
            act(TAB[:], ANG[:], AF.Sin, ["ANG"], ["TAB"])
            WF = [sb("WF", [128, 8, D], F32, ph) for _ in range(2)]
            WMo = [sb("WMo", [128, 8, D], BF16, ph) for _ in range(2)]
            hwn = 0
            for i in (0, 1, 2, 3, 5, 6, 7, 4, 8):
                src_i = w_mod[:, i * D:(i + 1) * D].rearrange("(kc p) n -> p kc n", p=128)
                if i == 4:
                    slot = 0
                    wt = WM[slot]
                    wkey = ("WM", slot)
                    dma("pool", wt[:], src_i, (), [wkey])
                else:
                    slot = hwn % 2
                    hwn += 1
                    wt = WMo[slot]
                    wkey = ("WMo", slot)
                    dma("sp", WF[slot][:], src_i, (), [("WF", slot)])
                    cp("dve", wt[:, 0:4, :], WF[slot][:, 0:4, :], [("WF", slot)], [wkey])
                    act(wt[:, 4:8, :], WF[slot][:, 4:8, :], AF.Copy, [("WF", slot)], [wkey])

                def emit_mod(e, i=i, wt=wt):
                    ins = None
                    for m in range(8):
                        for kc in range(8):
                            ins = e.matmul(PS[:, 6, (i * 8 + m) * 2:(i * 8 + m) * 2 + 2],
                                           lhsT=wt[:, kc, m * 128:(m + 1) * 128], rhs=SB2[:, kc, :],
                                           start=(kc == 0), stop=(kc == 7))
                    return ins
                P.add("pe", emit_mod, [wkey, "SB2"], [kb(6)])
            for d in range(2):
                for kind, wsrc in enumerate((w_r, w_i)):
                    v = wsrc[d].rearrange("(c two) i j -> two i c j", two=2)
                    for h in range(2):
                        dma("pool", BD[64 * h:64 * h + 64, d, kind, :, 64 * h:64 * h + 64], v[h], ["BD"], ["BD"])
            psmod = PS[:, 6, 0:144].rearrange("p (r c) -> p r c", c=2)
            for cond in range(2):
                tt("dve", MOD[:, cond, :], psmod[:, :, cond], VT[:, R_BMOD:R_BMOD + 72], ALU.add, [kb(6), "VT"], ["MOD"])
            for cond in range(2):
                for k in range(3):
                    stt(NS[:, cond, k, :], MOD[:, cond, (3 * k + 1) * 8:(3 * k + 1) * 8 + 8], 1.0,
                        VT[:, R_N1 + 8 * k:R_N1 + 8 * k + 8], ALU.add, ALU.mult, ["MOD", "VT"], ["NS"])
                    ts("dve", GH[:, cond, k, :], MOD[:, cond, (3 * k + 2) * 8:(3 * k + 2) * 8 + 8], 0.5, None,
                       ALU.mult, None, ["MOD"], ["GH"])
            if debug:
                dma("sp", DBG[:, 0:256], VT[:], ["VT"], ["DBG"])
                dma("sp", DBG[:, 256:400], MOD[:].rearrange("p a b -> p (a b)"), ["MOD"], ["DBG"])
                dma("sp", DBG[:, 400:656], TAB[:].rearrange("p a b -> p (a b)"), ["TAB"], ["DBG"])
                dma("sp", DBG[:, 656:672], LL[:], ["LL"], ["DBG"])
            P.barrier(["IDN", "VT", "MOD", "NS", "GH", "LL", "L4", "L8", "HBR", "HBI", "TAB", "ONES", "NEGH", "GVB",
                       "BSH", "WST", "BD", "STF", "STB"] + [kb(b) for b in range(8)])

        consts = ["IDN", "VT", "MOD", "NS", "GH", "L4", "L8", "HBR", "HBI", "TAB", "ONES", "NEGH", "GVB", "BSH", "WST", "BD"]

        def norm_stats(xg, B):
            SQ, LNT, RS = B["SQ"], B["MS"], B["RS"]
            for tti in range(2):
                cols = slice(tti * 512, (tti + 1) * 512)
                xk = [("xg", m, tti) for m in range(8)]
                act(SQ[:], xg[:, :, cols], AF.Square, xk, ["SQ"])
                mm(bank(6 + tti), [(ONES[:], SQ[:, m, :]) for m in range(8)], ["SQ", "ONES"], [kb(6 + tti)])
            act(LNT[:], PS[:, 6:8, :], AF.Ln, [kb(6), kb(7), "EPSC"], ["MS"], bias=EPSC[:, 0:1], scale=1.0 / D)
            act(RS[:], LNT[:], AF.Exp, ["MS"], ["RS"], scale=-0.5)

        def norm_apply(xg, tti, cond, k, B, xmod_out=None, yf_out=None):
            cols = slice(tti * 512, (tti + 1) * 512)
            RS, TMP = B["RS"], B["TMP"]
            for m in range(8):
                if xmod_out is not None:
                    sl = m % 2
                    stt(TMP[sl][:], xg[:, m, cols], ns_ap(cond, k, m), RS[:, tti, :], ALU.mult, ALU.mult,
                        [("xg", m, tti), "RS", "NS"], [("TMP", sl)])
                    act(xmod_out[:, m, cols], TMP[sl][:], AF.Identity, [("TMP", sl), "MOD"], [("xmod", m, tti)],
                        bias=sh_ap(cond, k, m))
                else:
                    stt(yf_out[:, m, :], xg[:, m, cols], VT[:, R_NF + m:R_NF + m + 1], RS[:, tti, :], ALU.mult, ALU.mult,
                        [("xg", m, tti), "RS", "VT"], [("YF", m)])

        def ffn_group(which, cond, k_gate, xg, B, first=True):
            wg, wu, wd = ffw[which]
            xmod, H, GU, WD, SG = B["xmod"], B["H"], B["GU"], B["WD"], B["SG"]
            cnt = 0
            for jp in range(NJ // 2):
                slot = jp % 3
                wload(first, [(GU[slot][:, 0, :, :], wg[:, jp * 256:(jp + 1) * 256].rearrange("(kc p) n -> p kc n", p=128), ("GU", slot, 0)),
                              (GU[slot][:, 1, :, :], wu[:, jp * 256:(jp + 1) * 256].rearrange("(kc p) n -> p kc n", p=128), ("GU", slot, 1))],
                      GU[slot][:].rearrange("p a k n -> p (a k n)"), GUS[which][jp], [("GU", slot, 0), ("GU", slot, 1)],
                      ("GUS", which, jp))
                for jj in range(2):
                    j = 2 * jp + jj
                    for tti in range(2):
                        cols = slice(tti * 512, (tti + 1) * 512)
                        bg = cnt % 2
                        bu = 2 + cnt % 2
                        cnt += 1
                        xk = [("xmod", m, tti) for m in range(8)]
                        mm(bank(bg), [(GU[slot][:, 0, kc, jj * 128:(jj + 1) * 128], xmod[:, kc, cols]) for kc in range(8)],
                           [("GU", slot, 0)] + xk, [kb(bg)])
                        mm(bank(bu), [(GU[slot][:, 1, kc, jj * 128:(jj + 1) * 128], xmod[:, kc, cols]) for kc in range(8)],
                           [("GU", slot, 1)] + xk, [kb(bu)])
                        sl = cnt % 2
                        act(SG[sl][:], bank(bg), AF.Silu, [kb(bg)], [("SG", sl)])
                        tt("dve", H[:, j, cols], SG[sl][:], bank(bu), ALU.mult, [("SG", sl), kb(bu)], [("H", j, tti)])
            cnt = 0
            for m in range(8):
                slot = m % 2
                wload(first, [(WD[slot][:], wd[:, m * 128:(m + 1) * 128].rearrange("(kc p) n -> p kc n", p=128), ("WD", slot))],
                      WD[slot][:].rearrange("p k n -> p (k n)"), WDS[which][m], [("WD", slot)], ("WDS", which, m))
                for tti in range(2):
                    cols = slice(tti * 512, (tti + 1) * 512)
                    b = 4 + cnt % 2
                    cnt += 1
                    mm(bank(b), [(WD[slot][:, j, :], H[:, j, cols]) for j in range(NJ)],
                       [("WD", slot)] + [("H", j, tti) for j in range(NJ)], [kb(b)])
                    stt(xg[:, m, cols], bank(b), gh_ap(cond, k_gate, m), xg[:, m, cols], ALU.mult, ALU.add,
                        [kb(b), "GH", ("xg", m, tti)], [("xg", m, tti)])

        def ff_buffers(ph, first):
            B = {}
            B["xmod"] = sb("xmod", [128, 8, 1024], BF16, ph)
            B["H"] = sb("H", [128, NJ, 1024], BF16, ph)
            B["GU"] = [sb("GU%d" % i, [128, 2, 8, 256], BF16, ph) for i in range(3)]
            B["WD"] = [sb("WD%d" % i, [128, NJ, 128], BF16, ph) for i in range(2)]
            B["SG"] = [sb("SG%d" % i, [128, 512], F32, ph) for i in range(2)]
            B["SQ"] = sb("SQ", [128, 8, 512], BF16, ph)
            B["MS"] = sb("MS", [128, 2, 512], F32, ph)
            B["RS"] = sb("RS", [128, 2, 512], F32, ph)
            B["TMP"] = [sb("TMP%d" % i, [128, 512], F32, ph) for i in range(2)]
            if first:
                B["XT"] = [sb("XT%d" % i, [128, D], F32, ph) for i in range(4)]
            else:
                B["YF"] = sb("YF", [128, 8, 512], F32, ph)
                B["YT"] = [sb("YT%d" % i, [128, D], F32, ph) for i in range(2)]
            return B

        def ff_keys():
            ks = [("xmod", m, t) for m in range(8) for t in range(2)] + [("H", j, t) for j in range(NJ) for t in range(2)]
            ks += [("GU", s, i) for s in range(3) for i in range(2)] + [("WD", s) for s in range(2)]
            ks += [("SG", 0), ("SG", 1), "SQ", "MS", "RS", ("TMP", 0), ("TMP", 1)]
            ks += [("XT", i) for i in range(4)] + [("YF", m) for m in range(8)] + [("YT", 0), ("YT", 1)]
            ks += [("xg", m, t) for m in range(8) for t in range(2)]
            return ks

        allps = [kb(b) for b in range(8)]

        def ff1_group(g, xg, B):
            cond = 0 if g < 4 else 1
            XT = B["XT"]
            for tti in range(2):
                T = 2 * g + tti
                t0 = T * 512
                cols = slice(tti * 512, (tti + 1) * 512)
                for s in range(4):
                    dma("sp", XT[s][:], xrows(t0 + s * 128, 128), (), [("XT", s)])
                for m in range(8):
                    b = 4 + m % 4

                    def emit_tr(e, m=m, b=b):
                        ins = None
                        for s in range(4):
                            ins = e.transpose(out=PS[:, b, s * 128:(s + 1) * 128], in_=XT[s][:, m * 128:(m + 1) * 128],
                                              identity=IDN[:])
                        return ins
                    P.add("pe", emit_tr, [("XT", s) for s in range(4)] + ["IDN"], [kb(b)])
                    if cond == 0:
                        pv = bank(b).rearrange("p (a b) -> p a b", b=64)
                        ov = xg[:, m, cols].rearrange("p (a b) -> p a b", b=64)
                        if m < 4:
                            tv = TAB[:, m, 8 * T:8 * T + 8].unsqueeze(2).broadcast_to([128, 8, 64])
                        else:
                            tv = TAB[:, m - 4, :].unsqueeze(1).broadcast_to([128, 8, 64])
                        tt("dve", ov, pv, tv, ALU.add, [kb(b), "TAB"], [("xg", m, tti)])
                    else:
                        cp("dve", xg[:, m, cols], bank(b), [kb(b)], [("xg", m, tti)])
            norm_stats(xg, B)
            for tti in range(2):
                norm_apply(xg, tti, cond, 0, B, xmod_out=B["xmod"])
            ffn_group(1, cond, 0, xg, B, first=(g == 0))
            norm_stats(xg, B)
            for tti in range(2):
                T = 2 * g + tti
                t0 = T * 512
                cols = slice(tti * 512, (tti + 1) * 512)
                norm_apply(xg, tti, cond, 1, B, xmod_out=B["xmod"])
                dma("sp", HM[:, :, t0:t0 + 512].rearrange("m p t -> p m t"), B["xmod"][:, :, cols],
                    [("xmod", m, tti) for m in range(8)], [("HM", T)])
                dma("sp", XA[:, :, t0:t0 + 512].rearrange("m p t -> p m t"), xg[:, :, cols],
                    [("xg", m, tti) for m in range(8)], [("XA", T)])

        def s1_group(sg):
            if sg == 0:
                T0, nt, nseq, L, cond = 0, 8, 1, 4096, 0
            else:
                T0, nt, nseq, L, cond = 8, 2, 4, 256, 1
            Ts = nt * 512
            spt = 512 // L if L < 512 else 1
            with contextlib.ExitStack() as ph:
                HMr = [sb("HMr", [128, 8, 512], BF16, ph) for _ in range(3)]
                XR = sb("XR", [128, nseq * (L + 3)], F32, ph)
                XC = sb("XC", [128, nseq, L], F32, ph)
                XCB = sb("XCB", [128, Ts], BF16, ph)
                AB = [[sb("ABI", [128, nseq, L], F32, ph) for _ in range(3)] for _ in range(2)]
                GGf = [sb("GGf", [128, Ts], BF16, ph) for _ in range(2)]
                YRt = [sb("YRt", [128, 512], BF16, ph) for _ in range(4)]
                WX = [sb("WX", [128, 2, 8, 128], BF16, ph) for _ in range(2)]
                TH = [sb("TH", [128, 512], F32, ph) for _ in range(2)]
                XR3 = XR[:, :].rearrange("p (s l) -> p s l", l=L + 3)
                XCf = XC[:].rearrange("p s l -> p (s l)")
                flat = lambda t3: t3[:].rearrange("p s l -> p (s l)")
                tk = lambda name: [(name, t) for t in range(nt)]
                memset("dve", XR3[:, :, 0:2], 0.0, ["XRhalo"])
                memset("dve", XR3[:, :, L + 2:L + 3], 0.0, ["XRhalo"])
                cnt = [0, 0, 0]

                def load_wx(c_):
                    sl_ = c_ % 2
                    wload(sg == 0 and not PRECAST, [(WX[sl_][:, 0, :, :], w_in[:, c_ * 128:(c_ + 1) * 128].rearrange("(kc p) n -> p kc n", p=128), ("WX", sl_, 0)),
                                    (WX[sl_][:, 1, :, :], w_in[:, D + c_ * 128:D + (c_ + 1) * 128].rearrange("(kc p) n -> p kc n", p=128), ("WX", sl_, 1))],
                          WX[sl_][:].rearrange("p a k n -> p (a k n)"), WXS[c_], [("WX", sl_, 0), ("WX", sl_, 1)], ("WXS", c_))

                def prep_tile(c, t):
                    slot = c % 2
                    GGc = GGf[c % 2]
                    T = T0 + t
                    cols = slice(t * 512, (t + 1) * 512)
                    hs = cnt[1] % 3
                    cnt[1] += 1
                    dma("sp", HMr[hs][:], HM[:, :, T * 512:(T + 1) * 512].rearrange("m p t -> p m t"), [("HM", T)], [("HMr", hs)])
                    bx = cnt[0] % 2
                    bgr = 6 + cnt[0] % 2
                    cnt[0] += 1
                    mm(bank(bx), [(WX[slot][:, 0, kc, :], HMr[hs][:, kc, :]) for kc in range(8)],
                       [("WX", slot, 0), ("HMr", hs)], [kb(bx)])
                    mm(bank(bgr), [(WX[slot][:, 1, kc, :], HMr[hs][:, kc, :]) for kc in range(8)],
                       [("WX", slot, 1), ("HMr", hs)], [kb(bgr)])
                    if L >= 512:
                        ov = XR3[:, 0, 2 + t * 512:2 + (t + 1) * 512]
                        iv = bank(bx)
                    else:
                        ov = XR3[:, t * spt:(t + 1) * spt, 2:2 + L]
                        iv = bank(bx).rearrange("p (s l) -> p s l", l=L)
                    act(ov, iv, AF.Copy, [kb(bx)], [("XR", t)])
                    act(GGc[:, cols], bank(bgr), AF.Copy, [kb(bgr)], [("GGf", c % 2, t)])

                def stage_gelu(c):
                    GGc = GGf[c % 2]
                    gk = [("GGf", c % 2, t) for t in range(nt)]
                    act(GGc[:], GGc[:], AF.Gelu_apprx_tanh, gk, gk)

                def stage_V(c):
                    ts("dve", XC[:], XR3[:, :, 0:L], VT[:, R_CW + c:R_CW + c + 1], VT[:, R_CB + c:R_CB + c + 1],
                       ALU.mult, ALU.add, tk("XR") + ["XRhalo", "VT"], tk("XC"))
                    for k in range(1, 4):
                        stt(XC[:], XR3[:, :, k:k + L], VT[:, R_CW + 8 * k + c:R_CW + 8 * k + c + 1], XC[:],
                            ALU.mult, ALU.add, tk("XR") + tk("XC") + ["VT", "XRhalo"], tk("XC"))
                    act(XCB[:], XCf, AF.Copy, tk("XC"), tk("XCB"))

                def stage_G_act(c, d, prep_c=None):
                    dc = d * 8 + c
                    A_, B_, I_ = AB[d]
                    Af, Bf, If = flat(A_), flat(B_), flat(I_)
                    for t in range(nt):
                        cols = slice(t * 512, (t + 1) * 512)
                        br = 2 + cnt[2] % 2
                        bi = 4 + cnt[2] % 2
                        sl = cnt[2] % 2
                        cnt[2] += 1
                        mm(bank(br), [(BD[:, d, 0, c, :], XCB[:, cols])], ["BD", ("XCB", t)], [kb(br)])
                        mm(bank(bi), [(BD[:, d, 1, c, :], XCB[:, cols])], ["BD", ("XCB", t)], [kb(bi)])
                        act(TH[sl][:], bank(br), AF.Tanh, [kb(br), "HBR"], [("TH", sl)], bias=HBR[:, dc:dc + 1], scale=0.5)
                        act(Af[:, cols], TH[sl][:], AF.Exp, [("TH", sl), "L4"], [("A", d, t)],
                            bias=L4[:, dc:dc + 1], scale=L4[:, dc:dc + 1])
                        tt("dve", Bf[:, cols], Af[:, cols], Af[:, cols], ALU.mult, [("A", d, t)], [("B", d, t)])
                        act(If[:, cols], bank(bi), AF.Tanh, [kb(bi), "HBI"], [("I", d, t)], bias=HBI[:, dc:dc + 1], scale=0.5)
                        if prep_c is not None:
                            prep_tile(prep_c, t)
                    if prep_c is not None and prep_c + 1 < 8:
                        load_wx(prep_c + 1)

                def stage_sqrt(c, d):
                    Bf = flat(AB[d][1])
                    ka = lambda n: [(n, d, t) for t in range(nt)]
                    act(Bf, Bf, AF.Sqrt, ka("B") + ["QUART"], ka("B"), bias=QUART[:, 0:1], scale=-0.25)

                def stage_G_dve1(c, d):
                    A_, B_, I_ = AB[d]
                    If = flat(I_)
                    ka = lambda n: [(n, d, t) for t in range(nt)]
                    stt(If, If, 1.0, XCf, ALU.add, ALU.mult, ka("I") + tk("XC"), ka("I"))

                def stage_G_dve2(c, d):
                    A_, B_, I_ = AB[d]
                    Af, Bf, If = flat(A_), flat(B_), flat(I_)
                    ka = lambda n: [(n, d, t) for t in range(nt)]
                    tt("dve", Bf, Bf, If, ALU.mult, ka("B") + ka("I"), ka("B"))
                    with P.hazard():
                        for s_ in range(nseq):
                            if sg == 0:
                                r0 = R_SF if d == 0 else R_SB
                                init = VT[:, r0 + c:r0 + c + 1]
                            else:
                                init = 0.0
                            if d == 0:
                                P.add("dve", lambda e, s_=s_, init=init, A_=A_, B_=B_: e.tensor_tensor_scan(
                                    out=B_[:, s_, :], data0=A_[:, s_, :], data1=B_[:, s_, :], initial=init,
                                    op0=ALU.mult, op1=ALU.add), ka("A") + ka("B") + ["VT"], ka("B"))
                            else:
                                P.add("dve", lambda e, s_=s_, init=init, A_=A_, B_=B_: e.tensor_tensor_scan(
                                    out=B_[:, s_, ::-1], data0=A_[:, s_, ::-1], data1=B_[:, s_, ::-1], initial=init,
                                    op0=ALU.mult, op1=ALU.add), ka("A") + ka("B") + ["VT"], ka("B"))
                        if sg == 1:
                            if d == 0:
                                cp("dve", STF[:, :].rearrange("p (s c) -> p s c", c=8)[:, :, c], B_[:, :, L - 1], ka("B"), ["STF"])
                            else:
                                cp("dve", STB[:, :].rearrange("p (s c) -> p s c", c=8)[:, :, c], B_[:, :, 0], ka("B"), ["STB"])

                def stage_ADD(c):
                    B0f, B1f = flat(AB[0][1]), flat(AB[1][1])
                    with P.hazard():
                        tt("dve", B0f, B0f, B1f, ALU.add, [("B", 0, t) for t in range(nt)] + [("B", 1, t) for t in range(nt)],
                           [("B", 0, t) for t in range(nt)])

                def stage_Y(c):
                    B0f = flat(AB[0][1])
                    GGc = GGf[c % 2]
                    for t in range(nt):
                        cols = slice(t * 512, (t + 1) * 512)
                        ys_ = t % 4
                        tt("dve", YRt[ys_][:], GGc[:, cols], B0f[:, cols], ALU.mult, [("GGf", c % 2, t), ("B", 0, t)], [("YRt", ys_)])
                        t0 = (T0 + t) * 512
                        dma("pool", YR[c, :, t0:t0 + 512], YRt[ys_][:], [("YRt", ys_)], [("YR", sg, c, t)])

                pre2 = []
                if False and sg == 1 and PRECAST:
                    STG2 = [sb("STG2", [128, 4096], BF16, ph) for _ in range(3)]
                    pre2 = ff_precast_jobs(2, STG2)
                load_wx(0)
                for t in range(nt):
                    prep_tile(0, t)
                load_wx(1)
                stage_V(0)
                for c in range(8):
                    for _ in range(3):
                        if pre2:
                            pre2.pop(0)()
                    nxt = c + 1 if c + 1 < 8 else None
                    if sg == 0:
                        stage_G_act(c, 0)
                        stage_sqrt(c, 0)
                        stage_G_dve1(c, 0)
                        stage_G_dve2(c, 0)
                        stage_G_act(c, 1, nxt)
                        stage_sqrt(c, 1)
                    else:
                        stage_G_act(c, 0)
                        stage_G_act(c, 1, nxt)
                        stage_sqrt(c, 0)
                        stage_sqrt(c, 1)
                        stage_G_dve1(c, 0)
                        stage_G_dve2(c, 0)
                    stage_gelu(c)
                    stage_G_dve1(c, 1)
                    if nxt is not None:
                        stage_V(nxt)
                    stage_G_dve2(c, 1)
                    stage_ADD(c)
                    stage_Y(c)
                if sg == 1:
                    STT_ = sb("STT_", [32, 128], F32, ph)
                    for nm, src, dst in (("f", STF, stf_d), ("b", STB, stb_d)):
                        P.add("pe", lambda e, src=src: e.transpose(out=PS[0:32, 0, 0:128], in_=src[:, :], identity=IDN[:]),
                              ["STF", "STB", "IDN"], [kb(0)])
                        cp("dve", STT_[:], PS[0:32, 0, 0:128], [kb(0)], ["STT_"])
                        dma("pool", dst.rearrange("s (c p) -> (s c) p", p=128), STT_[:], ["STT_"], [("st", nm)])
                P.barrier()

        def s2_keys():
            ks = [("HMg", t) for t in range(2)] + [("YRg", t) for t in range(2)] + ["WV", ("WU", 0), ("WU", 1)]
            ks += [("GV", 0), ("GV", 1), "JUNK", "SSV", "RSV"] + [("VN", n) for n in range(4)]
            ks += [("GUt", 0), ("GUt", 1)] + [("YG", g, t) for g in range(8) for t in range(2)]
            ks += [("MG", f, t) for f in range(8) for t in range(2)] + [("WM4", s, i) for s in range(2) for i in range(4)]
            ks += [("TA", 0), ("TA", 1), ("TB", 0), ("TB", 1), ("M1", 0), ("M1", 1), ("M2", 0), ("M2", 1), ("WO", 0), ("WO", 1)]
            return ks

        S2B = {}

        def s2_group(g, xg, ph_outer):
            cond = 0 if g < 4 else 1
            firstg = (len(S2B) == 0)
            castg = firstg and not PRECAST

            def sbm(name, shape, dt):
                if name not in S2B:
                    S2B[name] = sb(name, shape, dt, ph_outer)
                return S2B[name]
            if True:
                ph = None
                HMg = sbm("HMg", [128, 8, 1024], BF16)
                YRg = sbm("YRg", [128, 8, 1024], BF16)
                WV = sbm("WV", [128, 8, D], BF16)
                GVB = sbm("GVB", [128, D], F32)
                BSH = sbm("BSH", [2, 8, 128], BF16)
                WST = sbm("WST", [128, 8, 128], BF16)
                if firstg:
                    dma("sp", GVB[:], gvb_d[:, :], (), ["GVB"])
                    dma("sp", BSH[:].rearrange("o g p -> o (g p)"), BSHS[:, :], (), ["BSH"])
                    dma("sp", WST[:].rearrange("p g q -> p (g q)"), WSTS[:, :], (), ["WST"])
                WU = [sbm("WU%d" % i, [128, 8, 128], BF16) for i in range(3)]
                GV = [sbm("GV%d" % i, [128, D], F32) for i in range(2)]
                RSV = sbm("RSV", [128, 4], F32)
                VN = sbm("VN", [128, 4, D], BF16)
                GUt = [sbm("GUt%d" % i, [128, 512], F32) for i in range(2)]
                YG = sbm("YG", [128, 8, 1024], BF16)
                MG = sbm("MG", [128, 8, 1024], BF16)
                WM4 = [sbm("WM4%d" % i, [128, 4, 8, 128], BF16) for i in range(3)]
                TA = [sbm("TA%d" % i, [128, 512], F32) for i in range(2)]
                TB = [sbm("TB%d" % i, [128, 512], F32) for i in range(2)]
                M1 = [sbm("M1%d" % i, [128, 512], F32) for i in range(2)]
                M2 = [sbm("M2%d" % i, [128, 512], F32) for i in range(2)]
                WO = WU
                if firstg:
                    wload(castg, [(WV[:], w_in[:, 3 * D:4 * D].rearrange("(kc p) n -> p kc n", p=128), "WV")],
                          WV[:].rearrange("p k n -> p (k n)"), WVS, ["WV"], "WVS")
                def load_hm_yr(g_):
                    for tti_ in range(2):
                        T_ = 2 * g_ + tti_
                        cols_ = slice(tti_ * 512, (tti_ + 1) * 512)
                        dma("sp", HMg[:, :, cols_], HM[:, :, T_ * 512:(T_ + 1) * 512].rearrange("m p t -> p m t"), [("HM", T_)], [("HMg", tti_)])
                    for tti_ in range(2):
                        T_ = 2 * g_ + tti_
                        cols_ = slice(tti_ * 512, (tti_ + 1) * 512)
                        dma("sp", YRg[:, :, cols_], YR[:, :, T_ * 512:(T_ + 1) * 512].rearrange("m p t -> p m t"), (), [("YRg", tti_)])
                if firstg:
                    load_hm_yr(g)
                cnt = 0
                for tti in range(2):
                    cols = slice(tti * 512, (tti + 1) * 512)
                    for n in range(4):
                        c0 = tti * 512 + n * 128
                        sl = n % 2
                        b0 = 0 if sl == 0 else 6
                        for half in range(2):
                            mm(bank(b0 + half), [(HMg[:, kc, c0:c0 + 128], WV[:, kc, half * 512:(half + 1) * 512]) for kc in range(8)],
                               [("HMg", tti), "WV"], [kb(b0 + half)])
                        act(GV[sl][:].rearrange("p (h n) -> p h n", n=512), PS[:, b0:b0 + 2, :], AF.Gelu_apprx_tanh,
                            [kb(b0), kb(b0 + 1)], [("GV", sl)])
                        act(VN[:, n, :], GV[sl][:], AF.Square, [("GV", sl)], [("VN", n), ("SSV", n)], accum=SSV[:, n:n + 1])
                        ts("dve", RSV[:, n:n + 1], SSV[:, n:n + 1], 1.0 / D, EPS, ALU.mult, ALU.add, [("SSV", n)], [("RSV", n)])
                        tt("pool", RSV[:, n:n + 1], RSV[:, n:n + 1], NEGH[:, 0:1], ALU.pow, [("RSV", n), "NEGH"], [("RSV", n)])
                        stt(VN[:, n, :], GV[sl][:], RSV[:, n:n + 1], GVB[:], ALU.mult, ALU.mult,
                            [("GV", sl), ("RSV", n), "GVB"], [("VN", n)])
                    for gi in range(8):
                        slot = gi % 3
                        wload(castg and tti == 0,
                              [(WU[slot][:], w_in[:, 2 * D + gi * 128:2 * D + (gi + 1) * 128].rearrange("(kc p) n -> p kc n", p=128), ("WU", slot))],
                              WU[slot][:].rearrange("p k n -> p (k n)"), WUS[gi], [("WU", slot)], ("WUS", gi))
                        bu = 2 + cnt % 2
                        bm = 4 + cnt % 2
                        sl = cnt % 2
                        cnt += 1
                        mm(bank(bu), [(WU[slot][:, kc, :], HMg[:, kc, cols]) for kc in range(8)],
                           [("WU", slot), ("HMg", tti)], [kb(bu)])
                        act(GUt[sl][:], bank(bu), AF.Gelu_apprx_tanh, [kb(bu)], [("GUt", sl)])

                        def emit_mix(e, gi=gi, bm=bm):
                            e.matmul(PS[:, bm, :].rearrange("p (a b) -> p a b", b=128), lhsT=ONES[0:2, :],
                                     rhs=BSH[0:2, gi, :].unsqueeze(1).broadcast_to([2, 4, 128]), start=True, stop=False)
                            ins = None
                            for n in range(4):
                                ins = e.matmul(PS[:, bm, n * 128:(n + 1) * 128], lhsT=VN[:, n, gi * 128:(gi + 1) * 128],
                                               rhs=WST[:, gi, :], start=False, stop=(n == 3))
                            return ins
                        P.add("pe", emit_mix, ["ONES", "BSH", "WST"] + [("VN", n) for n in range(4)], [kb(bm)])
                        tt("dve", YG[:, gi, cols], GUt[sl][:], bank(bm), ALU.mult, [("GUt", sl), kb(bm)], [("YG", gi, tti)])
                cnt = 0
                for f in range(8):
                    slot = f % 3
                    srcs = (w_in[:, 4 * D + f * 128:4 * D + (f + 1) * 128], w_in[:, 5 * D + f * 128:5 * D + (f + 1) * 128],
                            w_br[:, f * 128:(f + 1) * 128], w_bg[:, f * 128:(f + 1) * 128])
                    wload(castg, [(WM4[slot][:, i4, :, :], src.rearrange("(kc p) n -> p kc n", p=128), ("WM4", slot, i4))
                                   for i4, src in enumerate(srcs)],
                          WM4[slot][:].rearrange("p a k n -> p (a k n)"), WM4S[f], [("WM4", slot, i4) for i4 in range(4)], ("WM4S", f))
                    for tti in range(2):
                        cols = slice(tti * 512, (tti + 1) * 512)
                        par = cnt % 2
                        cnt += 1
                        rhs_src = (HMg, HMg, YRg, YG)
                        rk = ([("HMg", tti)], [("HMg", tti)], [("YRg", tti)], [("YG", gi, tti) for gi in range(8)])
                        for i4 in range(4):
                            b = 2 * i4 + par
                            mm(bank(b), [(WM4[slot][:, i4, kc, :], rhs_src[i4][:, kc, cols]) for kc in range(8)],
                               [("WM4", slot, i4)] + rk[i4], [kb(b)])
                        act(TA[par][:], bank(0 + par), AF.Tanh, [kb(0 + par)], [("TA", par)], scale=0.5)
                        act(TB[par][:], bank(2 + par), AF.Tanh, [kb(2 + par)], [("TB", par)], scale=0.5)
                        stt(M1[par][:], TA[par][:], 1.0, bank(4 + par), ALU.add, ALU.mult, [("TA", par), kb(4 + par)], [("M1", par)])
                        stt(M2[par][:], TB[par][:], 1.0, bank(6 + par), ALU.add, ALU.mult, [("TB", par), kb(6 + par)], [("M2", par)])
                        tt("pool", MG[:, f, cols], M1[par][:], M2[par][:], ALU.add, [("M1", par), ("M2", par)], [("MG", f, tti)])
                for tti in range(2):
                    T = 2 * g + tti
                    t0 = T * 512
                    cols = slice(tti * 512, (tti + 1) * 512)
                    dma("sp", xg[:, :, cols], XA[:, :, t0:t0 + 512].rearrange("m p t -> p m t"), [("XA", T)],
                        [("xg", m, tti) for m in range(8)])
                cnt = 0
                for m in range(8):
                    slot = m % 3
                    wload(castg, [(WO[slot][:], w_out[:, m * 128:(m + 1) * 128].rearrange("(kc p) n -> p kc n", p=128), ("WU", slot))],
                          WO[slot][:].rearrange("p k n -> p (k n)"), WOS[m], [("WU", slot)], ("WOS", m))
                    if m == 1 and g + 1 < 5:
                        load_hm_yr(g + 1)
                    for tti in range(2):
                        cols = slice(tti * 512, (tti + 1) * 512)
                        b = cnt % 2
                        cnt += 1
                        mm(bank(b), [(WO[slot][:, kc, :], MG[:, kc, cols]) for kc in range(8)],
                           [("WU", slot)] + [("MG", f, tti) for f in range(8)], [kb(b)])
                        stt(xg[:, m, cols], bank(b), gh_ap(cond, 1, m), xg[:, m, cols], ALU.mult, ALU.add,
                            [kb(b), "GH", ("xg", m, tti)], [("xg", m, tti)])
                for tti in range(2):
                    T = 2 * g + tti
                    dma("pool", XA[:, :, T * 512:(T + 1) * 512].rearrange("m p t -> p m t"), xg[:, :, tti * 512:(tti + 1) * 512],
                        [("xg", m, tti) for m in range(8)], [("XA", T)])

        def ff2_group(g, xg):
            cond = 0 if g < 4 else 1
            with contextlib.ExitStack() as ph:
                B = ff_buffers(ph, first=False)
                norm_stats(xg, B)
                for tti in range(2):
                    norm_apply(xg, tti, cond, 2, B, xmod_out=B["xmod"])
                ffn_group(2, cond, 2, xg, B, first=(g == 0))
                YF, YT = B["YF"], B["YT"]
                cnt = 0
                norm_stats(xg, B)
                for tti in range(2):
                    T = 2 * g + tti
                    t0 = T * 512
                    norm_apply(xg, tti, cond, None, B, yf_out=YF)
                    for s in range(4):
                        sl = cnt % 2
                        cnt += 1
                        for half in range(2):
                            b = 2 * (cnt % 2) + half

                            def emit_tr(e, s=s, half=half, b=b):
                                ins = None
                                for mmi in range(4):
                                    m = 4 * half + mmi
                                    ins = e.transpose(out=PS[:, b, mmi * 128:(mmi + 1) * 128], in_=YF[:, m, s * 128:(s + 1) * 128],
                                                      identity=IDN[:])
                                return ins
                            P.add("pe", emit_tr, [("YF", m) for m in range(8)] + ["IDN"], [kb(b)])
                            act(YT[sl][:, half * 512:(half + 1) * 512], bank(b), AF.Copy, [kb(b)], [("YT", sl)])
                        dma("pool", yrows(t0 + s * 128, 128), YT[sl][:], [("YT", sl)], [("yout", T, s)])
                P.barrier(ff_keys() + allps + s2_keys())

        QUART = sb("QUART", [128, 1], F32)
        memset("dve", QUART[:], 0.25, ["QUART"])
        EPSC = sb("EPSC", [128, 1], F32)
        memset("dve", EPSC[:], EPS, ["EPSC"])

        def ffn_tile(which, cond, k_gate, xg, xm, H, B, first, hooks):
            wg, wu, wd = ffw[which]
            GU, WD, SG = B["GU"], B["WD"], B["SG"]
            cnt = B["cnt"]
            for jp in range(NJ // 2):
                for hk in hooks.get(jp, ()):
                    hk()
                slot = cnt[0] % len(GU)
                cnt[0] += 1
                wload(first, [(GU[slot][:, 0, :, :], wg[:, jp * 256:(jp + 1) * 256].rearrange("(kc p) n -> p kc n", p=128), ("GU", slot, 0)),
                              (GU[slot][:, 1, :, :], wu[:, jp * 256:(jp + 1) * 256].rearrange("(kc p) n -> p kc n", p=128), ("GU", slot, 1))],
                      GU[slot][:].rearrange("p a k n -> p (a k n)"), GUS[which][jp], [("GU", slot, 0), ("GU", slot, 1)],
                      ("GUS", which, jp))
                for jj in range(2):
                    j = 2 * jp + jj
                    bg = cnt[1] % 2
                    bu = 2 + cnt[1] % 2
                    sl = cnt[1] % len(SG)
                    cnt[1] += 1
                    xk = [("xm", id(xm), m) for m in range(8)]
                    mm(bank(bg), [(GU[slot][:, 0, kc, jj * 128:(jj + 1) * 128], xm[:, kc, :]) for kc in range(8)],
                       [("GU", slot, 0)] + xk, [kb(bg)])
                    mm(bank(bu), [(GU[slot][:, 1, kc, jj * 128:(jj + 1) * 128], xm[:, kc, :]) for kc in range(8)],
                       [("GU", slot, 1)] + xk, [kb(bu)])
                    act(SG[sl][:], bank(bg), AF.Silu, [kb(bg)], [("SG", sl)])
                    tt("dve", H[:, j, :], SG[sl][:], bank(bu), ALU.mult, [("SG", sl), kb(bu)], [("H", j)])
            for m in range(8):
                slot = cnt[2] % len(WD)
                cnt[2] += 1
                wload(first, [(WD[slot][:], wd[:, m * 128:(m + 1) * 128].rearrange("(kc p) n -> p kc n", p=128), ("WD", slot))],
                      WD[slot][:].rearrange("p k n -> p (k n)"), WDS[which][m], [("WD", slot)], ("WDS", which, m))
                b = 4 + cnt[2] % 2
                mm(bank(b), [(WD[slot][:, j, :], H[:, j, :]) for j in range(NJ)],
                   [("WD", slot)] + [("H", j) for j in range(NJ)], [kb(b)])
                stt(xg[:, m, :], bank(b), gh_ap(cond, k_gate, m), xg[:, m, :], ALU.mult, ALU.add,
                    [kb(b), "GH", ("xg", id(xg), m)], [("xg", id(xg), m)])

        def run_ff_tiles(which, ntiles=10):
            with contextlib.ExitStack() as ph:
                xgT = [sb("xgT", [128, 8, 512], F32, ph) for _ in range(3)]
                xmT = [sb("xmT", [128, 8, 512], BF16, ph) for _ in range(2)]
                HMo = sb("HMo", [128, 8, 512], BF16, ph) if which == 1 else None
                H = sb("Ht", [128, NJ, 512], BF16, ph)
                B = {"GU": [sb("GU", [128, 2, 8, 256], BF16, ph) for _ in range(3 if which == 1 else 4)],
                     "WD": [sb("WD", [128, NJ, 128], BF16, ph) for _ in range(2 if which == 1 else 3)],
                     "SG": [sb("SG", [128, 512], F32, ph) for _ in range(2 if which == 1 else 3)], "cnt": [0, 0, 0]}
                SQ = [sb("SQ", [128, 8, 512], BF16, ph) for _ in range(2)]
                LN = sb("LN", [128, 2, 512], F32, ph)
                RS = sb("RS", [128, 2, 512], F32, ph)
                TMP = [sb("TMP", [128, 512], F32, ph) for _ in range(2)]
                if which == 1:
                    XT = [sb("XT", [128, D], F32, ph) for _ in range(4)]
                else:
                    YF = sb("YF", [128, 8, 512], F32, ph)
                    YT = [sb("YT", [128, D], F32, ph) for _ in range(2)]
                tcnt = [0]
                k_in = 0 if which == 1 else 2
                pre_jobs = []
                if which == 1 and PRECAST:
                    STG = [sb("STG", [128, 4096], BF16, ph) for _ in range(2)]
                    pre_jobs = mixer_precast_jobs(STG) + ff_precast_jobs(2, STG)

                def condof(t):
                    return 0 if t < 8 else 1

                def P1(t, part=None):
                    xg = xgT[t % 3]
                    t0 = t * 512
                    if which == 2:
                        if part in (None, 0):
                            dma("sp", xg[:, :, :], XA[:, :, t0:t0 + 512].rearrange("m p t -> p m t"), [("XA", t)],
                                [("xg", id(xg), m) for m in range(8)])
                        return
                    if part in (None, 0):
                        for s in range(4):
                            dma("sp", XT[s][:], xrows(t0 + s * 128, 128), (), [("XT", s)])
                    if part == 0:
                        return
                    for m in range(8):
                        b = 4 + tcnt[0] % 2
                        tcnt[0] += 1

                        def emit_tr(e, m=m, b=b):
                            ins = None
                            for s in range(4):
                                ins = e.transpose(out=PS[:, b, s * 128:(s + 1) * 128], in_=XT[s][:, m * 128:(m + 1) * 128],
                                                  identity=IDN[:])
                            return ins
                        P.add("pe", emit_tr, [("XT", s) for s in range(4)] + ["IDN"], [kb(b)])
                        if condof(t) == 0:
                            pv = bank(b).rearrange("p (a b) -> p a b", b=64)
                            ov = xg[:, m, :].rearrange("p (a b) -> p a b", b=64)
                            if m < 4:
                                tv = TAB[:, m, 8 * t:8 * t + 8].unsqueeze(2).broadcast_to([128, 8, 64])
                            else:
                                tv = TAB[:, m - 4, :].unsqueeze(1).broadcast_to([128, 8, 64])
                            tt("dve", ov, pv, tv, ALU.add, [kb(b), "TAB"], [("xg", id(xg), m)])
                        else:
                            cp("dve", xg[:, m, :], bank(b), [kb(b)], [("xg", id(xg), m)])

                def N_sq(t, w, h=None):
                    xg = xgT[t % 3]
                    for hh in ((0, 1) if h is None else (h,)):
                        ms = range(4 * hh, 4 * hh + 4)
                        act(SQ[w][:, 4 * hh:4 * hh + 4, :], xg[:, 4 * hh:4 * hh + 4, :], AF.Square,
                            [("xg", id(xg), m) for m in ms], [("SQ", w, hh)])

                def N_mm(w):
                    mm(bank(6 + w), [(ONES[:], SQ[w][:, m, :]) for m in range(8)], [("SQ", w, 0), ("SQ", w, 1), "ONES"], [kb(6 + w)])

                def N_ln(w):
                    act(LN[:, w, :], bank(6 + w), AF.Ln, [kb(6 + w), "EPSC"], [("LN", w)], bias=EPSC[:, 0:1], scale=1.0 / D)
                    act(RS[:, w, :], LN[:, w, :], AF.Exp, [("LN", w)], [("RS", w)], scale=-0.5)

                def N_ln2():
                    act(LN[:], PS[:, 6:8, :], AF.Ln, [kb(6), kb(7), "EPSC"], [("LN", 0), ("LN", 1)], bias=EPSC[:, 0:1], scale=1.0 / D)
                    act(RS[:], LN[:], AF.Exp, [("LN", 0), ("LN", 1)], [("RS", 0), ("RS", 1)], scale=-0.5)

                def APPLY(t, w, k, store, h=None):
                    xg = xgT[t % 3]
                    xm = HMo if store else xmT[t % 2]
                    cond = condof(t)
                    for m in (range(8) if h is None else range(4 * h, 4 * h + 4)):
                        sl = m % 2
                        stt(TMP[sl][:], xg[:, m, :], ns_ap(cond, k, m), RS[:, w, :], ALU.mult, ALU.mult,
                            [("xg", id(xg), m), ("RS", w), "NS"], [("TMP", sl)])
                        act(xm[:, m, :], TMP[sl][:], AF.Identity, [("TMP", sl), "MOD"], [("xm", id(xm), m)],
                            bias=sh_ap(cond, k, m))
                    if store and h in (None, 1):
                        t0 = t * 512
                        dma("act", HM[:, :, t0:t0 + 512].rearrange("m p t -> p m t"), xm[:, :, :],
                            [("xm", id(xm), m) for m in range(8)], [("HM", t)])
                        dma("act", XA[:, :, t0:t0 + 512].rearrange("m p t -> p m t"), xg[:, :, :],
                            [("xg", id(xg), m) for m in range(8)], [("XA", t)])

                def FIN_dve(t):
                    xg = xgT[t % 3]
                    for m in range(8):
                        stt(YF[:, m, :], xg[:, m, :], VT[:, R_NF + m:R_NF + m + 1], RS[:, 0, :], ALU.mult, ALU.mult,
                            [("xg", id(xg), m), ("RS", 0), "VT"], [("YF", m)])

                def FIN_tr(t, srange):
                    t0 = t * 512
                    for s_ in srange:
                        sl = tcnt[0] % 2
                        tcnt[0] += 1
                        for half in range(2):
                            b = 4 + half

                            def emit_tr(e, s_=s_, half=half, b=b):
                                ins = None
                                for mmi in range(4):
                                    m = 4 * half + mmi
                                    ins = e.transpose(out=PS[:, b, mmi * 128:(mmi + 1) * 128], in_=YF[:, m, s_ * 128:(s_ + 1) * 128],
                                                      identity=IDN[:])
                                return ins
                            P.add("pe", emit_tr, [("YF", m) for m in range(8)] + ["IDN"], [kb(b)])
                            act(YT[sl][:, half * 512:(half + 1) * 512], bank(b), AF.Copy, [kb(b)], [("YT", sl)])
                        dma("pool", yrows(t0 + s_ * 128, 128), YT[sl][:], [("YT", sl)], [("yout", t, s_)])

                P1(0)
                N_sq(0, 1)
                N_mm(1)
                N_ln(1)
                APPLY(0, 1, k_in, False)
                for t in range(ntiles):
                    hooks = {}
                    both = (t >= 1 and t + 1 < ntiles)
                    if t >= 1:
                        hooks.setdefault(0, []).append(lambda t=t: N_sq(t - 1, 0, 0))
                        hooks.setdefault(1, []).append(lambda t=t: N_sq(t - 1, 0, 1))
                        hooks.setdefault(2, []).append(lambda: N_mm(0))
                    if t + 1 < ntiles:
                        hooks.setdefault(0, []).append(lambda t=t: P1(t + 1, 0))
                        hooks.setdefault(2, []).append(lambda t=t: P1(t + 1, 1))
                        hooks.setdefault(3, []).append(lambda t=t: N_sq(t + 1, 1, 0))
                        hooks.setdefault(4, []).append(lambda t=t: N_sq(t + 1, 1, 1))
                        hooks.setdefault(5, []).append(lambda: N_mm(1))
                    if both:
                        hooks.setdefault(6, []).append(N_ln2)
                    elif t >= 1:
                        hooks.setdefault(6, []).append(lambda: N_ln(0))
                    else:
                        hooks.setdefault(6, []).append(lambda: N_ln(1))
                    if t >= 1:
                        if which == 1:
                            hooks.setdefault(7, []).append(lambda t=t: APPLY(t - 1, 0, 1, True, 0))
                            hooks.setdefault(8, []).append(lambda t=t: APPLY(t - 1, 0, 1, True, 1))
                        else:
                            hooks.setdefault(7, []).append(lambda t=t: FIN_dve(t - 1))
                            hooks.setdefault(8, []).append(lambda t=t: FIN_tr(t - 1, (0, 1)))
                            hooks.setdefault(10, []).append(lambda t=t: FIN_tr(t - 1, (2, 3)))
                    if t + 1 < ntiles:
                        hooks.setdefault(9, []).append(lambda t=t: APPLY(t + 1, 1, k_in, False, 0))
                        hooks.setdefault(10, []).append(lambda t=t: APPLY(t + 1, 1, k_in, False, 1))
                    if t >= 1:
                        for jp_ in (1, 3, 5, 8, 10):
                            if pre_jobs:
                                hooks.setdefault(jp_, []).append(pre_jobs.pop(0))
                    ffn_tile(which, condof(t), k_in, xgT[t % 3], xmT[t % 2], H, B, t == 0 and not (which == 2 and PRECAST), hooks)
                t = ntiles - 1
                N_sq(t, 0)
                N_mm(0)
                N_ln(0)
                if which == 1:
                    APPLY(t, 0, 1, True)
                else:
                    FIN_dve(t)
                    FIN_tr(t, (0, 1, 2, 3))
                P.barrier()

        def run_ff1(groups):
            with contextlib.ExitStack() as ph:
                xg = sb("xg", [128, 8, 1024], F32, ph)
                B = ff_buffers(ph, first=True)
                for g in groups:
                    ff1_group(g, xg, B)
                P.barrier(ff_keys() + allps)

        def run_s2(groups):
            with contextlib.ExitStack() as ph:
                xg = sb("xg", [128, 8, 1024], F32, ph)
                for g in groups:
                    s2_group(g, xg, ph)
                P.barrier()

        stages = stop_after
        run_ff_tiles(1)
        if stages != "ff1":
            s1_group(0)
            s1_group(1)
            if stages != "s1":
                run_s2([0, 1, 2, 3, 4])
                if stages != "s2":
                    run_ff_tiles(2)

        P.build(nc, top, {"sp": 16, "act": 4, "pool": 12, "pe": 1, "dve": 1})
    nc._prog_stats = dict(n_ops=len(P.ops))
    return nc


def make_in_maps(inputs):
    f = lambda a: np.ascontiguousarray(np.asarray(a, dtype=np.float32))
    x_prompt, x_sample = f(inputs["x_prompt"]), f(inputs["x_sample"])
    c, c_ctx = f(inputs["c"]), f(inputs["c_ctx"])
    sf, sbw = f(inputs["state_rnn_fwd"]), f(inputs["state_rnn_bwd"])
    shared = {
        "ident": np.eye(128, dtype=np.float32),
        "gvb": np.ascontiguousarray(np.broadcast_to(f(inputs["gmlp_norm"])[0][None, :], (128, D))),
        "bs": f(inputs["b_s"])[0].reshape(1, D),
        "w_mod": f(inputs["w_mod"])[0],
        "ff1_gate": f(inputs["ff1_gate"])[0], "ff1_up": f(inputs["ff1_up"])[0], "ff1_down": f(inputs["ff1_down"])[0],
        "ff2_gate": f(inputs["ff2_gate"])[0], "ff2_up": f(inputs["ff2_up"])[0], "ff2_down": f(inputs["ff2_down"])[0],
        "w_in": f(inputs["w_in"])[0], "w_r": f(inputs["w_r"])[0], "w_i": f(inputs["w_i"])[0], "w_s": f(inputs["w_s"])[0],
        "w_br": f(inputs["w_br"])[0], "w_bg": f(inputs["w_bg"])[0], "w_out": f(inputs["w_out"])[0],
    }
    base = np.zeros((256, 128), np.float32)
    base[R_C1:R_C1 + 8] = c_ctx.reshape(8, 128)
    base[R_BMOD:R_BMOD + 72] = f(inputs["b_mod"])[0].reshape(72, 128)
    base[R_N1:R_N1 + 8] = f(inputs["norm1"])[0].reshape(8, 128)
    base[R_N2:R_N2 + 8] = f(inputs["norm2"])[0].reshape(8, 128)
    base[R_N3:R_N3 + 8] = f(inputs["norm3"])[0].reshape(8, 128)
    base[R_NF:R_NF + 8] = f(inputs["norm_f"]).reshape(8, 128)
    base[R_CW:R_CW + 32] = f(inputs["conv_w"])[0].reshape(32, 128)
    base[R_CB:R_CB + 8] = f(inputs["conv_b"])[0].reshape(8, 128)
    base[R_BR:R_BR + 16] = f(inputs["b_r"])[0].reshape(16, 128)
    base[R_BI:R_BI + 16] = f(inputs["b_i"])[0].reshape(16, 128)
    base[R_LAM:R_LAM + 16] = f(inputs["lam"])[0].reshape(16, 128)
    maps = []
    for b in range(8):
        v = base.copy()
        v[R_C0:R_C0 + 8] = c[b].reshape(8, 128)
        v[R_SF:R_SF + 8] = sf[b, 0].reshape(8, 128)
        v[R_SB:R_SB + 8] = sbw[b, 0].reshape(8, 128)
        m = dict(shared)
        m["xs"] = x_sample[b]
        m["xp"] = x_prompt[4 * b:4 * b + 4].reshape(1024, D)
        m["vecs"] = v
        maps.append(m)
    return maps


def kernel(**inputs):
    maps = make_in_maps(inputs)
    nc = build_nc()
    res = run_bass_kernel_spmd(nc, maps, core_ids=list(range(8)))
    y_prompt = np.zeros((32, 256, D), np.float32)
    y_sample = np.zeros((8, 4096, D), np.float32)
    nsf = np.zeros((32, 1, D), np.float32)
    nsb = np.zeros((32, 1, D), np.float32)
    for b in range(8):
        r = res.results[b]
        y_sample[b] = np.asarray(r["ys"], dtype=np.float32)
        y_prompt[4 * b:4 * b + 4] = np.asarray(r["yp"], dtype=np.float32).reshape(4, 256, D)
        nsf[4 * b:4 * b + 4, 0] = np.asarray(r["stf"], dtype=np.float32)
        nsb[4 * b:4 * b + 4, 0] = np.asarray(r["stb"], dtype=np.float32)
    return (y_prompt, y_sample, nsf, nsb)
```
